# Optimizing a Trainium2 kernel written in Bass

```python
import math
import jax
import jax.numpy as jnp
from jax import lax
from jax.lax import linalg as lax_linalg
import numpy as np

D_MODEL = 1024
BATCH = 32
SEQ = 2048
DEPTH = 4
DEC_BATCH = 16
DEC_SEQ = 32
PAST_LEN = 1024

CHUNK = 64
Q_BLOCK = 128
MLA_HEADS = 4
MLA_NOPE = 64
MLA_ROPE = 32
MLA_V = 64
MLA_Q_LORA = 384
MLA_KV_LORA = 256
ROPE_THETA = 10000.0
GDN_HEADS = 4
GDN_DK = 128
GDN_DV = 128
CONV_W = 4
FOX_HEADS = 4
FOX_HD = 64
FOX_BIAS_INIT = 3.0
N_MEM = 256
MEM_HEADS = 4
MEM_HD = 128
D_FF = 4 * D_MODEL
ALPHA = (2 * DEPTH) ** 0.25
BETA = (8 * DEPTH) ** -0.25
LN_EPS = 1e-5
RMS_EPS = 1e-6

GDN_QK = GDN_HEADS * GDN_DK
GDN_V = GDN_HEADS * GDN_DV
GDN_CONV_DIM = 2 * GDN_QK + GDN_V
FOX_W = FOX_HEADS * FOX_HD
MLA_W = MLA_HEADS * MLA_V
MIX_WIDTH = MLA_W + GDN_V + FOX_W
MEM_W = MEM_HEADS * MEM_HD
IN_SPLITS = (MLA_Q_LORA, MLA_KV_LORA + MLA_ROPE, GDN_QK, GDN_QK, GDN_V, GDN_V, GDN_HEADS, GDN_HEADS,
             FOX_W, FOX_W, FOX_W, FOX_HEADS)
IN_DIM = sum(IN_SPLITS)

kernel_name = 'hybrid_streaming_encoder_step'


def split_cols(x, sizes):
    offs = np.cumsum(sizes)[:-1].tolist()
    return jnp.split(x, offs, axis=-1)


def layer_norm(x, g, b):
    xf = x.astype(jnp.float32)
    mu = jnp.mean(xf, -1, keepdims=True)
    xc = xf - mu
    var = jnp.mean(xc * xc, -1, keepdims=True)
    return (xc * lax.rsqrt(var + LN_EPS) * g.astype(jnp.float32) + b.astype(jnp.float32)).astype(x.dtype)


def rms_norm(x, g):
    xf = x.astype(jnp.float32)
    y = xf * lax.rsqrt(jnp.mean(xf * xf, -1, keepdims=True) + RMS_EPS)
    return (y * g.astype(jnp.float32)).astype(x.dtype)


def l2_norm(x):
    xf = x.astype(jnp.float32)
    return xf * lax.rsqrt(jnp.sum(xf * xf, -1, keepdims=True) + 1e-6)


def rope(x, pos):
    half = MLA_ROPE // 2
    inv = ROPE_THETA ** (-jnp.arange(half, dtype=jnp.float32) * (2.0 / MLA_ROPE))
    ang = pos.astype(jnp.float32)[:, None] * inv[None, :]
    cos = jnp.cos(ang)[None, :, None, :]
    sin = jnp.sin(ang)[None, :, None, :]
    xf = x.astype(jnp.float32)
    x1, x2 = xf[..., :half], xf[..., half:]
    return jnp.concatenate([x1 * cos - x2 * sin, x2 * cos + x1 * sin], -1).astype(x.dtype)


def chunk_causal(t, s):
    return (s // CHUNK) <= (t // CHUNK)


def frame_causal(t, s):
    return s <= t


def attention(q, k, v, mask_fn, fcum=None):
    Tq = q.shape[1]
    L = k.shape[1]
    q_start = L - Tq
    scale = q.shape[-1] ** -0.5
    outs = []
    for lo in range(0, Tq, Q_BLOCK):
        hi = min(Tq, lo + Q_BLOCK)
        kend = q_start + hi
        qp = q_start + jnp.arange(lo, hi)
        kp = jnp.arange(kend)
        s = jnp.einsum('bqhd,bkhd->bhqk', q[:, lo:hi], k[:, :kend]).astype(jnp.float32) * scale
        if fcum is not None:
            fq = jnp.swapaxes(fcum[:, q_start + lo:q_start + hi], 1, 2)
            fk = jnp.swapaxes(fcum[:, :kend], 1, 2)
            s = s + (fq[..., :, None] - fk[..., None, :])
        s = jnp.where(mask_fn(qp[:, None], kp[None, :]), s, -jnp.inf)
        p = jax.nn.softmax(s, axis=-1).astype(v.dtype)
        outs.append(jnp.einsum('bhqk,bkhd->bqhd', p, v[:, :kend]))
    return jnp.concatenate(outs, axis=1)


def gated_delta_chunked(q, k, v, g, beta, s0, chunk):
    B, T, H, DK = q.shape
    DV = v.shape[-1]
    n = T // chunk
    f32 = jnp.float32

    def to_chunks(a):
        a = a.astype(f32).reshape((B, n, chunk, H) + a.shape[3:])
        return jnp.moveaxis(a, (1, 3), (0, 2))

    qc = to_chunks(q) * (DK ** -0.5)
    kc, vc = to_chunks(k), to_chunks(v)
    gc, bc = to_chunks(g), to_chunks(beta)
    G = jnp.cumsum(gc, axis=-1)
    idx = jnp.arange(chunk)
    incl = idx[:, None] >= idx[None, :]
    strict = idx[:, None] > idx[None, :]
    decay = jnp.exp(jnp.where(incl, G[..., :, None] - G[..., None, :], -jnp.inf))
    kk = jnp.einsum('nbhik,nbhjk->nbhij', kc, kc)
    a_mat = jnp.where(strict, bc[..., :, None] * kk * decay, 0.0)
    m_mat = jnp.eye(chunk, dtype=f32) + a_mat
    rhs = jnp.concatenate([vc * bc[..., None], kc * (bc * jnp.exp(G))[..., None]], -1)
    sol = lax_linalg.triangular_solve(m_mat, rhs, left_side=True, lower=True, unit_diagonal=True)
    u, w = sol[..., :DV], sol[..., DV:]
    qk = jnp.einsum('nbhik,nbhjk->nbhij', qc, kc) * decay
    q_dec = qc * jnp.exp(G)[..., None]
    k_dec = kc * jnp.exp(G[..., -1:] - G)[..., None]
    g_last = jnp.exp(G[..., -1])

    def step(S, xs):
        u_i, w_i, qk_i, qd_i, kd_i, gl_i = xs
        v_new = u_i - jnp.einsum('bhck,bhkv->bhcv', w_i, S)
        o = jnp.einsum('bhck,bhkv->bhcv', qd_i, S) + jnp.einsum('bhij,bhjv->bhiv', qk_i, v_new)
        S = S * gl_i[..., None, None] + jnp.einsum('bhck,bhcv->bhkv', kd_i, v_new)
        return S, o

    S, o = lax.scan(step, s0.astype(f32), (u, w, qk, q_dec, k_dec, g_last))
    o = jnp.moveaxis(o, (0, 2), (1, 3)).reshape(B, T, H, DV)
    return o, S


def gated_deltanet(gq, gk, gv, gz, ga, gb, conv_past, s_past, conv_w, a_log, dt_bias, norm_g):
    B, T, _ = gq.shape
    qkv = jnp.concatenate([gq, gk, gv], -1)
    qkv_full = jnp.concatenate([conv_past.astype(qkv.dtype), qkv], 1)
    y = lax.conv_general_dilated(qkv_full, conv_w[:, None, :].astype(qkv.dtype), (1,), 'VALID',
                                 dimension_numbers=('NWC', 'WIO', 'NWC'),
                                 feature_group_count=GDN_CONV_DIM)
    y = jax.nn.silu(y)
    cq, ck, cv = split_cols(y, (GDN_QK, GDN_QK, GDN_V))
    q = l2_norm(cq.reshape(B, T, GDN_HEADS, GDN_DK))
    k = l2_norm(ck.reshape(B, T, GDN_HEADS, GDN_DK))
    v = cv.reshape(B, T, GDN_HEADS, GDN_DV)
    g = -jnp.exp(a_log.astype(jnp.float32)) * jax.nn.softplus(ga.astype(jnp.float32) + dt_bias.astype(jnp.float32))
    beta = jax.nn.sigmoid(gb.astype(jnp.float32))
    o, s_new = gated_delta_chunked(q, k, v, g, beta, s_past, min(CHUNK, T))
    z = gz.reshape(B, T, GDN_HEADS, GDN_DV).astype(jnp.float32)
    o = rms_norm(o, norm_g) * jax.nn.silu(z)
    return o.reshape(B, T, GDN_V).astype(gq.dtype), s_new.astype(s_past.dtype), qkv_full[:, -(CONV_W - 1):]


def memory_attention(x, mem_k, mem_v, w_xq, w_xo):
    B, T, _ = x.shape
    q = (x @ w_xq).reshape(B, T, MEM_HEADS, MEM_HD)
    s = jnp.einsum('bthd,bmhd->bhtm', q, mem_k).astype(jnp.float32) * (MEM_HD ** -0.5)
    p = jax.nn.softmax(s, axis=-1).astype(mem_v.dtype)
    o = jnp.einsum('bhtm,bmhd->bthd', p, mem_v).reshape(B, T, MEM_W)
    return o @ w_xo


def encoder_layer(x, past, mem_k, mem_v, lp):
    (w_in, qa_g, kva_g, w_uq, w_ukv, conv_w, a_log, dt_bias, gdn_g, fox_bf,
     w_out, w_xq, w_xo, w_ff1, w_ff2, ln_g, ln_b) = lp
    ckv_past, kpe_past, fk_past, fv_past, logf_past, s_past, conv_past = past
    B, T, _ = x.shape
    dt = x.dtype
    q_start = ckv_past.shape[1]
    pos = q_start + jnp.arange(T)
    (cq, kva, gq, gk, gv, gz, ga, gb, fq, fk, fv, ff) = split_cols(x @ w_in, IN_SPLITS)

    c_q = rms_norm(cq, qa_g)
    q_a = (c_q @ w_uq).reshape(B, T, MLA_HEADS, MLA_NOPE + MLA_ROPE)
    q_a = jnp.concatenate([q_a[..., :MLA_NOPE], rope(q_a[..., MLA_NOPE:], pos)], -1)
    c_kv = rms_norm(kva[..., :MLA_KV_LORA], kva_g)
    k_pe = rope(kva[:, :, None, MLA_KV_LORA:], pos)[:, :, 0]
    ckv_all = jnp.concatenate([ckv_past, c_kv], 1)
    kpe_all = jnp.concatenate([kpe_past, k_pe], 1)
    L = ckv_all.shape[1]
    kv_a = (ckv_all @ w_ukv).reshape(B, L, MLA_HEADS, MLA_NOPE + MLA_V)
    k_a = jnp.concatenate([kv_a[..., :MLA_NOPE],
                           jnp.broadcast_to(kpe_all[:, :, None, :], (B, L, MLA_HEADS, MLA_ROPE))], -1)
    o_a = attention(q_a, k_a, kv_a[..., MLA_NOPE:], chunk_causal)

    o_b, s_new, conv_new = gated_deltanet(gq, gk, gv, gz, ga, gb, conv_past, s_past,
                                          conv_w, a_log, dt_bias, gdn_g)

    fq = fq.reshape(B, T, FOX_HEADS, FOX_HD)
    fk = fk.reshape(B, T, FOX_HEADS, FOX_HD)
    fv = fv.reshape(B, T, FOX_HEADS, FOX_HD)
    logf = jax.nn.log_sigmoid(ff.astype(jnp.float32) + fox_bf.astype(jnp.float32))
    fk_all = jnp.concatenate([fk_past, fk], 1)
    fv_all = jnp.concatenate([fv_past, fv], 1)
    fcum = jnp.cumsum(jnp.concatenate([logf_past.astype(jnp.float32), logf], 1), axis=1)
    o_c = attention(fq, fk_all, fv_all, frame_causal, fcum)

    mix = jnp.concatenate([o_a.reshape(B, T, MLA_W), o_b, o_c.reshape(B, T, FOX_W)], -1) @ w_out
    x = layer_norm(ALPHA * x + mix, ln_g[0], ln_b[0])
    x = layer_norm(ALPHA * x + memory_attention(x, mem_k, mem_v, w_xq, w_xo), ln_g[1], ln_b[1])
    ffn = jnp.square(jax.nn.relu(x @ w_ff1)) @ w_ff2
    x = layer_norm(ALPHA * x + ffn, ln_g[2], ln_b[2])
    return x, (c_kv, k_pe, fk, fv, logf.astype(dt), s_new, conv_new)


def _stack(entries, i):
    return jnp.stack([e[i] for e in entries])


def setup_inputs(seed: int = 0) -> dict:
    key = jax.random.key(seed)
    ks = jax.random.split(key, 34)
    f32 = jnp.float32

    def nrm(i, shape, scale=1.0):
        return jax.random.normal(ks[i], shape, f32) * scale

    col_scale = jnp.concatenate([
        jnp.ones((MLA_Q_LORA + MLA_KV_LORA + MLA_ROPE + 2 * GDN_QK,), f32),
        jnp.full((GDN_V,), BETA, f32),
        jnp.ones((GDN_V + 2 * GDN_HEADS + 2 * FOX_W,), f32),
        jnp.full((FOX_W,), BETA, f32),
        jnp.ones((FOX_HEADS,), f32)])
    ukv_scale = jnp.tile(jnp.concatenate([jnp.ones((MLA_NOPE,), f32), jnp.full((MLA_V,), BETA, f32)]), MLA_HEADS)
    dt = jnp.exp(jax.random.uniform(ks[21], (DEPTH, GDN_HEADS), f32, minval=math.log(1e-3), maxval=math.log(1e-1)))
    return {
        'x_prompt': nrm(0, (BATCH, SEQ, D_MODEL)),
        'x_sample': nrm(1, (DEC_BATCH, DEC_SEQ, D_MODEL)),
        'cache_mla_ckv': nrm(2, (DEPTH, DEC_BATCH, PAST_LEN, MLA_KV_LORA)),
        'cache_mla_kpe': nrm(3, (DEPTH, DEC_BATCH, PAST_LEN, MLA_ROPE)),
        'cache_fox_k': nrm(4, (DEPTH, DEC_BATCH, PAST_LEN, FOX_HEADS, FOX_HD)),
        'cache_fox_v': nrm(5, (DEPTH, DEC_BATCH, PAST_LEN, FOX_HEADS, FOX_HD), 0.5),
        'cache_fox_logf': jax.nn.log_sigmoid(FOX_BIAS_INIT + nrm(6, (DEPTH, DEC_BATCH, PAST_LEN, FOX_HEADS))),
        'state_gdn': nrm(7, (DEPTH, DEC_BATCH, GDN_HEADS, GDN_DK, GDN_DV), 0.1),
        'state_gdn_conv': nrm(8, (DEPTH, DEC_BATCH, CONV_W - 1, GDN_CONV_DIM)),
        'cache_mem_k': nrm(9, (DEPTH, DEC_BATCH, N_MEM, MEM_HEADS, MEM_HD)),
        'cache_mem_v': nrm(10, (DEPTH, DEC_BATCH, N_MEM, MEM_HEADS, MEM_HD), 0.5),
        'mem_prompt': nrm(11, (BATCH, N_MEM, D_MODEL)),
        'ln_in_g': 1.0 + nrm(12, (D_MODEL,), 0.02),
        'ln_in_b': nrm(13, (D_MODEL,), 0.02),
        'w_in': nrm(14, (DEPTH, D_MODEL, IN_DIM), D_MODEL ** -0.5) * col_scale,
        'qa_g': 1.0 + nrm(15, (DEPTH, MLA_Q_LORA), 0.02),
        'kva_g': 1.0 + nrm(16, (DEPTH, MLA_KV_LORA), 0.02),
        'w_uq': nrm(17, (DEPTH, MLA_Q_LORA, MLA_HEADS * (MLA_NOPE + MLA_ROPE)), MLA_Q_LORA ** -0.5),
        'w_ukv': nrm(18, (DEPTH, MLA_KV_LORA, MLA_HEADS * (MLA_NOPE + MLA_V)), MLA_KV_LORA ** -0.5) * ukv_scale,
        'gdn_conv_w': nrm(19, (DEPTH, CONV_W, GDN_CONV_DIM), CONV_W ** -0.5),
        'gdn_a_log': jnp.log(jax.random.uniform(ks[20], (DEPTH, GDN_HEADS), f32, minval=1.0, maxval=16.0)),
        'gdn_dt_bias': dt + jnp.log(-jnp.expm1(-dt)),
        'gdn_norm_g': 1.0 + nrm(22, (DEPTH, GDN_DV), 0.02),
        'fox_bf': FOX_BIAS_INIT + nrm(23, (DEPTH, FOX_HEADS), 0.1),
        'w_out': nrm(24, (DEPTH, MIX_WIDTH, D_MODEL), MIX_WIDTH ** -0.5 * BETA),
        'w_xq': nrm(25, (DEPTH, D_MODEL, MEM_W), D_MODEL ** -0.5),
        'w_mk': nrm(26, (DEPTH, D_MODEL, MEM_W), D_MODEL ** -0.5),
        'w_mv': nrm(27, (DEPTH, D_MODEL, MEM_W), D_MODEL ** -0.5 * BETA),
        'w_xo': nrm(28, (DEPTH, MEM_W, D_MODEL), MEM_W ** -0.5 * BETA),
        'w_ff1': nrm(29, (DEPTH, D_MODEL, D_FF), D_MODEL ** -0.5),
        'w_ff2': nrm(30, (DEPTH, D_FF, D_MODEL), D_FF ** -0.5 * BETA),
        'ln_g': 1.0 + nrm(31, (DEPTH, 3, D_MODEL), 0.02),
        'ln_b': nrm(32, (DEPTH, 3, D_MODEL), 0.02),
    }


def reference(x_prompt, x_sample, cache_mla_ckv, cache_mla_kpe, cache_fox_k, cache_fox_v, cache_fox_logf,
              state_gdn, state_gdn_conv, cache_mem_k, cache_mem_v, mem_prompt, ln_in_g, ln_in_b,
              w_in, qa_g, kva_g, w_uq, w_ukv, gdn_conv_w, gdn_a_log, gdn_dt_bias, gdn_norm_g, fox_bf,
              w_out, w_xq, w_mk, w_mv, w_xo, w_ff1, w_ff2, ln_g, ln_b):
    xp = layer_norm(x_prompt, ln_in_g, ln_in_b)
    xs = layer_norm(x_sample, ln_in_g, ln_in_b)
    dt = xp.dtype
    bp = x_prompt.shape[0]
    empty_past = (jnp.zeros((bp, 0, MLA_KV_LORA), dt), jnp.zeros((bp, 0, MLA_ROPE), dt),
                  jnp.zeros((bp, 0, FOX_HEADS, FOX_HD), dt), jnp.zeros((bp, 0, FOX_HEADS, FOX_HD), dt),
                  jnp.zeros((bp, 0, FOX_HEADS), dt), jnp.zeros((bp, GDN_HEADS, GDN_DK, GDN_DV), dt),
                  jnp.zeros((bp, CONV_W - 1, GDN_CONV_DIM), dt))
    p_new, p_mem, s_new = [], [], []
    for l in range(DEPTH):
        lp = (w_in[l], qa_g[l], kva_g[l], w_uq[l], w_ukv[l], gdn_conv_w[l], gdn_a_log[l], gdn_dt_bias[l],
              gdn_norm_g[l], fox_bf[l], w_out[l], w_xq[l], w_xo[l], w_ff1[l], w_ff2[l], ln_g[l], ln_b[l])
        mk = (mem_prompt @ w_mk[l]).reshape(bp, N_MEM, MEM_HEADS, MEM_HD)
        mv = (mem_prompt @ w_mv[l]).reshape(bp, N_MEM, MEM_HEADS, MEM_HD)
        xp, ent_p = encoder_layer(xp, empty_past, mk, mv, lp)
        p_new.append(ent_p)
        p_mem.append((mk, mv))
        past = (cache_mla_ckv[l], cache_mla_kpe[l], cache_fox_k[l], cache_fox_v[l], cache_fox_logf[l],
                state_gdn[l], state_gdn_conv[l])
        xs, ent_s = encoder_layer(xs, past, cache_mem_k[l], cache_mem_v[l], lp)
        s_new.append(ent_s)
    return (xp, xs,
            _stack(p_new, 0), _stack(p_new, 1), _stack(p_new, 2), _stack(p_new, 3), _stack(p_new, 4),
            _stack(p_new, 5), _stack(p_new, 6), _stack(p_mem, 0), _stack(p_mem, 1),
            _stack(s_new, 0), _stack(s_new, 1), _stack(s_new, 2), _stack(s_new, 3), _stack(s_new, 4),
            _stack(s_new, 5), _stack(s_new, 6))
```

```python
import math
import numpy as np
import concourse.bass as bass
import concourse.mybir as mybir
from concourse.bass_utils import run_bass_kernel_spmd

F32 = mybir.dt.float32
BF16 = mybir.dt.bfloat16
AF = mybir.ActivationFunctionType
ALU = mybir.AluOpType
AX = mybir.AxisListType

D = 1024
DEPTH = 4
MLA_H, MLA_NOPE, MLA_ROPE, MLA_V, MLA_QL, MLA_KVL = 4, 64, 32, 64, 384, 256
GDN_H, GDN_DK, GDN_DV, CONV_W = 4, 128, 128, 4
FOX_H, FOX_HD = 4, 64
N_MEM, MEM_H, MEM_HD = 256, 4, 128
D_FF = 4096
ALPHA = (2 * DEPTH) ** 0.25
LN_EPS = 1e-5
RMS_EPS = 1e-6
IN_DIM = 3500
C_CQ, C_KVA, C_GQ, C_GK, C_GV, C_GZ, C_GA, C_GB, C_FQ, C_FK, C_FV, C_FF = (
    0, 384, 672, 1184, 1696, 2208, 2720, 2724, 2728, 2984, 3240, 3496)

ENGS = ["pe", "act", "dve", "pool", "sp"]
CELL = 64
N_SB = 229376 // CELL
N_PS = 8
NCELL = N_SB + N_PS
DMA_RING = 20
SB_BASE = 16512
SB_LIMIT = 229248


def _dsize(dt):
    s = str(dt)
    if "32" in s:
        return 4
    if "16" in s:
        return 2
    if "8" in s:
        return 1
    raise ValueError(s)


class Sched:
    def __init__(self, nc):
        self.nc = nc
        self.ops = []
        self.lw = np.full(NCELL, -1, np.int64)
        self.lr = {e: np.full(NCELL, -1, np.int64) for e in ENGS}
        self.lr_dma = np.full(NCELL, -1, np.int64)
        self.n_alloc = 0
        self.psum = [nc.alloc_psum_tensor(f"psb{i}", [128, 512], F32) for i in range(N_PS)]
        self.psum_names = {p.name: i for i, p in enumerate(self.psum)}
        self.big = nc.alloc_sbuf_tensor_at("arena", [128, (SB_LIMIT - SB_BASE) // 2], BF16, offset=SB_BASE)

    def alloc(self, name, shape, dtype, off):
        n = int(np.prod(shape[1:]))
        nbytes = n * _dsize(dtype)
        assert off % 64 == 0 and off >= SB_BASE and off + nbytes <= SB_LIMIT, (name, off, nbytes)
        e0 = (off - SB_BASE) // 2
        v = self.big[0:shape[0], e0:e0 + nbytes // 2]
        if _dsize(dtype) == 4:
            v = v.bitcast(dtype)
        elif str(dtype) != str(BF16):
            raise ValueError(dtype)
        if len(shape) == 3:
            v = v.rearrange("p (a b) -> p a b", a=shape[1])
        elif len(shape) == 4:
            v = v.rearrange("p (a b c) -> p a b c", a=shape[1], b=shape[2])
        elif len(shape) == 5:
            v = v.rearrange("p (a b c d) -> p a b c d", a=shape[1], b=shape[2], c=shape[3])
        return v

    def _cells(self, ap):
        sp = str(ap.space)
        if sp == "PSUM":
            b = self.psum_names[ap.tensor.name]
            return N_SB + b, N_SB + b + 1
        pat = ap.ap
        pstride = pat[0][0]
        off = int(ap.offset)
        e0 = off % pstride if pstride > 0 else off
        span = 1
        for st, cnt in pat[1:]:
            span += abs(st) * (cnt - 1)
        esz = _dsize(ap.dtype)
        b0 = ap.tensor.manual_sbuf_range[0] + e0 * esz
        b1 = b0 + span * esz
        return b0 // CELL, (b1 + CELL - 1) // CELL

    def op(self, eng, fn, reads=(), writes=(), dma=False):
        idx = len(self.ops)
        deps = set()
        rc = [self._cells(a) for a in reads if a is not None and str(a.space) != "DRAM"]
        wc = [self._cells(a) for a in writes if a is not None and str(a.space) != "DRAM"]
        lw = self.lw

        def grab(arr, a, b):
            if b - a == 1:
                v = arr[a]
                if v >= 0:
                    deps.add(int(v))
                return
            sl = arr[a:b]
            mx = sl.max()
            if mx < 0:
                return
            if sl.min() == mx:
                deps.add(int(mx))
                return
            for v in np.unique(sl):
                if v >= 0:
                    deps.add(int(v))
        for a, b in rc:
            grab(lw, a, b)
            if a >= N_SB:
                for e in ENGS:
                    if e != eng:
                        grab(self.lr[e], a, b)
        for a, b in wc:
            grab(lw, a, b)
            for e in ENGS:
                grab(self.lr[e], a, b)
            grab(self.lr_dma, a, b)
        for a, b in rc:
            if dma:
                grab(self.lr_dma, a, b)
                self.lr_dma[a:b] = idx
            else:
                self.lr[eng][a:b] = idx
        for a, b in wc:
            lw[a:b] = idx
            for e in ENGS:
                self.lr[e][a:b] = -1
            self.lr_dma[a:b] = -1
        deps.discard(idx)
        self.ops.append(dict(eng=eng, fn=fn, deps=deps, dma=dma))
        return idx

    def emit(self):
        nc = self.nc
        ops = self.ops
        eng_of = [o["eng"] for o in ops]
        is_dma = [o["dma"] for o in ops]
        signal = [False] * len(ops)
        for o in ops:
            nd = {}
            dm = []
            for d in o["deps"]:
                if is_dma[d]:
                    dm.append(d)
                else:
                    e = eng_of[d]
                    if e == "pe" and o["eng"] == "pe" and not o["dma"]:
                        continue
                    if nd.get(e, -1) < d:
                        nd[e] = d
            o["cdeps"] = nd
            o["ddeps"] = dm
            for d in nd.values():
                signal[d] = True
        cnt = {}
        run = {e: 0 for e in ENGS}
        dma_n = {e: 0 for e in ENGS}
        dma_slot = {}
        for i, o in enumerate(ops):
            if o["dma"]:
                n = dma_n[o["eng"]]
                dma_slot[i] = (o["eng"], n % DMA_RING, n // DMA_RING + 1)
                dma_n[o["eng"]] = n + 1
            elif signal[i]:
                run[o["eng"]] += 1
                cnt[i] = run[o["eng"]]
        dma_engs = [e for e in ENGS if dma_n[e] > 0]
        sems = {e: nc.alloc_semaphore(f"s_{e}") for e in ENGS}
        dsem = {e: [nc.alloc_semaphore(f"d_{e}{k}") for k in range(DMA_RING)] for e in dma_engs}
        by_eng = {e: [i for i, o in enumerate(ops) if o["eng"] == e] for e in ENGS}
        self.stats = {e: len(by_eng[e]) for e in ENGS}

        def run_engine(e, eng):
            waited = {}
            nwait = 0
            for i in by_eng[e]:
                o = ops[i]
                for pe_, d in o["cdeps"].items():
                    c = cnt[d]
                    if waited.get(pe_, 0) < c:
                        eng.wait_ge(sems[pe_], c)
                        waited[pe_] = c
                        nwait += 1
                for d in o["ddeps"]:
                    qe, slot, use = dma_slot[d]
                    key = (qe, slot)
                    if waited.get(key, 0) < use:
                        eng.wait_ge(dsem[qe][slot], 16 * use)
                        waited[key] = use
                        nwait += 1
                if o["dma"]:
                    qe, slot, use = dma_slot[i]
                    key = (qe, slot)
                    if use > 1 and waited.get(key, 0) < use - 1:
                        eng.wait_ge(dsem[qe][slot], 16 * (use - 1))
                        waited[key] = use - 1
                        nwait += 1
                    o["fn"](eng).then_inc(dsem[qe][slot], 16)
                else:
                    ins = o["fn"](eng)
                    if signal[i]:
                        ins.then_inc(sems[e], 1)
            if e in dma_engs:
                n = dma_n[e]
                for slot in range(min(n, DMA_RING)):
                    use = (n - 1 - slot) // DMA_RING + 1
                    eng.wait_ge(dsem[e][slot], 16 * use)
            self.stats[e + "_waits"] = nwait

        with nc.Block() as block:
            @block.tensor
            def _(eng):
                run_engine("pe", eng)

            @block.scalar
            def _(eng):
                run_engine("act", eng)

            @block.vector
            def _(eng):
                run_engine("dve", eng)

            @block.gpsimd
            def _(eng):
                run_engine("pool", eng)

            @block.sync
            def _(eng):
                run_engine("sp", eng)


class Arena:
    def __init__(self, S, base, limit):
        self.S, self.base, self.off, self.limit = S, base, base, limit

    def alloc(self, name, shape, dtype):
        nbytes = int(np.prod(shape[1:])) * _dsize(dtype)
        off = self.off
        self.off = (off + nbytes + 63) // 64 * 64
        assert self.off <= self.limit, (name, self.off, self.limit)
        return self.S.alloc(name, shape, dtype, off)


class Ops:
    def __init__(self, S):
        self.S = S

    def mm(self, out, lhsT, rhs, start=True, stop=True):
        self.S.op("pe", lambda e: e.matmul(out, lhsT, rhs, start=start, stop=stop, skip_group_check=True),
                  reads=[lhsT, rhs], writes=[out])

    def tr(self, out, in_, ident):
        self.S.op("pe", lambda e: e.transpose(out, in_, ident), reads=[in_, ident], writes=[out])

    def act(self, out, in_, func, bias=None, scale=1.0, accum=None, eng="act"):
        kw = {}
        rd = [in_]
        if bias is not None:
            kw["bias"] = bias
            if not isinstance(bias, (int, float)):
                rd.append(bias)
        if not isinstance(scale, (int, float)):
            rd.append(scale)
        kw["scale"] = scale
        wr = [out]
        if accum is not None:
            kw["accum_out"] = accum
            wr.append(accum)
        self.S.op("act", lambda e: e.activation(out, in_, func, **kw), reads=rd, writes=wr)

    def tt(self, out, in0, in1, op, eng="dve"):
        self.S.op(eng, lambda e: e.tensor_tensor(out, in0, in1, op), reads=[in0, in1], writes=[out])

    def ts(self, out, in0, s1, s2, op0, op1=None, eng="dve", accum=None):
        rd = [in0] + [s for s in (s1, s2) if s is not None and not isinstance(s, (int, float))]
        wr = [out] + ([accum] if accum is not None else [])
        if op1 is None:
            self.S.op(eng, lambda e: e.tensor_scalar(out, in0, s1, None, op0), reads=rd, writes=wr)
        elif accum is None:
            self.S.op(eng, lambda e: e.tensor_scalar(out, in0, s1, s2, op0, op1), reads=rd, writes=wr)
        else:
            self.S.op(eng, lambda e: e.tensor_scalar(out, in0, s1, s2, op0, op1, accum_out=accum),
                      reads=rd, writes=wr)

    def stt(self, out, in0, scalar, in1, op0, op1, eng="dve"):
        rd = [in0, in1] + ([scalar] if not isinstance(scalar, (int, float)) else [])
        self.S.op(eng, lambda e: e.scalar_tensor_tensor(out, in0, scalar, in1, op0, op1), reads=rd, writes=[out])

    def copy(self, out, in_, eng="dve"):
        if eng == "act":
            self.S.op("act", lambda e: e.copy(out, in_), reads=[in_], writes=[out])
        else:
            self.S.op(eng, lambda e: e.tensor_copy(out, in_), reads=[in_], writes=[out])

    def memset(self, out, val, eng="dve"):
        self.S.op(eng, lambda e: e.memset(out, val), writes=[out])

    def recip(self, out, in_):
        self.S.op("dve", lambda e: e.reciprocal(out, in_), reads=[in_], writes=[out])

    def reduce(self, out, in_, op, axis=AX.X):
        self.S.op("dve", lambda e: e.tensor_reduce(out, in_, axis, op), reads=[in_], writes=[out])

    def dma(self, out, in_, q="sp"):
        self.S.op(q, lambda e: e.dma_start(out=out, in_=in_), reads=[in_], writes=[out], dma=True)


def bc(ap, shape):
    return ap.broadcast_to(list(shape))


class Cfg:
    def __init__(self, n_prompt=4, T=2048, n_sample=2, TS=32, past=1024, depth=DEPTH, gdn=True, upto=9):
        self.upto = upto
        self.n_prompt, self.T, self.n_sample, self.TS, self.past, self.depth, self.gdn = (
            n_prompt, T, n_sample, TS, past, depth, gdn)


def host_consts(cfg):
    c = {}
    c["ident_f"] = np.eye(128, dtype=np.float32)
    m = np.arange(128)
    c["triu_f"] = (m[:, None] <= m[None, :]).astype(np.float32)
    c["ones_f"] = np.ones((128, 128), np.float32)
    for nm, ch in (("g", 64), ("gs", 32)):
        same = (m[:, None] // ch) == (m[None, :] // ch)
        c["Mle_" + nm] = ((m[:, None] <= m[None, :]) & same).astype(np.float32)
        c["Mgt_" + nm] = ((m[:, None] > m[None, :]) & same).astype(np.float32)
        c["Sel0_" + nm] = np.repeat(((m // ch) == 0).astype(np.float32)[:, None], 128, 1)
        c["Sel1_" + nm] = np.repeat(((m // ch) == 1).astype(np.float32)[:, None], 128, 1)
    half = MLA_ROPE // 2
    inv = (10000.0 ** (-np.arange(half, dtype=np.float32) * (2.0 / MLA_ROPE))).astype(np.float32)

    def rope_tab(pos):
        ang = pos.astype(np.float32)[:, None] * inv[None, :]
        cs, sn = np.cos(ang).astype(np.float32), np.sin(ang).astype(np.float32)
        C = np.concatenate([cs, cs], 1)
        Sg = np.concatenate([-sn, sn], 1)
        return C, Sg
    C, Sg = rope_tab(np.arange(max(cfg.T, 128)))
    nt = C.shape[0] // 128
    c["ropeC"] = np.ascontiguousarray(C.reshape(nt, 128, 32).transpose(1, 0, 2))
    c["ropeS"] = np.ascontiguousarray(Sg.reshape(nt, 128, 32).transpose(1, 0, 2))
    C, Sg = rope_tab(cfg.past + np.arange(cfg.TS))
    c["ropeCs"] = C
    c["ropeSs"] = Sg
    return c


class Grp:
    pass


def build(cfg):
    nc = bass.Bass("TRN2", target_bir_lowering=False)
    S = Sched(nc)
    O = Ops(S)
    L = cfg.depth
    T = cfg.T
    NP = cfg.n_prompt
    NS = cfg.n_sample
    TS = cfg.TS
    PAST = cfg.past
    NTP = max(T // 128, 1)

    def din(name, shape):
        return nc.dram_tensor(name, list(shape), F32, kind="ExternalInput").ap()

    def dout(name, shape):
        return nc.dram_tensor(name, list(shape), F32, kind="ExternalOutput").ap()

    w_in = din("w_in", [L, D, IN_DIM])
    w_uq = din("w_uq", [L, MLA_QL, 384])
    w_ukv = din("w_ukv", [L, MLA_KVL, 512])
    w_out = din("w_out", [L, D, D])
    w_xq = din("w_xq", [L, D, 512])
    w_mk = din("w_mk", [L, D, 512])
    w_mv = din("w_mv", [L, D, 512])
    w_xo = din("w_xo", [L, 512, D])
    w_ff1 = din("w_ff1", [L, D, D_FF])
    w_ff2 = din("w_ff2", [L, D_FF, D])
    qa_g = din("qa_g", [L, 384])
    kva_g = din("kva_g", [L, 256])
    fox_bf = din("fox_bf", [L, 4])
    ln_in_fm = din("ln_in_fm", [128, 2, 8])
    ln_fm = din("ln_fm", [128, L, 3, 2, 8])
    conv_fm = din("conv_fm", [128, L, 12, 4])
    a_log = din("gdn_a_log", [L, 4])
    dt_bias = din("gdn_dt_bias", [L, 4])
    norm_g = din("gdn_norm_g", [L, 128])
    c_Mle = din("Mle_g", [128, 128])
    c_Mgt = din("Mgt_g", [128, 128])
    c_Sel0 = din("Sel0_g", [128, 128])
    c_Sel1 = din("Sel1_g", [128, 128])
    c_ident_f = din("ident_f", [128, 128])
    c_triu_f = din("triu_f", [128, 128])
    c_ones_f = din("ones_f", [128, 128])
    c_ropeC = din("ropeC", [128, NTP, 32])
    c_ropeS = din("ropeS", [128, NTP, 32])

    if NP:
        x_p = din("x_prompt", [NP, T, D])
        mem_p = din("mem_prompt", [NP, N_MEM, D])
        y_p = dout("y_prompt", [NP, T, D])
        po = dict(ckv=dout("p_mla_ckv", [L, NP, T, 256]), kpe=dout("p_mla_kpe", [L, NP, T, 32]),
                  fk=dout("p_fox_k", [L, NP, T, 256]), fv=dout("p_fox_v", [L, NP, T, 256]),
                  logf=dout("p_fox_logf", [L, NP, T, 4]), gdn=dout("p_gdn", [L, NP, 4, 128, 128]),
                  conv=dout("p_gdn_conv", [L, NP, 3, 1536]))
        o_mk = dout("p_mem_k", [L, NP, N_MEM, 512])
        o_mv = dout("p_mem_v", [L, NP, N_MEM, 512])
    if NS:
        x_s = din("x_sample", [NS, TS, D])
        y_s = dout("y_sample", [NS, TS, D])
        ci = dict(ckv=din("cache_mla_ckv", [L, NS, PAST, 256]), kpe=din("cache_mla_kpe", [L, NS, PAST, 32]),
                  fk=din("cache_fox_k", [L, NS, PAST, 256]), fv=din("cache_fox_v", [L, NS, PAST, 256]),
                  logf=din("cache_fox_logf", [L, NS, PAST, 4]), gdn=din("state_gdn", [L, NS, 4, 128, 128]),
                  conv=din("state_gdn_conv", [L, NS, 3, 1536]),
                  mk=din("cache_mem_k", [L, NS, N_MEM, 512]), mv=din("cache_mem_v", [L, NS, N_MEM, 512]))
        so = dict(ckv=dout("s_mla_ckv", [L, NS, TS, 256]), kpe=dout("s_mla_kpe", [L, NS, TS, 32]),
                  fk=dout("s_fox_k", [L, NS, TS, 256]), fv=dout("s_fox_v", [L, NS, TS, 256]),
                  logf=dout("s_fox_logf", [L, NS, TS, 4]), gdn=dout("s_gdn", [L, NS, 4, 128, 128]),
                  conv=dout("s_gdn_conv", [L, NS, 3, 1536]))
        c_ropeCs = din("ropeCs", [TS, 32])
        c_ropeSs = din("ropeSs", [TS, 32])

    main = Arena(S, SB_BASE, SB_LIMIT)
    ident_f = main.alloc("ident_f", [128, 128], F32)
    ident_b = main.alloc("ident_b", [128, 128], BF16)
    ones_b = main.alloc("ones_b", [128, 128], BF16)
    triu_f = main.alloc("triu_f", [128, 128], F32)
    triu_b = main.alloc("triu_b", [128, 128], BF16)
    ones_f = main.alloc("ones_f", [128, 128], F32)
    ropeC = main.alloc("ropeC", [128, NTP, 32], F32)
    ropeS = main.alloc("ropeS", [128, NTP, 32], F32)
    ropeCs = main.alloc("ropeCs", [128, 1, 32], F32)
    ropeSs = main.alloc("ropeSs", [128, 1, 32], F32)
    Mle = main.alloc("Mle", [128, 128], F32)
    Mgt = main.alloc("Mgt", [128, 128], F32)
    Sel0 = main.alloc("Sel0", [128, 128], F32)
    Sel1 = main.alloc("Sel1", [128, 128], F32)
    cwt = main.alloc("cwt", [128, L, 12, 4], F32)
    lnin = main.alloc("lnin", [128, 2, 8], F32)
    lnp = main.alloc("lnp", [128, L, 3, 2, 8], F32)
    TX = max(T if NP else 0, NS * TS)
    xf = main.alloc("xf", [128, 8, TX], F32)
    mixT = main.alloc("mixT", [128, 8, TX], BF16)
    memT = main.alloc("memT", [128, 8, N_MEM], BF16)
    ARENA0 = main.off

    O.dma(ident_f[:], c_ident_f)
    O.dma(triu_f[:], c_triu_f)
    O.dma(ones_f[:], c_ones_f)
    O.dma(ropeC[:], c_ropeC)
    O.dma(ropeS[:], c_ropeS)
    if NS:
        O.dma(ropeCs[0:TS, 0, :], c_ropeCs)
        O.dma(ropeSs[0:TS, 0, :], c_ropeSs)
    O.dma(Mle[:], c_Mle)
    O.dma(Mgt[:], c_Mgt)
    O.dma(Sel0[:], c_Sel0)
    O.dma(Sel1[:], c_Sel1)
    O.dma(cwt[:], conv_fm)
    O.dma(lnin[:], ln_in_fm)
    O.dma(lnp[:], ln_fm)
    O.copy(ident_b[:], ident_f[:])
    O.copy(triu_b[:], triu_f[:])
    O.copy(ones_b[:], ones_f[:])

    ps = S.psum

    def psb(i):
        return ps[i][:].bitcast(BF16)

    rr = {"ev": 0}

    def evac_eng():
        rr["ev"] ^= 1
        return "act" if rr["ev"] else "dve"

    def wload(dst, src):
        O.dma(dst, src, q="pool")

    def kpn(src):
        return src.rearrange("(k p) n -> p k n", p=128)

    def pbc(src_row):
        return src_row.partition_broadcast(128).rearrange("p o n -> p (o n)")

    def ln_block(ar, cols, g_ap, b_ap):
        n = cols.stop - cols.start
        v16 = ar.alloc("ln_v16", [128, 8, n], BF16)
        sq16 = ar.alloc("ln_sq16", [128, 8, n], BF16)
        mean = ar.alloc("ln_mean", [128, n], F32)
        msq = ar.alloc("ln_msq", [128, n], F32)
        rstd = ar.alloc("ln_rstd", [128, n], F32)
        tmp = ar.alloc("ln_tmp", [128, 8, n], F32)
        xv = xf[:, :, cols]
        O.copy(v16[:], xv, eng="act")
        O.act(sq16[:], xv, AF.Square)
        for c in range(8):
            O.mm(ps[6][:, 0:n], ones_b[:], v16[:, c, :], start=(c == 0), stop=(c == 7))
        for c in range(8):
            O.mm(ps[7][:, 0:n], ones_b[:], sq16[:, c, :], start=(c == 0), stop=(c == 7))
        O.act(mean[:], ps[6][:, 0:n], AF.Copy, scale=1.0 / D)
        O.tt(msq[:], mean[:], mean[:], ALU.mult)
        O.stt(rstd[:], ps[7][:, 0:n], 1.0 / D, msq[:], ALU.mult, ALU.subtract)
        O.ts(rstd[:], rstd[:], LN_EPS, None, ALU.add)
        O.act(rstd[:], rstd[:], AF.Sqrt)
        O.recip(rstd[:], rstd[:])
        O.tt(tmp[:], xv, bc(mean[:].unsqueeze(1), [128, 8, n]), ALU.subtract)
        O.tt(tmp[:], tmp[:], bc(rstd[:].unsqueeze(1), [128, 8, n]), ALU.mult)
        for c in range(8):
            O.act(xf[:, c, cols], tmp[:, c, :], AF.Identity, bias=b_ap[:, c:c + 1], scale=g_ap[:, c:c + 1])

    def attention(ar, n_kt, kt_ops, qt_ops, v_ops, scale, out_dst, nq, bias_fn=None, mask_fn=None,
                  q0_fn=None, heads=4, dv=64, rows_fn=None):
        E = [ar.alloc(f"attE{i}", [128, nq], BF16) for i in range(3)]
        rden = ar.alloc("att_rden", [128, nq], F32)
        ei = 0
        group = 2 if dv == 64 else 1
        for h0 in range(0, heads, group):
            hs = list(range(h0, h0 + group))
            first = {h: True for h in hs}
            last_kt = n_kt - 1
            for kt in range(n_kt):
                c0 = q0_fn(kt) if q0_fn else 0
                if c0 >= nq:
                    continue
                r = rows_fn(kt) if rows_fn else 128
                for h in hs:
                    sb = ei % 2
                    Et = E[ei % 3]
                    ei += 1
                    O.mm(ps[sb][0:r, c0:nq], kt_ops(h, kt), qt_ops(h)[:, c0:nq])
                    b_ap = bias_fn(h, kt) if bias_fn else None
                    O.act(Et[0:r, c0:nq], ps[sb][0:r, c0:nq], AF.Exp, bias=b_ap, scale=scale)
                    if mask_fn:
                        mask_fn(kt, Et, c0)
                    po_ = (h - h0) * 64 if dv == 64 else 0
                    pn = 64 if dv == 64 else 128
                    O.mm(ps[2][po_:po_ + pn, c0:nq], v_ops(h, kt), Et[0:r, c0:nq], start=first[h], stop=(kt == last_kt))
                    O.mm(ps[3][po_:po_ + pn, c0:nq], ones_b[0:r, 0:pn], Et[0:r, c0:nq], start=first[h], stop=(kt == last_kt))
                    first[h] = False
            O.recip(rden[:], ps[3][:, 0:nq])
            O.tt(out_dst(h0 // group), ps[2][:, 0:nq], rden[:], ALU.mult)

    def load_ln_transpose(src_rows, tp, dst_cols, scale_ap, bias_ap, xt, scratch, normalize=True):
        st6, mv, rs = scratch
        O.dma(xt[0:tp, :], src_rows)
        if normalize:
            for j in range(2):
                S.op("dve", lambda e, j=j: e.bn_stats(st6[0:tp, j, :], xt[0:tp, j * 512:(j + 1) * 512]),
                     reads=[xt[0:tp, j * 512:(j + 1) * 512]], writes=[st6[0:tp, j, :]])
            S.op("dve", lambda e: e.bn_aggr(mv[0:tp, :], st6[0:tp].rearrange("p a b -> p (a b)")),
                 reads=[st6[0:tp]], writes=[mv[0:tp, :]])
            O.ts(rs[0:tp, :], mv[0:tp, 1:2], LN_EPS, None, ALU.add)
            O.act(rs[0:tp, :], rs[0:tp, :], AF.Sqrt)
            O.recip(rs[0:tp, :], rs[0:tp, :])
            O.ts(xt[0:tp, :], xt[0:tp, :], mv[0:tp, 0:1], rs[0:tp, 0:1], ALU.subtract, ALU.mult)
        for half in range(2):
            pb = ps[4 + half]
            for c4 in range(4):
                c = half * 4 + c4
                O.tr(pb[:, c4 * tp:(c4 + 1) * tp], xt[0:tp, c * 128:(c + 1) * 128], ident_f[0:tp, 0:tp])
            if normalize:
                for c4 in range(4):
                    c = half * 4 + c4
                    O.act(xf[:, c, dst_cols], pb[:, c4 * tp:(c4 + 1) * tp], AF.Identity,
                          bias=bias_ap[:, c:c + 1], scale=scale_ap[:, c:c + 1])
            else:
                O.copy(memT[:, half * 4:half * 4 + 4, dst_cols],
                       pb[:, 0:4 * tp].rearrange("p (c n) -> p c n", c=4), eng="act")

    def input_stage(G):
        ar = Arena(S, ARENA0, SB_LIMIT)
        xin = [ar.alloc(f"xin{i}", [128, D], F32) for i in range(2)]
        scratch = (ar.alloc("st6", [128, 2, 6], F32), ar.alloc("mv", [128, 2], F32), ar.alloc("rs", [128, 1], F32))
        n = 0
        for si in range(G.nseq):
            for ti in range(G.NT):
                c0 = G.col0(si) + ti * G.tp
                load_ln_transpose(G.x_in(si)[ti * G.tp:(ti + 1) * G.tp, :], G.tp, slice(c0, c0 + G.tp),
                                  lnin[:, 0, :], lnin[:, 1, :], xin[n % 2], scratch)
                n += 1
            if G.prompt:
                for mt in range(2):
                    load_ln_transpose(mem_p[G.gseq(si), mt * 128:(mt + 1) * 128, :], 128, slice(mt * 128, (mt + 1) * 128),
                                      None, None, xin[n % 2], scratch, normalize=False)
                    n += 1

    def final_stage(G):
        ar = Arena(S, ARENA0, SB_LIMIT)
        yst = [ar.alloc(f"yst{i}", [128, D], F32) for i in range(2)]
        n = 0
        tp = G.tp
        for si in range(G.nseq):
            for ti in range(G.NT):
                yt = yst[n % 2]
                n += 1
                c0 = G.col0(si) + ti * tp
                for half in range(2):
                    pb = ps[4 + half]
                    for c4 in range(4):
                        c = half * 4 + c4
                        O.tr(pb[0:tp, c4 * 128:(c4 + 1) * 128], xf[:, c, c0:c0 + tp], ident_f[:])
                    O.copy(yt[0:tp, half * 512:(half + 1) * 512], pb[0:tp, 0:512], eng=("act" if half else "dve"))
                O.dma(G.y_out(si)[ti * tp:(ti + 1) * tp, :], yt[0:tp, :])

    def p1_mla(G, l):
        tp, TB, TPB, NB, NPT = G.tp, G.TB, G.TPB, G.NB, G.NPT
        NKT = NPT + G.NT
        LK = G.past + G.T
        ar = Arena(S, ARENA0, SB_LIMIT)
        Wm = ar.alloc("Wm", [128, 8, 672], BF16)
        Wuq = ar.alloc("Wuq", [128, 3, 384], BF16)
        Wukv = ar.alloc("Wukv", [128, 2, 512], BF16)
        qag = ar.alloc("qag", [128, 384], F32)
        kvag = ar.alloc("kvag", [128, 256], F32)
        KT = ar.alloc("KT", [128, 4, LK], BF16)
        Vc = ar.alloc("Vc", [128, NKT, 4, 64], BF16)
        QT = ar.alloc("QT", [128, 4, TB], BF16)
        xb = ar.alloc("xb", [128, 8, TB], BF16)
        cqn = ar.alloc("cqn", [128, 384], BF16)
        ckvf = [ar.alloc(f"ckvf{i}", [128, 256], F32) for i in range(2)]
        ckvb = ar.alloc("ckvb", [128, 256], BF16)
        kpef = [ar.alloc(f"kpef{i}", [128, 32], F32) for i in range(2)]
        kpeb = ar.alloc("kpeb", [128, 32], BF16)
        rt1 = ar.alloc("rt1", [128, 4, 32], F32)
        rt2 = ar.alloc("rt2", [128, 4, 32], F32)
        cqT = ar.alloc("cqT", [128, 3, 128], BF16)
        ckvT = ar.alloc("ckvT", [128, 2, 128], BF16)
        qtm = ar.alloc("qtm", [128, 4, 96], BF16)
        ktm = ar.alloc("ktm", [128, 4, 96], BF16)
        ssq = ar.alloc("ssq", [128, 2], F32)
        junk = ar.alloc("junk", [128, 384], BF16)
        if NPT:
            ckvp = ar.alloc("ckvp", [128, NPT, 256], BF16)
            kpep = ar.alloc("kpep", [128, NPT, 32], BF16)
        wload(Wm[:], kpn(w_in[l, :, C_CQ:C_CQ + 672]))
        wload(Wuq[:], kpn(w_uq[l]))
        wload(Wukv[:], kpn(w_ukv[l]))
        O.dma(qag[:], pbc(qa_g[l:l + 1, :]))
        O.dma(kvag[:], pbc(kva_g[l:l + 1, :]))

        def kv_tile(ckvb_ap, kpeb_ap, kt, r):
            kc = slice(kt * 128, kt * 128 + r)
            for k in range(2):
                O.tr(psb(7)[:, k * r:(k + 1) * r], ckvb_ap[:, k * 128:(k + 1) * 128], ident_b[0:r, 0:r])
            O.copy(ckvT[:, :, 0:r], psb(7)[:, 0:2 * r].rearrange("p (k n) -> p k n", k=2), eng="dve")
            for k in range(2):
                O.mm(ps[5][0:r, 0:512], ckvT[:, k, 0:r], Wukv[:, k, :], start=(k == 0), stop=(k == 1))
            kv3 = ps[5][0:r, 0:512].rearrange("p (h d) -> p h d", h=4)
            O.copy(ktm[0:r, :, 0:64], kv3[:, :, 0:64], eng="act")
            O.copy(ktm[0:r, :, 64:96], bc(kpeb_ap.unsqueeze(1), [r, 4, 32]), eng="dve")
            O.copy(Vc[0:r, kt, :, :], kv3[:, :, 64:128], eng="act")
            for h in range(4):
                O.tr(psb(7)[0:96, h * r:(h + 1) * r], ktm[0:r, h, :], ident_b[0:r, 0:r])
            O.copy(KT[0:96, :, kc], psb(7)[0:96, 0:4 * r].rearrange("p (h n) -> p h n", h=4), eng="act")

        for si in range(G.nseq):
            if NPT:
                wload(ckvp[:], ci["ckv"][l, G.gseq(si)].rearrange("(t p) n -> p t n", p=128))
                wload(kpep[:], ci["kpe"][l, G.gseq(si)].rearrange("(t p) n -> p t n", p=128))
                for pt in range(NPT):
                    kv_tile(ckvp[:, pt, :], kpep[:, pt, :], pt, 128)
            for b in range(NB):
                x0 = G.col0(si) + b * TB
                bcols = slice(x0, x0 + TB)
                O.copy(xb[:], xf[:, :, bcols], eng="dve")
                for t in range(TPB):
                    ti = b * TPB + t
                    tc = slice(t * tp, (t + 1) * tp)
                    gc = slice(ti * tp, (ti + 1) * tp)
                    rC, rS = G.rope(ti)
                    for k in range(8):
                        O.mm(ps[4][0:tp, 0:384], xb[:, k, tc], Wm[:, k, 0:384], start=(k == 0), stop=(k == 7))
                    for k in range(8):
                        O.mm(ps[5][0:tp, 0:288], xb[:, k, tc], Wm[:, k, 384:672], start=(k == 0), stop=(k == 7))
                    O.act(junk[0:tp, 0:384], ps[4][0:tp, 0:384], AF.Square, accum=ssq[0:tp, 0:1])
                    O.act(junk[0:tp, 0:256], ps[5][0:tp, 0:256], AF.Square, accum=ssq[0:tp, 1:2])
                    O.ts(ssq[0:tp, 0:1], ssq[0:tp, 0:1], 1.0 / 384, RMS_EPS, ALU.mult, ALU.add)
                    O.ts(ssq[0:tp, 1:2], ssq[0:tp, 1:2], 1.0 / 256, RMS_EPS, ALU.mult, ALU.add)
                    O.act(ssq[0:tp, :], ssq[0:tp, :], AF.Sqrt)
                    O.recip(ssq[0:tp, :], ssq[0:tp, :])
                    O.stt(cqn[0:tp, :], ps[4][0:tp, 0:384], ssq[0:tp, 0:1], qag[0:tp, :], ALU.mult, ALU.mult)
                    cf = ckvf[ti % 2]
                    kf = kpef[ti % 2]
                    O.stt(cf[0:tp, :], ps[5][0:tp, 0:256], ssq[0:tp, 1:2], kvag[0:tp, :], ALU.mult, ALU.mult)
                    O.copy(ckvb[0:tp, :], cf[0:tp, :], eng="act")
                    O.tt(rt1[0:tp, 0, :], ps[5][0:tp, 256:288], rC, ALU.mult)
                    O.tt(rt2[0:tp, 0, 0:16], ps[5][0:tp, 272:288], rS[:, 0:16], ALU.mult)
                    O.tt(rt2[0:tp, 0, 16:32], ps[5][0:tp, 256:272], rS[:, 16:32], ALU.mult)
                    O.tt(kf[0:tp, :], rt1[0:tp, 0, :], rt2[0:tp, 0, :], ALU.add)
                    O.copy(kpeb[0:tp, :], kf[0:tp, :], eng="act")
                    O.dma(G.out("ckv", l, si)[gc, :], cf[0:tp, :])
                    O.dma(G.out("kpe", l, si)[gc, :], kf[0:tp, :])
                    for k in range(3):
                        O.tr(psb(6)[:, k * tp:(k + 1) * tp], cqn[0:tp, k * 128:(k + 1) * 128], ident_b[0:tp, 0:tp])
                    O.copy(cqT[:, :, 0:tp], psb(6)[:, 0:3 * tp].rearrange("p (k n) -> p k n", k=3), eng="act")
                    for k in range(3):
                        O.mm(ps[4][0:tp, 0:384], cqT[:, k, 0:tp], Wuq[:, k, :], start=(k == 0), stop=(k == 2))
                    q3 = ps[4][0:tp, 0:384].rearrange("p (h d) -> p h d", h=4)
                    O.copy(qtm[0:tp, :, 0:64], q3[:, :, 0:64], eng="act")
                    O.tt(rt1[0:tp], q3[:, :, 64:96], bc(rC.unsqueeze(1), [tp, 4, 32]), ALU.mult)
                    O.tt(rt2[0:tp, :, 0:16], q3[:, :, 80:96], bc(rS[:, 0:16].unsqueeze(1), [tp, 4, 16]), ALU.mult)
                    O.tt(rt2[0:tp, :, 16:32], q3[:, :, 64:80], bc(rS[:, 16:32].unsqueeze(1), [tp, 4, 16]), ALU.mult)
                    O.tt(qtm[0:tp, :, 64:96], rt1[0:tp], rt2[0:tp], ALU.add)
                    for h in range(4):
                        O.tr(psb(6)[0:96, h * tp:(h + 1) * tp], qtm[0:tp, h, :], ident_b[0:tp, 0:tp])
                    O.copy(QT[0:96, :, tc], psb(6)[0:96, 0:4 * tp].rearrange("p (h n) -> p h n", h=4), eng="dve")
                    kv_tile(ckvb[0:tp, :], kpeb[0:tp, :], NPT + ti, tp)
                n_kt = NPT + (b + 1) * TPB

                def rows_fn(kt):
                    return 128 if kt < NPT else tp

                def q0_fn(kt, b=b):
                    return max(0, (kt - NPT - b * TPB) * 128) if G.prompt else 0

                def mask_fn(kt, Et, c0, b=b):
                    if G.prompt and kt >= b * TPB:
                        O.memset(Et[64:128, c0:c0 + 64], 0.0, eng="pool")
                attention(ar_attn(ar), n_kt,
                          lambda h, kt: KT[0:96, h, kt * 128:kt * 128 + rows_fn(kt)],
                          lambda h: QT[0:96, h, :],
                          lambda h, kt: Vc[0:rows_fn(kt), kt, h, :],
                          96 ** -0.5,
                          lambda hp, bcols=bcols: mixT[:, hp, bcols],
                          TB, q0_fn=q0_fn, mask_fn=mask_fn, rows_fn=rows_fn)

    def p2_fox(G, l):
        tp, TB, TPB, NB, NPT = G.tp, G.TB, G.TPB, G.NB, G.NPT
        NKT = NPT + G.NT
        LK = G.past + G.T
        ar = Arena(S, ARENA0, SB_LIMIT)
        Wf = ar.alloc("Wf", [128, 8, 772], BF16)
        KTf = ar.alloc("KTf", [128, 2, LK], BF16)
        Vf = ar.alloc("Vf", [128, NKT, 4, 64], BF16)
        fqT = ar.alloc("fqT", [128, 2, TB], BF16)
        xb = ar.alloc("xb", [128, 8, TB], BF16)
        fkvf = [ar.alloc(f"fkvf{i}", [128, 512], F32) for i in range(2)]
        lgf = [ar.alloc(f"lgf{i}", [128, 4], F32) for i in range(2)]
        fcum = ar.alloc("fcum", [128, NKT, 4], F32)
        carry = ar.alloc("carry", [128, 4], F32)
        biasF = ar.alloc("biasF", [128, NKT, 4], F32)
        bfb = ar.alloc("bfb", [128, 4], F32)
        ltmp = ar.alloc("ltmp", [128, 4], F32)
        if NPT:
            fkp = ar.alloc("fkp", [128, NPT, 256], BF16)
            lgp = ar.alloc("lgp", [128, NPT, 4], F32)
        wload(Wf[:], kpn(w_in[l, :, C_FQ:C_FQ + 772]))
        O.dma(bfb[:], pbc(fox_bf[l:l + 1, :]))

        def cum_tile(lg_ap, kt, r):
            O.mm(ps[7][0:r, 8:12], triu_f[0:r, 0:r], lg_ap)
            O.mm(ps[7][:, 16:20], ones_f[0:r, :], lg_ap)
            O.tt(fcum[0:r, kt, :], ps[7][0:r, 8:12], carry[0:r, :], ALU.add)
            O.tt(carry[:], ps[7][:, 16:20], carry[:], ALU.add)

        for si in range(G.nseq):
            O.memset(carry[:], 0.0)
            if NPT:
                gs = G.gseq(si)
                wload(fkp[:], ci["fk"][l, gs].rearrange("(t p) n -> p t n", p=128))
                wload(Vf[:, 0:NPT, :, :], ci["fv"][l, gs].rearrange("(t p) (h d) -> p t h d", p=128, h=4))
                O.dma(lgp[:], ci["logf"][l, gs].rearrange("(t p) h -> p t h", p=128))
                for pt in range(NPT):
                    for c in range(2):
                        O.tr(psb(6)[:, c * 128:(c + 1) * 128], fkp[:, pt, c * 128:(c + 1) * 128], ident_b[:])
                    O.copy(KTf[:, :, pt * 128:(pt + 1) * 128], psb(6)[:, 0:256].rearrange("p (c n) -> p c n", c=2),
                           eng=evac_eng())
                    cum_tile(lgp[:, pt, :], pt, 128)
            for b in range(NB):
                x0 = G.col0(si) + b * TB
                bcols = slice(x0, x0 + TB)
                kcols = slice(G.past + b * TB, G.past + (b + 1) * TB)
                O.copy(xb[:], xf[:, :, bcols], eng="dve")
                for c in range(2):
                    for k in range(8):
                        O.mm(ps[4][:, 0:TB], Wf[:, k, c * 128:(c + 1) * 128], xb[:, k, :], start=(k == 0), stop=(k == 7))
                    O.copy(fqT[:, c, :], ps[4][:, 0:TB], eng="act")
                    for k in range(8):
                        O.mm(ps[5][:, 0:TB], Wf[:, k, 256 + c * 128:256 + (c + 1) * 128], xb[:, k, :],
                             start=(k == 0), stop=(k == 7))
                    O.copy(KTf[:, c, kcols], ps[5][:, 0:TB], eng="dve")
                for t in range(TPB):
                    ti = b * TPB + t
                    tc = slice(t * tp, (t + 1) * tp)
                    gc = slice(ti * tp, (ti + 1) * tp)
                    for k in range(8):
                        O.mm(ps[6][0:tp, 0:512], xb[:, k, tc], Wf[:, k, 256:768], start=(k == 0), stop=(k == 7))
                    for k in range(8):
                        O.mm(ps[7][0:tp, 0:4], xb[:, k, tc], Wf[:, k, 768:772], start=(k == 0), stop=(k == 7))
                    ff_ = fkvf[ti % 2]
                    O.copy(ff_[0:tp, :], ps[6][0:tp, 0:512], eng="act")
                    O.copy(Vf[0:tp, NPT + ti, :, :], ps[6][0:tp, 256:512].rearrange("p (h d) -> p h d", h=4), eng="dve")
                    O.dma(G.out("fk", l, si)[gc, :], ff_[0:tp, 0:256])
                    O.dma(G.out("fv", l, si)[gc, :], ff_[0:tp, 256:512])
                    lg = lgf[ti % 2]
                    O.tt(ltmp[0:tp, :], ps[7][0:tp, 0:4], bfb[0:tp, :], ALU.add)
                    O.act(ltmp[0:tp, :], ltmp[0:tp, :], AF.Exp, scale=-1.0)
                    O.act(ltmp[0:tp, :], ltmp[0:tp, :], AF.Ln, bias=1.0)
                    O.ts(lg[0:tp, :], ltmp[0:tp, :], -1.0, None, ALU.mult)
                    O.dma(G.out("logf", l, si)[gc, :], lg[0:tp, :])
                    cum_tile(lg[0:tp, :], NPT + ti, tp)
                n_kt = NPT + (b + 1) * TPB
                O.tt(biasF[:, 0:n_kt, :], bc(carry[:].unsqueeze(1), [128, n_kt, 4]), fcum[:, 0:n_kt, :], ALU.subtract)

                def rows_fn(kt):
                    return 128 if kt < NPT else tp

                def q0_fn(kt, b=b):
                    return max(0, (kt - NPT - b * TPB) * tp)

                def mask_fn(kt, Et, c0, b=b):
                    if kt >= NPT + b * TPB:
                        O.tt(Et[0:tp, c0:c0 + tp], Et[0:tp, c0:c0 + tp], triu_b[0:tp, 0:tp], ALU.mult, eng="pool")
                attention(ar_attn(ar), n_kt,
                          lambda h, kt: KTf[(h % 2) * 64:(h % 2) * 64 + 64, h // 2, kt * 128:kt * 128 + rows_fn(kt)],
                          lambda h: fqT[(h % 2) * 64:(h % 2) * 64 + 64, h // 2, :],
                          lambda h, kt: Vf[0:rows_fn(kt), kt, h, :],
                          64 ** -0.5,
                          lambda hp, bcols=bcols: mixT[:, 6 + hp, bcols],
                          TB, bias_fn=lambda h, kt: biasF[0:rows_fn(kt), kt, h:h + 1], q0_fn=q0_fn, mask_fn=mask_fn,
                          rows_fn=rows_fn)

    def p3_gdn(G, l):
        tp = G.tp
        CH = min(64, tp)
        NCH = tp // CH
        GB = min(256, G.T)
        NGB = G.T // GB
        TPG = GB // tp
        NLEV = 6 if CH == 64 else 5
        ar = Arena(S, ARENA0, SB_LIMIT)
        Wg = ar.alloc("Wg", [128, 8, 2056], BF16)
        halo = ar.alloc("halo", [128, 12, 3], F32)
        pre = [ar.alloc(f"pre{i}", [128, 3 + GB], F32) for i in range(2)]
        acc = [ar.alloc(f"acc{i}", [128, GB], F32) for i in range(2)]
        sq16 = ar.alloc("gsq16", [128, GB], BF16)
        rst = ar.alloc("grst", [128, GB], F32)
        qnT = ar.alloc("qnT", [128, 4, GB], BF16)
        knT = ar.alloc("knT", [128, 4, GB], BF16)
        vT = ar.alloc("vT", [128, 4, GB], BF16)
        ktm = ar.alloc("gktm", [128, TPG, 4, 128], BF16)
        vtm = ar.alloc("gvtm", [128, TPG, 4, 128], BF16)
        zs = ar.alloc("zs", [128, TPG, 4, 128], F32)
        gg = ar.alloc("gg", [128, TPG, 4], F32)
        bet = ar.alloc("bet", [128, TPG, 4], F32)
        xb = ar.alloc("gxb", [128, 8, GB], BF16)
        dtb = ar.alloc("dtb", [128, 4], F32)
        nega = ar.alloc("nega", [128, 4], F32)
        ngb = ar.alloc("ngb", [128, 128], F32)
        Sf = ar.alloc("Sf", [128, 4, 128], F32)
        Sb = ar.alloc("Sb", [128, 4, 128], BF16)
        Bg = ar.alloc("Bg", [128, 4, 128], F32)
        Dm = ar.alloc("Dm", [128, 4, 128], F32)
        DTm = ar.alloc("DTm", [128, 4, 128], F32)
        tmpf = ar.alloc("tmpf", [128, 4, 128], F32)
        Xm = ar.alloc("Xm", [128, 4, 128], BF16)
        XTm = ar.alloc("XTm", [128, 4, 128], BF16)
        Pm = [ar.alloc(f"Pm{i}", [128, 4, 128], BF16) for i in range(2)]
        PTm = [ar.alloc(f"PTm{i}", [128, 4, 128], BF16) for i in range(2)]
        Ym = ar.alloc("Ym", [128, 4, 128], BF16)
        vb = ar.alloc("vb", [128, 4, 128], BF16)
        kbg = ar.alloc("kbg", [128, 4, 128], BF16)
        kdec = ar.alloc("kdec", [128, 4, 128], BF16)
        usb = ar.alloc("usb", [128, 4, 128], F32)
        wTs = ar.alloc("wTs", [128, 4, 128], BF16)
        qkdT = ar.alloc("qkdT", [128, 4, 128], BF16)
        vnew = ar.alloc("vnew", [128, 4, 128], BF16)
        osb = ar.alloc("osb", [128, 4, 128], F32)
        tsb = ar.alloc("tsb", [128, 4, 128], F32)
        obt = ar.alloc("obt", [128, 4, 128], BF16)
        sv = ar.alloc("sv", [128, 32], F32)
        cst = ar.alloc("cst", [128, 1536], F32)
        wload(Wg[:], kpn(w_in[l, :, C_GQ:C_GQ + 2056]))
        O.dma(dtb[:], pbc(dt_bias[l:l + 1, :]))
        O.dma(nega[:], pbc(a_log[l:l + 1, :]))
        O.dma(ngb[:], pbc(norm_g[l:l + 1, :]))
        O.act(nega[:], nega[:], AF.Exp)
        O.ts(nega[:], nega[:], -1.0, None, ALU.mult)

        def v4(ap, n=128):
            return ap.rearrange("p (h n) -> p h n", h=4)

        for si in range(G.nseq):
            if G.prompt:
                O.memset(halo[:], 0.0)
                O.memset(Sf[:], 0.0)
                O.memset(Sb[:], 0.0)
            else:
                gs = G.gseq(si)
                O.dma(Sf[:], ci["gdn"][l, gs].rearrange("h k v -> k h v"))
                O.copy(Sb[:], Sf[:], eng="act")
                O.dma(cst[0:3, :], ci["conv"][l, gs])
                for c in range(12):
                    O.tr(ps[7][:, c * 3:(c + 1) * 3], cst[0:3, c * 128:(c + 1) * 128], ident_f[0:3, 0:3])
                O.copy(halo[:], ps[7][:, 0:36].rearrange("p (c j) -> p c j", c=12), eng="act")
            for gbk in range(NGB):
                x0 = G.col0(si) + gbk * GB
                gcols = slice(x0, x0 + GB)
                O.copy(xb[:], xf[:, :, gcols], eng="dve")
                for c in range(12):
                    pr = pre[c % 2]
                    ac = acc[c % 2]
                    pb = ps[c % 2]
                    for k in range(8):
                        O.mm(pb[:, 0:GB], Wg[:, k, c * 128:(c + 1) * 128], xb[:, k, :], start=(k == 0), stop=(k == 7))
                    O.copy(pr[:, 0:3], halo[:, c, :], eng="dve")
                    O.copy(pr[:, 3:3 + GB], pb[:, 0:GB], eng="act")
                    O.copy(halo[:, c, :], pr[:, GB:GB + 3], eng="dve")
                    O.ts(ac[:], pr[:, 0:GB], cwt[:, l, c, 0:1], None, ALU.mult)
                    for j in range(1, 4):
                        O.stt(ac[:], pr[:, j:j + GB], cwt[:, l, c, j:j + 1], ac[:], ALU.mult, ALU.add)
                    if c < 8:
                        dst = qnT if c < 4 else knT
                        h = c % 4
                        O.act(ac[:], ac[:], AF.Silu)
                        O.act(sq16[:], ac[:], AF.Square)
                        O.mm(ps[2][:, 0:GB], ones_b[:], sq16[:])
                        O.ts(rst[:], ps[2][:, 0:GB], 1e-6, None, ALU.add)
                        O.act(rst[:], rst[:], AF.Sqrt)
                        O.recip(rst[:], rst[:])
                        if c < 4:
                            O.stt(dst[:, h, :], ac[:], GDN_DK ** -0.5, rst[:], ALU.mult, ALU.mult)
                        else:
                            O.tt(dst[:, h, :], ac[:], rst[:], ALU.mult)
                    else:
                        O.act(vT[:, c - 8, :], ac[:], AF.Silu)
                for t in range(TPG):
                    ti = gbk * TPG + t
                    tc = slice(t * tp, (t + 1) * tp)
                    gc = slice(x0 + t * tp, x0 + (t + 1) * tp)
                    R = slice(0, tp)
                    idb = ident_b[0:tp, 0:tp]
                    for h in range(4):
                        O.tr(psb(3)[R, h * 128:(h + 1) * 128], knT[:, h, tc], ident_b[:])
                    O.copy(ktm[R, t, :, :], v4(psb(3)[R, 0:512]), eng="act")
                    for h in range(4):
                        O.tr(psb(4)[R, h * 128:(h + 1) * 128], vT[:, h, tc], ident_b[:])
                    O.copy(vtm[R, t, :, :], v4(psb(4)[R, 0:512]), eng="dve")
                    for k in range(8):
                        O.mm(ps[5][R, 0:512], xb[:, k, tc], Wg[:, k, 1536:2048], start=(k == 0), stop=(k == 7))
                    for k in range(8):
                        O.mm(ps[6][R, 0:8], xb[:, k, tc], Wg[:, k, 2048:2056], start=(k == 0), stop=(k == 7))
                    O.act(zs[R, t, :, :], v4(ps[5][R, 0:512]), AF.Silu)
                    g_t = gg[R, t, :]
                    b_t = bet[R, t, :]
                    O.tt(g_t, ps[6][R, 0:4], dtb[R, :], ALU.add)
                    O.ts(g_t, g_t, 30.0, None, ALU.min)
                    O.act(g_t, g_t, AF.Exp)
                    O.act(g_t, g_t, AF.Ln, bias=1.0)
                    O.tt(g_t, g_t, nega[R, :], ALU.mult)
                    O.act(b_t, ps[6][R, 4:8], AF.Sigmoid)
                    mle = Mle[R, R]
                    mgt = Mgt[R, R]
                    O.mm(ps[6][R, 16:20], mle, g_t)
                    O.mm(ps[6][R, 24:28], mgt, g_t)
                    O.mm(ps[6][:, 32:36], Sel0[R, :], g_t)
                    if NCH == 2:
                        O.mm(ps[6][:, 36:40], Sel1[R, :], g_t)
                    O.act(sv[R, 0:4], ps[6][R, 16:20], AF.Exp)
                    O.act(sv[R, 4:8], ps[6][R, 24:28], AF.Exp)
                    O.act(sv[:, 8:8 + 4 * NCH], ps[6][:, 32:32 + 4 * NCH], AF.Exp)
                    O.tt(sv[R, 16:20], sv[R, 0:4], b_t, ALU.mult)
                    O.ts(sv[R, 20:24], b_t, -1.0, None, ALU.mult)
                    O.tt(Bg[R, :, R], bc(mgt.unsqueeze(1), [tp, 4, tp]), bc(g_t.unsqueeze(2), [tp, 4, tp]), ALU.mult)
                    for h in range(4):
                        O.mm(ps[7][R, h * 128:h * 128 + tp], mle, Bg[R, h, R])
                    O.act(Dm[R, :, R], v4(ps[7][R, 0:512])[:, :, R], AF.Exp)
                    O.tt(Dm[R, :, R], Dm[R, :, R], bc(mgt.unsqueeze(1), [tp, 4, tp]), ALU.mult)
                    for h in range(4):
                        O.mm(ps[7][R, h * 128:h * 128 + tp], Bg[R, h, R], mle)
                    O.act(DTm[R, :, R], v4(ps[7][R, 0:512])[:, :, R], AF.Exp)
                    O.tt(DTm[R, :, R], DTm[R, :, R], bc(mle.unsqueeze(1), [tp, 4, tp]), ALU.mult)
                    for h in range(4):
                        O.mm(ps[0][R, h * 128:h * 128 + tp], knT[:, h, tc], knT[:, h, tc])
                    O.tt(tmpf[R, :, R], v4(ps[0][R, 0:512])[:, :, R], Dm[R, :, R], ALU.mult)
                    O.tt(Xm[R, :, R], tmpf[R, :, R], bc(sv[R, 20:24].unsqueeze(2), [tp, 4, tp]), ALU.mult)
                    for h in range(4):
                        O.mm(ps[1][R, h * 128:h * 128 + tp], knT[:, h, tc], qnT[:, h, tc])
                    O.tt(qkdT[R, :, R], v4(ps[1][R, 0:512])[:, :, R], DTm[R, :, R], ALU.mult)
                    for h in range(4):
                        O.tr(psb(3)[R, h * 128:h * 128 + tp], Xm[R, h, R], idb)
                    O.copy(XTm[R, :, R], v4(psb(3)[R, 0:512])[:, :, R], eng="act")
                    O.tt(Ym[R, :, R], XTm[R, :, R], bc(idb.unsqueeze(1), [tp, 4, tp]), ALU.add)
                    P_, PT_ = XTm, Xm
                    for lev in range(1, NLEV):
                        Pn, PTn = Pm[lev % 2], PTm[lev % 2]
                        lastl = lev == NLEV - 1
                        if not lastl:
                            for h in range(4):
                                O.mm(ps[2][R, h * 128:h * 128 + tp], PT_[R, h, R], P_[R, h, R])
                        for h in range(4):
                            O.mm(ps[4][R, h * 128:h * 128 + tp], P_[R, h, R], PT_[R, h, R])
                        if not lastl:
                            O.copy(Pn[R, :, R], v4(ps[2][R, 0:512])[:, :, R], eng="act")
                        O.copy(PTn[R, :, R], v4(ps[4][R, 0:512])[:, :, R], eng="dve")
                        for h in range(4):
                            O.mm(ps[5][R, h * 128:h * 128 + tp], PTn[R, h, R], Ym[R, h, R])
                        O.tt(Ym[R, :, R], v4(ps[5][R, 0:512])[:, :, R], Ym[R, :, R], ALU.add)
                        P_, PT_ = Pn, PTn
                    O.tt(vb[R], vtm[R, t, :, :], bc(b_t.unsqueeze(2), [tp, 4, 128]), ALU.mult)
                    O.tt(kbg[R], ktm[R, t, :, :], bc(sv[R, 16:20].unsqueeze(2), [tp, 4, 128]), ALU.mult)
                    O.tt(kdec[R], ktm[R, t, :, :], bc(sv[R, 4:8].unsqueeze(2), [tp, 4, 128]), ALU.mult)
                    for h in range(4):
                        O.mm(ps[0][R, h * 128:(h + 1) * 128], Ym[R, h, R], vb[R, h, :])
                    O.copy(usb[R], v4(ps[0][R, 0:512]), eng="act")
                    for h in range(4):
                        O.mm(ps[1][:, h * 128:h * 128 + tp], kbg[R, h, :], Ym[R, h, R])
                    O.copy(wTs[:, :, R], v4(ps[1][:, 0:512])[:, :, R], eng="dve")
                    for cch in range(NCH):
                        rows = slice(cch * CH, (cch + 1) * CH)
                        ccols = slice(t * tp + cch * CH, t * tp + (cch + 1) * CH)
                        for h in range(4):
                            O.mm(ps[2][rows, h * 128:(h + 1) * 128], wTs[:, h, rows], Sb[:, h, :])
                        O.tt(vnew[rows], usb[rows], v4(ps[2][rows, 0:512]), ALU.subtract)
                        for h in range(4):
                            O.mm(ps[3][rows, h * 128:(h + 1) * 128], qnT[:, h, ccols], Sb[:, h, :])
                        for h in range(4):
                            O.mm(ps[4][rows, h * 128:(h + 1) * 128], qkdT[rows, h, rows], vnew[rows, h, :])
                        for h in range(4):
                            O.mm(ps[5][:, h * 128:(h + 1) * 128], kdec[rows, h, :], vnew[rows, h, :])
                        O.tt(tsb[rows], v4(ps[3][rows, 0:512]), bc(sv[rows, 0:4].unsqueeze(2), [CH, 4, 128]), ALU.mult)
                        O.tt(osb[rows], tsb[rows], v4(ps[4][rows, 0:512]), ALU.add)
                        O.tt(Sf[:], Sf[:], bc(sv[:, 8 + 4 * cch:12 + 4 * cch].unsqueeze(2), [128, 4, 128]), ALU.mult)
                        O.tt(Sf[:], Sf[:], v4(ps[5][:, 0:512]), ALU.add)
                        O.copy(Sb[:], Sf[:], eng="act")
                    O.tt(tsb[R], osb[R], osb[R], ALU.mult)
                    O.reduce(sv[R, 24:28], tsb[R], ALU.add)
                    O.ts(sv[R, 24:28], sv[R, 24:28], 1.0 / GDN_DV, RMS_EPS, ALU.mult, ALU.add)
                    O.act(sv[R, 24:28], sv[R, 24:28], AF.Sqrt)
                    O.recip(sv[R, 24:28], sv[R, 24:28])
                    O.tt(tsb[R], osb[R], bc(sv[R, 24:28].unsqueeze(2), [tp, 4, 128]), ALU.mult)
                    O.tt(tsb[R], tsb[R], bc(ngb[R, :].unsqueeze(1), [tp, 4, 128]), ALU.mult)
                    O.tt(obt[R], tsb[R], zs[R, t, :, :], ALU.mult)
                    for h in range(4):
                        O.tr(psb(6)[:, h * tp:(h + 1) * tp], obt[R, h, :], idb)
                    O.copy(mixT[:, 2:6, gc], psb(6)[:, 0:4 * tp].rearrange("p (h n) -> p h n", h=4), eng="act")
            O.dma(G.out("gdn", l, si).rearrange("h k v -> k h v"), Sf[:])
            for grp in range(3):
                for c4 in range(4):
                    c = grp * 4 + c4
                    O.tr(ps[7][0:3, c4 * 128:(c4 + 1) * 128], halo[:, c, :], ident_f[:])
                O.copy(cst[0:3, grp * 512:(grp + 1) * 512], ps[7][0:3, 0:512], eng="act")
            O.dma(G.out("conv", l, si), cst[0:3, :])

    def dense_blocks(G):
        tot = G.nseq * G.T
        bs = min(512, tot)
        return [slice(i, i + bs) for i in range(0, tot, bs)]

    def p4_out(G, l):
        ar = Arena(S, ARENA0, SB_LIMIT)
        Wo = ar.alloc("Wo", [128, 8, D], BF16)
        wload(Wo[:], kpn(w_out[l]))
        for bcols in dense_blocks(G):
            n = bcols.stop - bcols.start
            for oc in range(8):
                pb = ps[oc % 4]
                for k in range(8):
                    O.mm(pb[:, 0:n], Wo[:, k, oc * 128:(oc + 1) * 128], mixT[:, k, bcols], start=(k == 0), stop=(k == 7))
                O.stt(xf[:, oc, bcols], xf[:, oc, bcols], ALPHA, pb[:, 0:n], ALU.mult, ALU.add)
            ln_block(ar_attn(ar), bcols, lnp[:, l, 0, 0, :], lnp[:, l, 0, 1, :])

    def p5_mem(G, l):
        TB, NB = G.TB, G.NB
        ar = Arena(S, ARENA0, SB_LIMIT)
        Wq = ar.alloc("Wq", [128, 8, 512], BF16)
        Wxo = ar.alloc("Wxo", [128, 4, D], BF16)
        KTm = ar.alloc("KTm", [128, 4, N_MEM], BF16)
        Vm = ar.alloc("Vm", [128, 2, 512], BF16)
        qmT = ar.alloc("qmT", [128, 4, TB], BF16)
        omT = ar.alloc("omT", [128, 4, TB], BF16)
        xb = ar.alloc("xb", [128, 8, TB], BF16)
        if G.prompt:
            Wk = ar.alloc("Wk", [128, 8, 512], BF16)
            Wv = ar.alloc("Wv", [128, 8, 512], BF16)
            mst = [ar.alloc(f"mst{i}", [128, 512], F32) for i in range(2)]
            wload(Wk[:], kpn(w_mk[l]))
            wload(Wv[:], kpn(w_mv[l]))
        else:
            kmp = ar.alloc("kmp", [128, 2, 512], BF16)
        wload(Wq[:], kpn(w_xq[l]))
        wload(Wxo[:], kpn(w_xo[l]))
        for si in range(G.nseq):
            gs = G.gseq(si)
            if G.prompt:
                for h in range(4):
                    for k in range(8):
                        O.mm(ps[4][:, 0:N_MEM], Wk[:, k, h * 128:(h + 1) * 128], memT[:, k, :], start=(k == 0), stop=(k == 7))
                    O.copy(KTm[:, h, :], ps[4][:, 0:N_MEM], eng=evac_eng())
                for mt in range(2):
                    mc = slice(mt * 128, (mt + 1) * 128)
                    for wi, (W_, o_) in enumerate(((Wk, o_mk), (Wv, o_mv))):
                        pb = ps[5 + wi]
                        for k in range(8):
                            O.mm(pb[:, 0:512], memT[:, k, mc], W_[:, k, :], start=(k == 0), stop=(k == 7))
                        st = mst[wi]
                        O.copy(st[:], pb[:, 0:512], eng="act")
                        if wi == 1:
                            O.copy(Vm[:, mt, :], pb[:, 0:512], eng="dve")
                        O.dma(o_[l, gs, mc, :], st[:])
            else:
                wload(kmp[:], ci["mk"][l, gs].rearrange("(t p) n -> p t n", p=128))
                wload(Vm[:], ci["mv"][l, gs].rearrange("(t p) n -> p t n", p=128))
                for mt in range(2):
                    for h in range(4):
                        O.tr(psb(4)[:, h * 128:(h + 1) * 128], kmp[:, mt, h * 128:(h + 1) * 128], ident_b[:])
                    O.copy(KTm[:, :, mt * 128:(mt + 1) * 128], psb(4)[:, 0:512].rearrange("p (h n) -> p h n", h=4),
                           eng=evac_eng())
            for b in range(NB):
                x0 = G.col0(si) + b * TB
                bcols = slice(x0, x0 + TB)
                O.copy(xb[:], xf[:, :, bcols], eng="dve")
                for h in range(4):
                    for k in range(8):
                        O.mm(ps[4 + h % 2][:, 0:TB], Wq[:, k, h * 128:(h + 1) * 128], xb[:, k, :], start=(k == 0), stop=(k == 7))
                    O.copy(qmT[:, h, :], ps[4 + h % 2][:, 0:TB], eng=evac_eng())
                attention(ar_attn(ar), 2,
                          lambda h, kt: KTm[:, h, kt * 128:(kt + 1) * 128],
                          lambda h: qmT[:, h, :],
                          lambda h, kt: Vm[:, kt, h * 128:(h + 1) * 128],
                          128 ** -0.5,
                          lambda hp: omT[:, hp, :],
                          TB, dv=128)
                for oc in range(8):
                    pb = ps[4 + oc % 2]
                    for k in range(4):
                        O.mm(pb[:, 0:TB], Wxo[:, k, oc * 128:(oc + 1) * 128], omT[:, k, :], start=(k == 0), stop=(k == 3))
                    O.stt(xf[:, oc, bcols], xf[:, oc, bcols], ALPHA, pb[:, 0:TB], ALU.mult, ALU.add)
                ln_block(ar_attn(ar), bcols, lnp[:, l, 1, 0, :], lnp[:, l, 1, 1, :])

    def p6_ffn(G, l):
        ar = Arena(S, ARENA0, SB_LIMIT)
        blocks = dense_blocks(G)
        n = blocks[0].stop - blocks[0].start
        xbf = mixT
        W1 = [ar.alloc(f"W1_{i}", [128, 8, 1024], BF16) for i in range(2)]
        W2 = [ar.alloc(f"W2_{i}", [128, 8, 1024], BF16) for i in range(2)]
        hb = [ar.alloc(f"hb{i}", [128, 8, n], BF16) for i in range(2)]
        relu = [ar.alloc(f"relu{i}", [128, n], F32) for i in range(2)]
        for bcols in blocks:
            O.copy(xbf[:, :, bcols], xf[:, :, bcols], eng="act")
            O.ts(xf[:, :, bcols], xf[:, :, bcols], ALPHA, None, ALU.mult)
        for q in range(4):
            w1, w2 = W1[q % 2], W2[q % 2]
            wload(w1[:], kpn(w_ff1[l, :, q * 1024:(q + 1) * 1024]))
            wload(w2[:], kpn(w_ff2[l, q * 1024:(q + 1) * 1024, :]))
            for bi, bcols in enumerate(blocks):
                hh = hb[bi % 2]
                for hc in range(8):
                    pb = ps[hc % 4]
                    for k in range(8):
                        O.mm(pb[:, 0:n], w1[:, k, hc * 128:(hc + 1) * 128], xbf[:, k, bcols], start=(k == 0), stop=(k == 7))
                    rl = relu[hc % 2]
                    O.act(rl[:], pb[:, 0:n], AF.Relu)
                    O.tt(hh[:, hc, :], rl[:], rl[:], ALU.mult, eng="pool")
                for oc in range(8):
                    pb = ps[4 + oc % 4]
                    for k in range(8):
                        O.mm(pb[:, 0:n], w2[:, k, oc * 128:(oc + 1) * 128], hh[:, k, :], start=(k == 0), stop=(k == 7))
                    O.tt(xf[:, oc, bcols], xf[:, oc, bcols], pb[:, 0:n], ALU.add)
        for bcols in blocks:
            ln_block(Arena(S, ARENA0, SB_LIMIT), bcols, lnp[:, l, 2, 0, :], lnp[:, l, 2, 1, :])

    groups = []
    for s in range(NP):
        G = Grp()
        G.prompt, G.nseq, G.T, G.tp, G.past = True, 1, T, 128, 0
        G.gseq = lambda si, s=s: s
        G.x_in = lambda si, s=s: x_p[s]
        G.y_out = lambda si, s=s: y_p[s]
        G.out = lambda name, l, si, s=s: po[name][l, s]
        G.rope = lambda ti: (ropeC[:, ti, :], ropeS[:, ti, :])
        groups.append(G)
    if NS:
        G = Grp()
        G.prompt, G.nseq, G.T, G.tp, G.past = False, NS, TS, TS, PAST
        G.gseq = lambda si: si
        G.x_in = lambda si: x_s[si]
        G.y_out = lambda si: y_s[si]
        G.out = lambda name, l, si: so[name][l, si]
        G.rope = lambda ti: (ropeCs[0:TS, 0, :], ropeSs[0:TS, 0, :])
        groups.append(G)
    for G in groups:
        G.NT = G.T // G.tp
        G.TB = min(512, G.T)
        G.NB = G.T // G.TB
        G.TPB = G.TB // G.tp
        G.NPT = G.past // 128
        G.col0 = lambda si, G=G: si * G.T
        input_stage(G)
        for l in range(L if cfg.upto >= 1 else 0):
            p1_mla(G, l)
            if cfg.upto < 2:
                continue
            p2_fox(G, l)
            if cfg.upto < 3:
                continue
            if cfg.gdn:
                p3_gdn(G, l)
            else:
                O.memset(mixT[:, 2:6, 0:G.nseq * G.T], 0.0, eng="pool")
            if cfg.upto < 4:
                continue
            p4_out(G, l)
            if cfg.upto < 5:
                continue
            p5_mem(G, l)
            if cfg.upto < 6:
                continue
            p6_ffn(G, l)
        final_stage(G)

    S.emit()
    return nc, S


def ar_attn(ar):
    return Arena(ar.S, ar.off, ar.limit)


def _fm(v):
    return np.ascontiguousarray(np.asarray(v, np.float32).reshape(8, 128).T)


def prep_core_inputs(cfg, inp, core, consts):
    NP, NS, L = cfg.n_prompt, cfg.n_sample, cfg.depth
    m = {}
    if NP:
        sl = slice(core * NP, (core + 1) * NP)
        m["x_prompt"] = np.ascontiguousarray(inp["x_prompt"][sl, :cfg.T])
        m["mem_prompt"] = np.ascontiguousarray(inp["mem_prompt"][sl])
    if NS:
        ss = slice(core * NS, (core + 1) * NS)
        m["x_sample"] = np.ascontiguousarray(inp["x_sample"][ss])
        for k in ("cache_mla_ckv", "cache_mla_kpe", "cache_fox_logf", "state_gdn", "state_gdn_conv"):
            m[k] = np.ascontiguousarray(np.asarray(inp[k], np.float32)[:L, ss])
        for k in ("cache_fox_k", "cache_fox_v"):
            a = np.asarray(inp[k], np.float32)[:L, ss]
            m[k] = np.ascontiguousarray(a.reshape(a.shape[0], a.shape[1], a.shape[2], 256))
        for k in ("cache_mem_k", "cache_mem_v"):
            a = np.asarray(inp[k], np.float32)[:L, ss]
            m[k] = np.ascontiguousarray(a.reshape(a.shape[0], a.shape[1], N_MEM, 512))
        m["ropeCs"] = consts["ropeCs"]
        m["ropeSs"] = consts["ropeSs"]
    for k in ("w_in", "w_uq", "w_ukv", "w_out", "w_xq", "w_mk", "w_mv", "w_xo", "w_ff1", "w_ff2",
              "qa_g", "kva_g", "fox_bf"):
        m[k] = np.ascontiguousarray(np.asarray(inp[k], np.float32)[:L])
    m["ln_in_fm"] = np.ascontiguousarray(np.stack([_fm(inp["ln_in_g"]), _fm(inp["ln_in_b"])], 1))
    lg = np.asarray(inp["ln_g"], np.float32)[:L]
    lb = np.asarray(inp["ln_b"], np.float32)[:L]
    arr = np.zeros((128, L, 3, 2, 8), np.float32)
    for l in range(L):
        for w in range(3):
            arr[:, l, w, 0, :] = _fm(lg[l, w])
            arr[:, l, w, 1, :] = _fm(lb[l, w])
    m["ln_fm"] = arr
    cw = np.asarray(inp["gdn_conv_w"], np.float32)[:L]
    m["conv_fm"] = np.ascontiguousarray(cw.reshape(L, 4, 12, 128).transpose(3, 0, 2, 1))
    for k in ("gdn_a_log", "gdn_dt_bias", "gdn_norm_g"):
        m[k] = np.ascontiguousarray(np.asarray(inp[k], np.float32)[:L])
    for k in ("ident_f", "triu_f", "ones_f", "ropeC", "ropeS", "Mle_g", "Mgt_g", "Sel0_g", "Sel1_g"):
        m[k] = consts[k]
    return m


N_CORES = 8
_OUT_ORDER = ["y_prompt", "y_sample", "p_mla_ckv", "p_mla_kpe", "p_fox_k", "p_fox_v", "p_fox_logf", "p_gdn",
              "p_gdn_conv", "p_mem_k", "p_mem_v", "s_mla_ckv", "s_mla_kpe", "s_fox_k", "s_fox_v", "s_fox_logf",
              "s_gdn", "s_gdn_conv"]


def kernel(**inputs):
    inp = {k: np.asarray(v) for k, v in inputs.items()}
    B, T = inp["x_prompt"].shape[:2]
    BS, TS = inp["x_sample"].shape[:2]
    past = inp["cache_mla_ckv"].shape[2]
    cfg = Cfg(n_prompt=B // N_CORES, T=T, n_sample=BS // N_CORES, TS=TS, past=past, depth=inp["w_in"].shape[0])
    nc, _ = build(cfg)
    consts = host_consts(cfg)
    in_maps = [prep_core_inputs(cfg, inp, c, consts) for c in range(N_CORES)]
    res = run_bass_kernel_spmd(nc, in_maps, core_ids=list(range(N_CORES)))
    R = res.results
    outs = []
    for name in _OUT_ORDER:
        if name in ("y_prompt", "y_sample"):
            a = np.concatenate([r[name] for r in R], axis=0)
        else:
            a = np.concatenate([r[name] for r in R], axis=1)
        if name in ("p_fox_k", "p_fox_v", "s_fox_k", "s_fox_v"):
            a = a.reshape(a.shape[0], a.shape[1], a.shape[2], FOX_H, FOX_HD)
        elif name in ("p_mem_k", "p_mem_v"):
            a = a.reshape(a.shape[0], a.shape[1], N_MEM, MEM_H, MEM_HD)
        outs.append(np.ascontiguousarray(a, dtype=np.float32))
    return tuple(outs)
```

```python
import math
import numpy as np
import concourse.bass as bass
import concourse.mybir as mybir
from concourse.bass_utils import run_bass_kernel_spmd

F32 = mybir.dt.float32
BF16 = mybir.dt.bfloat16
AF = mybir.ActivationFunctionType
ALU = mybir.AluOpType
AX = mybir.AxisListType

D = 1024
DEPTH = 4
MLA_H, MLA_NOPE, MLA_ROPE, MLA_V, MLA_QL, MLA_KVL = 4, 64, 32, 64, 384, 256
GDN_H, GDN_DK, GDN_DV, CONV_W = 4, 128, 128, 4
FOX_H, FOX_HD = 4, 64
N_MEM, MEM_H, MEM_HD = 256, 4, 128
D_FF = 4096
ALPHA = (2 * DEPTH) ** 0.25
LN_EPS = 1e-5
RMS_EPS = 1e-6
IN_DIM = 3500
C_CQ, C_KVA, C_GQ, C_GK, C_GV, C_GZ, C_GA, C_GB, C_FQ, C_FK, C_FV, C_FF = (
    0, 384, 672, 1184, 1696, 2208, 2720, 2724, 2728, 2984, 3240, 3496)

ENGS = ["pe", "act", "dve", "pool", "sp"]
CELL = 16
N_SB = 229376 // CELL
N_PS = 8
NCELL = N_SB + N_PS
DMA_RING = 20
DMA_RING_Q = {"pool": 6}
SB_BASE = 16512
SB_LIMIT = 229248


def _dsize(dt):
    s = str(dt)
    if "32" in s:
        return 4
    if "16" in s:
        return 2
    if "8" in s:
        return 1
    raise ValueError(s)


class Sched:
    def __init__(self, nc):
        self.nc = nc
        self.ops = []
        self.lw = np.full(NCELL, -1, np.int64)
        self.lr = {e: np.full(NCELL, -1, np.int64) for e in ENGS}
        self.lr_dma = np.full(NCELL, -1, np.int64)
        self.n_alloc = 0
        self.tag = ""
        self.psum = [nc.alloc_psum_tensor(f"psb{i}", [128, 512], F32) for i in range(N_PS)]
        self.psum_names = {p.name: i for i, p in enumerate(self.psum)}
        self.big = nc.alloc_sbuf_tensor_at("arena", [128, (SB_LIMIT - SB_BASE) // 2], BF16, offset=SB_BASE)

    def alloc(self, name, shape, dtype, off):
        n = int(np.prod(shape[1:]))
        nbytes = n * _dsize(dtype)
        assert off % 64 == 0 and off >= SB_BASE and off + nbytes <= SB_LIMIT, (name, off, nbytes)
        e0 = (off - SB_BASE) // 2
        v = self.big[0:shape[0], e0:e0 + nbytes // 2]
        if _dsize(dtype) == 4:
            v = v.bitcast(dtype)
        elif str(dtype) != str(BF16):
            raise ValueError(dtype)
        if len(shape) == 3:
            v = v.rearrange("p (a b) -> p a b", a=shape[1])
        elif len(shape) == 4:
            v = v.rearrange("p (a b c) -> p a b c", a=shape[1], b=shape[2])
        elif len(shape) == 5:
            v = v.rearrange("p (a b c d) -> p a b c d", a=shape[1], b=shape[2], c=shape[3])
        return v

    def _cells(self, ap):
        sp = str(ap.space)
        if sp == "PSUM":
            b = self.psum_names[ap.tensor.name]
            return N_SB + b, N_SB + b + 1
        pat = ap.ap
        pstride = pat[0][0]
        off = int(ap.offset)
        e0 = off % pstride if pstride > 0 else off
        span = 1
        for st, cnt in pat[1:]:
            span += abs(st) * (cnt - 1)
        esz = _dsize(ap.dtype)
        b0 = ap.tensor.manual_sbuf_range[0] + e0 * esz
        b1 = b0 + span * esz
        return b0 // CELL, (b1 + CELL - 1) // CELL

    def op(self, eng, fn, reads=(), writes=(), dma=False, est=None, lat=None):
        idx = len(self.ops)
        deps = set()
        oo = set()
        rc = [self._cells(a) for a in reads if a is not None and str(a.space) != "DRAM"]
        wc = [self._cells(a) for a in writes if a is not None and str(a.space) != "DRAM"]
        lw = self.lw

        def grab(arr, a, b):
            if b - a == 1:
                v = arr[a]
                if v >= 0:
                    deps.add(int(v))
                return
            sl = arr[a:b]
            mx = sl.max()
            if mx < 0:
                return
            if sl.min() == mx:
                deps.add(int(mx))
                return
            for v in np.unique(sl):
                if v >= 0:
                    deps.add(int(v))
        for a, b in rc:
            grab(lw, a, b)
            if a >= N_SB:
                for e in ENGS:
                    if e != eng:
                        grab(self.lr[e], a, b)
        for a, b in wc:
            grab(lw, a, b)
            for e in ENGS:
                grab(self.lr[e], a, b)
            grab(self.lr_dma, a, b)
        for a, b in rc:
            if dma:
                grab(self.lr_dma, a, b)
                self.lr_dma[a:b] = idx
            else:
                sl = self.lr[eng][a:b]
                if sl.max() >= 0:
                    for v in np.unique(sl):
                        if v >= 0:
                            oo.add(int(v))
                self.lr[eng][a:b] = idx
        for a, b in wc:
            lw[a:b] = idx
            for e in ENGS:
                self.lr[e][a:b] = -1
            self.lr_dma[a:b] = -1
        deps.discard(idx)
        oo -= deps
        oo.discard(idx)
        self.ops.append(dict(eng=eng, fn=fn, deps=deps, oo=oo, dma=dma, tag=self.tag, est=est if est is not None else 0.3,
                             lat=lat if lat is not None else 0.0))
        return idx

    def reschedule(self, window=6000, xlat=0.2):
        import heapq
        ops = self.ops
        n = len(ops)
        succ = [[] for _ in range(n)]
        indeg = [0] * n
        for i, o in enumerate(ops):
            for d in o["deps"]:
                succ[d].append(i)
            for d in o["oo"]:
                succ[d].append(i)
            indeg[i] = len(o["deps"]) + len(o["oo"])
        eng_id = {e: k for k, e in enumerate(ENGS)}
        eng = [eng_id[o["eng"]] for o in ops]
        finish = [0.0] * n
        readyt = [0.0] * n
        pending = [[] for _ in ENGS]
        ready = [[] for _ in ENGS]
        future = []
        free_at = [0.0] * len(ENGS)
        placed = [False] * n
        critp = [None] * n
        last_on = [None] * len(ENGS)
        lo = 0
        order = []

        def release(i):
            if i >= lo + window:
                heapq.heappush(future, i)
            else:
                heapq.heappush(pending[eng[i]], (readyt[i], i))
        for i in range(n):
            if indeg[i] == 0:
                release(i)
        while len(order) < n:
            best = None
            for e in range(len(ENGS)):
                pe_, re_ = pending[e], ready[e]
                fa = free_at[e]
                while pe_ and pe_[0][0] <= fa:
                    heapq.heappush(re_, heapq.heappop(pe_)[1])
                if re_:
                    cand = (fa, re_[0], e, True)
                elif pe_:
                    cand = (pe_[0][0], pe_[0][1], e, False)
                else:
                    continue
                if best is None or cand[:2] < best[:2]:
                    best = cand
            if best is None:
                i = heapq.heappop(future)
                heapq.heappush(pending[eng[i]], (readyt[i], i))
                continue
            st, i, e, from_ready = best
            if from_ready:
                heapq.heappop(ready[e])
            else:
                heapq.heappop(pending[e])
            o = ops[i]
            free_at[e] = st + o["est"]
            finish[i] = st + o["est"] + o["lat"]
            placed[i] = True
            o["t0"], o["t1"] = st, finish[i]
            order.append(i)
            o["why"] = ("dep", critp[i]) if (critp[i] is not None and readyt[i] >= st - 1e-9) else ("eng", last_on[e])
            last_on[e] = i
            for j in succ[i]:
                t = finish[i] + (xlat if eng[j] != e or o["dma"] else 0.05)
                if t > readyt[j]:
                    readyt[j] = t
                    critp[j] = i
                indeg[j] -= 1
                if indeg[j] == 0:
                    release(j)
            while lo < n and placed[lo]:
                lo += 1
            while future and future[0] < lo + window:
                k = heapq.heappop(future)
                heapq.heappush(pending[eng[k]], (readyt[k], k))
        newpos = [0] * n
        for p_, i in enumerate(order):
            newpos[i] = p_
        for o in ops:
            k_, w_ = o.get("why", ("eng", None))
            o["why"] = (k_, newpos[w_] if w_ is not None else None)
        new_ops = []
        for i in order:
            o = ops[i]
            o["deps"] = {newpos[d] for d in o["deps"]}
            o["oo"] = {newpos[d] for d in o["oo"]}
            new_ops.append(o)
        self.ops = new_ops
        self.sim_time = max(finish) if n else 0.0

    def emit(self):
        nc = self.nc
        ops = self.ops
        eng_of = [o["eng"] for o in ops]
        is_dma = [o["dma"] for o in ops]
        signal = [False] * len(ops)
        for o in ops:
            nd = {}
            dm = []
            for d in o["deps"]:
                if is_dma[d]:
                    dm.append(d)
                else:
                    e = eng_of[d]
                    if e == "pe" and o["eng"] == "pe" and not o["dma"]:
                        continue
                    if nd.get(e, -1) < d:
                        nd[e] = d
            o["cdeps"] = nd
            o["ddeps"] = dm
            for d in nd.values():
                signal[d] = True
        cnt = {}
        run = {e: 0 for e in ENGS}
        dma_n = {e: 0 for e in ENGS}
        dma_slot = {}
        for i, o in enumerate(ops):
            if o["dma"]:
                n = dma_n[o["eng"]]
                rq = DMA_RING_Q.get(o["eng"], DMA_RING)
                dma_slot[i] = (o["eng"], n % rq, n // rq + 1)
                dma_n[o["eng"]] = n + 1
            elif signal[i]:
                run[o["eng"]] += 1
                cnt[i] = run[o["eng"]]
        dma_engs = [e for e in ENGS if dma_n[e] > 0]
        sems = {e: nc.alloc_semaphore(f"s_{e}") for e in ENGS}
        dsem = {e: [nc.alloc_semaphore(f"d_{e}{k}") for k in range(DMA_RING)] for e in dma_engs}
        by_eng = {e: [i for i, o in enumerate(ops) if o["eng"] == e] for e in ENGS}
        self.stats = {e: len(by_eng[e]) for e in ENGS}

        def run_engine(e, eng):
            waited = {}
            nwait = 0
            for i in by_eng[e]:
                o = ops[i]
                for pe_, d in o["cdeps"].items():
                    c = cnt[d]
                    if waited.get(pe_, 0) < c:
                        eng.wait_ge(sems[pe_], c)
                        waited[pe_] = c
                        nwait += 1
                for d in o["ddeps"]:
                    qe, slot, use = dma_slot[d]
                    key = (qe, slot)
                    if waited.get(key, 0) < use:
                        eng.wait_ge(dsem[qe][slot], 16 * use)
                        waited[key] = use
                        nwait += 1
                if o["dma"]:
                    qe, slot, use = dma_slot[i]
                    key = (qe, slot)
                    if use > 1 and waited.get(key, 0) < use - 1:
                        eng.wait_ge(dsem[qe][slot], 16 * (use - 1))
                        waited[key] = use - 1
                        nwait += 1
                    o["fn"](eng).then_inc(dsem[qe][slot], 16)
                else:
                    ins = o["fn"](eng)
                    if signal[i]:
                        ins.then_inc(sems[e], 1)
            if e in dma_engs:
                n = dma_n[e]
                rq = DMA_RING_Q.get(e, DMA_RING)
                for slot in range(min(n, rq)):
                    use = (n - 1 - slot) // rq + 1
                    eng.wait_ge(dsem[e][slot], 16 * use)
            self.stats[e + "_waits"] = nwait

        with nc.Block() as block:
            @block.tensor
            def _(eng):
                run_engine("pe", eng)

            @block.scalar
            def _(eng):
                run_engine("act", eng)

            @block.vector
            def _(eng):
                run_engine("dve", eng)

            @block.gpsimd
            def _(eng):
                run_engine("pool", eng)

            @block.sync
            def _(eng):
                run_engine("sp", eng)


class Arena:
    def __init__(self, S, base, limit):
        self.S, self.base, self.off, self.limit = S, base, base, limit

    def alloc(self, name, shape, dtype):
        nbytes = int(np.prod(shape[1:])) * _dsize(dtype)
        off = self.off
        self.off = (off + nbytes + 63) // 64 * 64
        assert self.off <= self.limit, (name, self.off, self.limit)
        return self.S.alloc(name, shape, dtype, off)


class Ops:
    def __init__(self, S):
        self.S = S

    def mm(self, out, lhsT, rhs, start=True, stop=True):
        cols = _free(rhs)
        est = 0.05 + max(cols, 64) / 2400.0 * (4.0 if _dsize(rhs.dtype) == 4 else 1.0)
        self.S.op("pe", lambda e: e.matmul(out, lhsT, rhs, start=start, stop=stop, skip_group_check=True),
                  reads=[lhsT, rhs], writes=[out], est=est, lat=0.1)

    def tr(self, out, in_, ident):
        self.S.op("pe", lambda e: e.transpose(out, in_, ident), reads=[in_, ident], writes=[out],
                  est=0.12 * (4.0 if _dsize(in_.dtype) == 4 else 1.0), lat=0.1)

    def act(self, out, in_, func, bias=None, scale=1.0, accum=None, eng="act"):
        kw = {}
        rd = [in_]
        if bias is not None:
            kw["bias"] = bias
            if not isinstance(bias, (int, float)):
                rd.append(bias)
        if not isinstance(scale, (int, float)):
            rd.append(scale)
        kw["scale"] = scale
        wr = [out]
        if accum is not None:
            kw["accum_out"] = accum
            wr.append(accum)
        self.S.op("act", lambda e: e.activation(out, in_, func, **kw), reads=rd, writes=wr,
                  est=0.25 + _free(out) / 1200.0 + (0.1 if accum is not None else 0.0))

    def tt(self, out, in0, in1, op, eng="dve"):
        self.S.op(eng, lambda e: e.tensor_tensor(out, in0, in1, op), reads=[in0, in1], writes=[out], est=_vest(eng, out))

    def ts(self, out, in0, s1, s2, op0, op1=None, eng="dve", accum=None):
        rd = [in0] + [s for s in (s1, s2) if s is not None and not isinstance(s, (int, float))]
        wr = [out] + ([accum] if accum is not None else [])
        if op1 is None:
            self.S.op(eng, lambda e: e.tensor_scalar(out, in0, s1, None, op0), reads=rd, writes=wr, est=_vest(eng, out))
        elif accum is None:
            self.S.op(eng, lambda e: e.tensor_scalar(out, in0, s1, s2, op0, op1), reads=rd, writes=wr, est=_vest(eng, out))
        else:
            self.S.op(eng, lambda e: e.tensor_scalar(out, in0, s1, s2, op0, op1, accum_out=accum),
                      reads=rd, writes=wr)

    def stt(self, out, in0, scalar, in1, op0, op1, eng="dve"):
        rd = [in0, in1] + ([scalar] if not isinstance(scalar, (int, float)) else [])
        self.S.op(eng, lambda e: e.scalar_tensor_tensor(out, in0, scalar, in1, op0, op1), reads=rd, writes=[out],
                  est=_vest(eng, out))

    def copy(self, out, in_, eng="dve"):
        if eng == "act":
            self.S.op("act", lambda e: e.copy(out, in_), reads=[in_], writes=[out], est=0.25 + _free(out) / 1200.0)
        else:
            self.S.op(eng, lambda e: e.tensor_copy(out, in_), reads=[in_], writes=[out], est=_vest(eng, out))

    def memset(self, out, val, eng="dve"):
        self.S.op(eng, lambda e: e.memset(out, val), writes=[out], est=_vest(eng, out))

    def recip(self, out, in_):
        self.S.op("dve", lambda e: e.reciprocal(out, in_), reads=[in_], writes=[out], est=_vest("dve", out))

    def reduce(self, out, in_, op, axis=AX.X):
        self.S.op("dve", lambda e: e.tensor_reduce(out, in_, axis, op), reads=[in_], writes=[out], est=_vest("dve", in_))

    def dma(self, out, in_, q="sp"):
        nbytes = int(np.prod(out.shape)) * 4
        self.S.op(q, lambda e: e.dma_start(out=out, in_=in_), reads=[in_], writes=[out], dma=True,
                  est=0.1 + nbytes / 300e3 * 0.5, lat=2.0 + nbytes / 300e3 * 0.5)


def _free(ap):
    n = 1
    for d in ap.shape[1:]:
        n *= int(d)
    return n


def _vest(eng, ap):
    n = _free(ap)
    return (0.16 + n / 960.0) if eng == "dve" else (0.35 + n / 400.0)


def bc(ap, shape):
    return ap.broadcast_to(list(shape))


class Cfg:
    def __init__(self, n_prompt=4, T=2048, n_sample=2, TS=32, past=1024, depth=DEPTH, gdn=True, upto=9):
        self.upto = upto
        self.resched = True
        self.window = 6000
        self.n_prompt, self.T, self.n_sample, self.TS, self.past, self.depth, self.gdn = (
            n_prompt, T, n_sample, TS, past, depth, gdn)


def host_consts(cfg):
    c = {}
    c["ident_f"] = np.eye(128, dtype=np.float32)
    m = np.arange(128)
    c["triu_f"] = (m[:, None] <= m[None, :]).astype(np.float32)
    c["ones_f"] = np.ones((128, 128), np.float32)
    for nm, ch in (("g", 64), ("gs", 32)):
        same = (m[:, None] // ch) == (m[None, :] // ch)
        c["Mle_" + nm] = ((m[:, None] <= m[None, :]) & same).astype(np.float32)
        c["Mgt_" + nm] = ((m[:, None] > m[None, :]) & same).astype(np.float32)
        c["Sel0_" + nm] = np.repeat(((m // ch) == 0).astype(np.float32)[:, None], 128, 1)
        c["Sel1_" + nm] = np.repeat(((m // ch) == 1).astype(np.float32)[:, None], 128, 1)
    half = MLA_ROPE // 2
    inv = (10000.0 ** (-np.arange(half, dtype=np.float32) * (2.0 / MLA_ROPE))).astype(np.float32)

    def rope_tab(pos):
        ang = pos.astype(np.float32)[:, None] * inv[None, :]
        cs, sn = np.cos(ang).astype(np.float32), np.sin(ang).astype(np.float32)
        C = np.concatenate([cs, cs], 1)
        Sg = np.concatenate([-sn, sn], 1)
        return C, Sg
    C, Sg = rope_tab(np.arange(max(cfg.T, 128)))
    nt = C.shape[0] // 128
    c["ropeC"] = np.ascontiguousarray(C.reshape(nt, 128, 32).transpose(1, 0, 2))
    c["ropeS"] = np.ascontiguousarray(Sg.reshape(nt, 128, 32).transpose(1, 0, 2))
    C, Sg = rope_tab(cfg.past + np.arange(cfg.TS))
    c["ropeCs"] = C
    c["ropeSs"] = Sg
    return c


class Grp:
    pass


def build(cfg):
    nc = bass.Bass("TRN2", target_bir_lowering=False)
    S = Sched(nc)
    O = Ops(S)
    L = cfg.depth
    T = cfg.T
    NP = cfg.n_prompt
    NS = cfg.n_sample
    TS = cfg.TS
    PAST = cfg.past
    NTP = max(T // 128, 1)

    def din(name, shape):
        return nc.dram_tensor(name, list(shape), F32, kind="ExternalInput").ap()

    def dout(name, shape):
        return nc.dram_tensor(name, list(shape), F32, kind="ExternalOutput").ap()

    w_in = din("w_in", [L, D, IN_DIM])
    w_uq = din("w_uq", [L, MLA_QL, 384])
    w_ukv = din("w_ukv", [L, MLA_KVL, 512])
    w_out = din("w_out", [L, D, D])
    w_xq = din("w_xq", [L, D, 512])
    w_mk = din("w_mk", [L, D, 512])
    w_mv = din("w_mv", [L, D, 512])
    w_xo = din("w_xo", [L, 512, D])
    w_ff1 = din("w_ff1", [L, D, D_FF])
    w_ff2 = din("w_ff2", [L, D_FF, D])
    qa_g = din("qa_g", [L, 384])
    kva_g = din("kva_g", [L, 256])
    fox_bf = din("fox_bf", [L, 4])
    ln_in_fm = din("ln_in_fm", [128, 2, 8])
    ln_fm = din("ln_fm", [128, L, 3, 2, 8])
    conv_fm = din("conv_fm", [128, L, 12, 4])
    a_log = din("gdn_a_log", [L, 4])
    dt_bias = din("gdn_dt_bias", [L, 4])
    norm_g = din("gdn_norm_g", [L, 128])
    c_Mle = din("Mle_g", [128, 128])
    c_Mgt = din("Mgt_g", [128, 128])
    c_Sel0 = din("Sel0_g", [128, 128])
    c_Sel1 = din("Sel1_g", [128, 128])
    c_ident_f = din("ident_f", [128, 128])
    c_triu_f = din("triu_f", [128, 128])
    c_ones_f = din("ones_f", [128, 128])
    c_ropeC = din("ropeC", [128, NTP, 32])
    c_ropeS = din("ropeS", [128, NTP, 32])

    if NP:
        x_p = din("x_prompt", [NP, T, D])
        mem_p = din("mem_prompt", [NP, N_MEM, D])
        y_p = dout("y_prompt", [NP, T, D])
        po = dict(ckv=dout("p_mla_ckv", [L, NP, T, 256]), kpe=dout("p_mla_kpe", [L, NP, T, 32]),
                  fk=dout("p_fox_k", [L, NP, T, 256]), fv=dout("p_fox_v", [L, NP, T, 256]),
                  logf=dout("p_fox_logf", [L, NP, T, 4]), gdn=dout("p_gdn", [L, NP, 4, 128, 128]),
                  conv=dout("p_gdn_conv", [L, NP, 3, 1536]))
        o_mk = dout("p_mem_k", [L, NP, N_MEM, 512])
        o_mv = dout("p_mem_v", [L, NP, N_MEM, 512])
    if NS:
        x_s = din("x_sample", [NS, TS, D])
        y_s = dout("y_sample", [NS, TS, D])
        ci = dict(ckv=din("cache_mla_ckv", [L, NS, PAST, 256]), kpe=din("cache_mla_kpe", [L, NS, PAST, 32]),
                  fk=din("cache_fox_k", [L, NS, PAST, 256]), fv=din("cache_fox_v", [L, NS, PAST, 256]),
                  logf=din("cache_fox_logf", [L, NS, PAST, 4]), gdn=din("state_gdn", [L, NS, 4, 128, 128]),
                  conv=din("state_gdn_conv", [L, NS, 3, 1536]),
                  mk=din("cache_mem_k", [L, NS, N_MEM, 512]), mv=din("cache_mem_v", [L, NS, N_MEM, 512]))
        so = dict(ckv=dout("s_mla_ckv", [L, NS, TS, 256]), kpe=dout("s_mla_kpe", [L, NS, TS, 32]),
                  fk=dout("s_fox_k", [L, NS, TS, 256]), fv=dout("s_fox_v", [L, NS, TS, 256]),
                  logf=dout("s_fox_logf", [L, NS, TS, 4]), gdn=dout("s_gdn", [L, NS, 4, 128, 128]),
                  conv=dout("s_gdn_conv", [L, NS, 3, 1536]))
        c_ropeCs = din("ropeCs", [TS, 32])
        c_ropeSs = din("ropeSs", [TS, 32])

    main = Arena(S, SB_BASE, SB_LIMIT)
    ident_f = main.alloc("ident_f", [128, 128], F32)
    ident_b = main.alloc("ident_b", [128, 128], BF16)
    ones_b = main.alloc("ones_b", [128, 128], BF16)
    triu_f = main.alloc("triu_f", [128, 128], F32)
    triu_b = main.alloc("triu_b", [128, 128], BF16)
    ones_f = main.alloc("ones_f", [128, 128], F32)
    ropeCs = main.alloc("ropeCs", [128, 1, 32], F32)
    ropeSs = main.alloc("ropeSs", [128, 1, 32], F32)
    Mle = main.alloc("Mle", [128, 128], F32)
    Mgt = main.alloc("Mgt", [128, 128], F32)
    Sel0 = main.alloc("Sel0", [128, 128], F32)
    Sel1 = main.alloc("Sel1", [128, 128], F32)
    cwt = main.alloc("cwt", [128, L, 12, 4], F32)
    lnin = main.alloc("lnin", [128, 2, 8], F32)
    lnp = main.alloc("lnp", [128, L, 3, 2, 8], F32)
    TX = max(T if NP else 0, NS * TS)
    xf = main.alloc("xf", [128, 8, TX], F32)
    mixT = main.alloc("mixT", [128, 8, TX], BF16)
    memT = main.alloc("memT", [128, 8, N_MEM], BF16)
    ARENA0 = main.off

    O.dma(ident_f[:], c_ident_f)
    O.dma(triu_f[:], c_triu_f)
    O.dma(ones_f[:], c_ones_f)
    if NS:
        O.dma(ropeCs[0:TS, 0, :], c_ropeCs)
        O.dma(ropeSs[0:TS, 0, :], c_ropeSs)
    O.dma(Mle[:], c_Mle)
    O.dma(Mgt[:], c_Mgt)
    O.dma(Sel0[:], c_Sel0)
    O.dma(Sel1[:], c_Sel1)
    O.dma(cwt[:], conv_fm)
    O.dma(lnin[:], ln_in_fm)
    O.dma(lnp[:], ln_fm)
    O.copy(ident_b[:], ident_f[:])
    O.copy(triu_b[:], triu_f[:])
    O.copy(ones_b[:], ones_f[:])

    ps = S.psum

    def psb(i):
        return ps[i][:].bitcast(BF16)

    rr = {"ev": 0}

    def evac_eng():
        rr["ev"] ^= 1
        return "act" if rr["ev"] else "dve"

    def wload(dst, src):
        O.dma(dst, src, q="pool")

    def kpn(src):
        return src.rearrange("(k p) n -> p k n", p=128)

    def pbc(src_row):
        return src_row.partition_broadcast(128).rearrange("p o n -> p (o n)")

    def ln_block(ar, cols, g_ap, b_ap):
        n = cols.stop - cols.start
        v16 = ar.alloc("ln_v16", [128, 8, n], BF16)
        sq16 = ar.alloc("ln_sq16", [128, 8, n], BF16)
        mean = ar.alloc("ln_mean", [128, n], F32)
        msq = ar.alloc("ln_msq", [128, n], F32)
        rstd = ar.alloc("ln_rstd", [128, n], F32)
        xv = xf[:, :, cols]
        O.copy(v16[:], xv, eng="act")
        O.act(sq16[:], xv, AF.Square)
        for c in range(8):
            O.mm(ps[6][:, 0:n], ones_b[:], v16[:, c, :], start=(c == 0), stop=(c == 7))
        for c in range(8):
            O.mm(ps[7][:, 0:n], ones_b[:], sq16[:, c, :], start=(c == 0), stop=(c == 7))
        O.act(mean[:], ps[6][:, 0:n], AF.Copy, scale=1.0 / D)
        O.tt(msq[:], mean[:], mean[:], ALU.mult)
        O.stt(rstd[:], ps[7][:, 0:n], 1.0 / D, msq[:], ALU.mult, ALU.subtract)
        O.ts(rstd[:], rstd[:], LN_EPS, None, ALU.add)
        O.act(rstd[:], rstd[:], AF.Sqrt)
        O.recip(rstd[:], rstd[:])
        for hf in range(2):
            xh = xf[:, hf * 4:hf * 4 + 4, cols]
            O.tt(xh, xh, bc(mean[:].unsqueeze(1), [128, 4, n]), ALU.subtract)
            O.tt(xh, xh, bc(rstd[:].unsqueeze(1), [128, 4, n]), ALU.mult)
            for c in range(hf * 4, hf * 4 + 4):
                O.act(xf[:, c, cols], xf[:, c, cols], AF.Identity, bias=b_ap[:, c:c + 1], scale=g_ap[:, c:c + 1])

    def attention(ar, n_kt, kt_ops, qt_ops, v_ops, scale, out_dst, nq, bias_fn=None, mask_fn=None,
                  q0_fn=None, heads=4, dv=64, rows_fn=None):
        E = [ar.alloc(f"attE{i}", [128, nq], BF16) for i in range(3)]
        rden = ar.alloc("att_rden", [128, nq], F32)
        ei = 0
        group = 2 if dv == 64 else 1
        for h0 in range(0, heads, group):
            hs = list(range(h0, h0 + group))
            first = {h: True for h in hs}
            last_kt = n_kt - 1
            for kt in range(n_kt):
                c0 = q0_fn(kt) if q0_fn else 0
                if c0 >= nq:
                    continue
                r = rows_fn(kt) if rows_fn else 128
                for h in hs:
                    sb = ei % 2
                    Et = E[ei % 3]
                    ei += 1
                    O.mm(ps[sb][0:r, c0:nq], kt_ops(h, kt), qt_ops(h)[:, c0:nq])
                    b_ap = bias_fn(h, kt) if bias_fn else None
                    O.act(Et[0:r, c0:nq], ps[sb][0:r, c0:nq], AF.Exp, bias=b_ap, scale=scale)
                    if mask_fn:
                        mask_fn(kt, Et, c0)
                    po_ = (h - h0) * 64 if dv == 64 else 0
                    pn = 64 if dv == 64 else 128
                    O.mm(ps[2][po_:po_ + pn, c0:nq], v_ops(h, kt), Et[0:r, c0:nq], start=first[h], stop=(kt == last_kt))
                    O.mm(ps[3][po_:po_ + pn, c0:nq], ones_b[0:r, 0:pn], Et[0:r, c0:nq], start=first[h], stop=(kt == last_kt))
                    first[h] = False
            O.recip(rden[:], ps[3][:, 0:nq])
            O.tt(out_dst(h0 // group), ps[2][:, 0:nq], rden[:], ALU.mult)

    def load_ln_transpose(src_rows, tp, dst_cols, scale_ap, bias_ap, xt, scratch, normalize=True):
        st6, mv, rs = scratch
        O.dma(xt[0:tp, :], src_rows)
        if normalize:
            for j in range(2):
                S.op("dve", lambda e, j=j: e.bn_stats(st6[0:tp, j, :], xt[0:tp, j * 512:(j + 1) * 512]),
                     reads=[xt[0:tp, j * 512:(j + 1) * 512]], writes=[st6[0:tp, j, :]])
            S.op("dve", lambda e: e.bn_aggr(mv[0:tp, :], st6[0:tp].rearrange("p a b -> p (a b)")),
                 reads=[st6[0:tp]], writes=[mv[0:tp, :]])
            O.ts(rs[0:tp, :], mv[0:tp, 1:2], LN_EPS, None, ALU.add)
            O.act(rs[0:tp, :], rs[0:tp, :], AF.Sqrt)
            O.recip(rs[0:tp, :], rs[0:tp, :])
            O.ts(xt[0:tp, :], xt[0:tp, :], mv[0:tp, 0:1], rs[0:tp, 0:1], ALU.subtract, ALU.mult)
        for half in range(2):
            pb = ps[4 + half]
            for c4 in range(4):
                c = half * 4 + c4
                O.tr(pb[:, c4 * tp:(c4 + 1) * tp], xt[0:tp, c * 128:(c + 1) * 128], ident_f[0:tp, 0:tp])
            if normalize:
                for c4 in range(4):
                    c = half * 4 + c4
                    O.act(xf[:, c, dst_cols], pb[:, c4 * tp:(c4 + 1) * tp], AF.Identity,
                          bias=bias_ap[:, c:c + 1], scale=scale_ap[:, c:c + 1])
            else:
                O.copy(memT[:, half * 4:half * 4 + 4, dst_cols],
                       pb[:, 0:4 * tp].rearrange("p (c n) -> p c n", c=4), eng="act")

    def input_stage(G):
        ar = Arena(S, ARENA0, SB_LIMIT)
        xin = [ar.alloc(f"xin{i}", [128, D], F32) for i in range(2)]
        scratch = (ar.alloc("st6", [128, 2, 6], F32), ar.alloc("mv", [128, 2], F32), ar.alloc("rs", [128, 1], F32))
        n = 0
        for si in range(G.nseq):
            for ti in range(G.NT):
                c0 = G.col0(si) + ti * G.tp
                load_ln_transpose(G.x_in(si)[ti * G.tp:(ti + 1) * G.tp, :], G.tp, slice(c0, c0 + G.tp),
                                  lnin[:, 0, :], lnin[:, 1, :], xin[n % 2], scratch)
                n += 1
            if G.prompt:
                for mt in range(2):
                    load_ln_transpose(mem_p[G.gseq(si), mt * 128:(mt + 1) * 128, :], 128, slice(mt * 128, (mt + 1) * 128),
                                      None, None, xin[n % 2], scratch, normalize=False)
                    n += 1

    def final_stage(G):
        ar = Arena(S, ARENA0, SB_LIMIT)
        yst = [ar.alloc(f"yst{i}", [128, D], F32) for i in range(2)]
        n = 0
        tp = G.tp
        for si in range(G.nseq):
            for ti in range(G.NT):
                yt = yst[n % 2]
                n += 1
                c0 = G.col0(si) + ti * tp
                for half in range(2):
                    pb = ps[4 + half]
                    for c4 in range(4):
                        c = half * 4 + c4
                        O.tr(pb[0:tp, c4 * 128:(c4 + 1) * 128], xf[:, c, c0:c0 + tp], ident_f[:])
                    O.copy(yt[0:tp, half * 512:(half + 1) * 512], pb[0:tp, 0:512], eng=("act" if half else "dve"))
                O.dma(G.y_out(si)[ti * tp:(ti + 1) * tp, :], yt[0:tp, :])

    def p1_mla(G, l):
        tp, TB, TPB, NB, NPT = G.tp, G.TB, G.TPB, G.NB, G.NPT
        NKT = NPT + G.NT
        LK = G.past + G.T
        ar = Arena(S, ARENA0, SB_LIMIT)
        Wm = ar.alloc("Wm", [128, 8, 672], BF16)
        Wuq = ar.alloc("Wuq", [128, 3, 384], BF16)
        Wukv = ar.alloc("Wukv", [128, 2, 512], BF16)
        qag = ar.alloc("qag", [128, 384], F32)
        kvag = ar.alloc("kvag", [128, 256], F32)
        KT = ar.alloc("KT", [128, 4, LK], BF16)
        Vc = ar.alloc("Vc", [128, NKT, 4, 64], BF16)
        QT = ar.alloc("QT", [128, 4, TB], BF16)
        xb = ar.alloc("xb", [128, 8, TB], BF16)
        ckvf = [ar.alloc(f"ckvf{i}", [128, 256], F32) for i in range(2)]
        kpef = [ar.alloc(f"kpef{i}", [128, 32], F32) for i in range(2)]
        TB2 = [dict(cqn=ar.alloc(f"cqn{i}", [128, 384], BF16), ckvb=ar.alloc(f"ckvb{i}", [128, 256], BF16),
                    kpeb=ar.alloc(f"kpeb{i}", [128, 32], BF16), rt1=ar.alloc(f"rt1{i}", [128, 4, 32], F32),
                    rt2=ar.alloc(f"rt2{i}", [128, 4, 32], F32), cqT=ar.alloc(f"cqT{i}", [128, 3, 128], BF16),
                    ckvT=ar.alloc(f"ckvT{i}", [128, 2, 128], BF16), qtm=ar.alloc(f"qtm{i}", [128, 4, 96], BF16),
                    ktm=ar.alloc(f"ktm{i}", [128, 4, 96], BF16), ssq=ar.alloc(f"ssq{i}", [128, 16], F32),
                    junk=ar.alloc(f"junk{i}", [128, 384], BF16)) for i in range(2)]
        if NPT:
            ckvp = ar.alloc("ckvp", [128, NPT, 256], BF16)
            kpep = ar.alloc("kpep", [128, NPT, 32], BF16)
        if G.prompt:
            ropeC = ar.alloc("ropeC", [128, NTP, 32], F32)
            ropeS = ar.alloc("ropeS", [128, NTP, 32], F32)
            O.dma(ropeC[:], c_ropeC)
            O.dma(ropeS[:], c_ropeS)
            G.rope = lambda ti: (ropeC[:, ti, :], ropeS[:, ti, :])
        wload(Wm[:], kpn(w_in[l, :, C_CQ:C_CQ + 672]))
        wload(Wuq[:], kpn(w_uq[l]))
        wload(Wukv[:], kpn(w_ukv[l]))
        O.dma(qag[:], pbc(qa_g[l:l + 1, :]))
        O.dma(kvag[:], pbc(kva_g[l:l + 1, :]))

        def kv_tile(ckvb_ap, kpeb_ap, kt, r):
            kc = slice(kt * 128, kt * 128 + r)
            B_ = TB2[kt % 2]
            ckvT, ktm = B_["ckvT"], B_["ktm"]
            b5, b7 = (5, 7) if kt % 2 == 0 else (4, 6)
            for k in range(2):
                O.tr(psb(b7)[:, k * r:(k + 1) * r], ckvb_ap[:, k * 128:(k + 1) * 128], ident_b[0:r, 0:r])
            O.copy(ckvT[:, :, 0:r], psb(b7)[:, 0:2 * r].rearrange("p (k n) -> p k n", k=2), eng="dve")
            for k in range(2):
                O.mm(ps[b5][0:r, 0:512], ckvT[:, k, 0:r], Wukv[:, k, :], start=(k == 0), stop=(k == 1))
            kv3 = ps[b5][0:r, 0:512].rearrange("p (h d) -> p h d", h=4)
            O.copy(ktm[0:r, :, 0:64], kv3[:, :, 0:64], eng="act")
            O.copy(ktm[0:r, :, 64:96], bc(kpeb_ap.unsqueeze(1), [r, 4, 32]), eng="dve")
            O.copy(Vc[0:r, kt, :, :], kv3[:, :, 64:128], eng="act")
            for h in range(4):
                O.tr(psb(b7)[0:96, h * r:(h + 1) * r], ktm[0:r, h, :], ident_b[0:r, 0:r])
            O.copy(KT[0:96, :, kc], psb(b7)[0:96, 0:4 * r].rearrange("p (h n) -> p h n", h=4), eng="act")

        for si in range(G.nseq):
            if NPT:
                wload(ckvp[:], ci["ckv"][l, G.gseq(si)].rearrange("(t p) n -> p t n", p=128))
                wload(kpep[:], ci["kpe"][l, G.gseq(si)].rearrange("(t p) n -> p t n", p=128))
                for pt in range(NPT):
                    kv_tile(ckvp[:, pt, :], kpep[:, pt, :], pt, 128)
            for b in range(NB):
                x0 = G.col0(si) + b * TB
                bcols = slice(x0, x0 + TB)
                O.copy(xb[:], xf[:, :, bcols], eng="dve")
                for t in range(TPB):
                    ti = b * TPB + t
                    tc = slice(t * tp, (t + 1) * tp)
                    gc = slice(ti * tp, (ti + 1) * tp)
                    rC, rS = G.rope(ti)
                    B_ = TB2[(NPT + ti) % 2]
                    cqn, ckvb, kpeb, rt1, rt2, cqT, qtm, ssq, junk = (B_[k_] for k_ in (
                        "cqn", "ckvb", "kpeb", "rt1", "rt2", "cqT", "qtm", "ssq", "junk"))
                    ps4, ps5, ps6 = (ps[4], ps[5], 6) if (NPT + ti) % 2 == 0 else (ps[6], ps[4], 5)
                    for k in range(8):
                        O.mm(ps4[0:tp, 0:384], xb[:, k, tc], Wm[:, k, 0:384], start=(k == 0), stop=(k == 7))
                    for k in range(8):
                        O.mm(ps5[0:tp, 0:288], xb[:, k, tc], Wm[:, k, 384:672], start=(k == 0), stop=(k == 7))
                    O.act(junk[0:tp, 0:384], ps4[0:tp, 0:384], AF.Square, accum=ssq[0:tp, 0:1])
                    O.act(junk[0:tp, 0:256], ps5[0:tp, 0:256], AF.Square, accum=ssq[0:tp, 1:2])
                    O.ts(ssq[0:tp, 0:1], ssq[0:tp, 0:1], 1.0 / 384, RMS_EPS, ALU.mult, ALU.add)
                    O.ts(ssq[0:tp, 1:2], ssq[0:tp, 1:2], 1.0 / 256, RMS_EPS, ALU.mult, ALU.add)
                    O.act(ssq[0:tp, 0:2], ssq[0:tp, 0:2], AF.Sqrt)
                    O.recip(ssq[0:tp, 0:2], ssq[0:tp, 0:2])
                    O.stt(cqn[0:tp, :], ps4[0:tp, 0:384], ssq[0:tp, 0:1], qag[0:tp, :], ALU.mult, ALU.mult)
                    cf = ckvf[ti % 2]
                    kf = kpef[ti % 2]
                    O.stt(cf[0:tp, :], ps5[0:tp, 0:256], ssq[0:tp, 1:2], kvag[0:tp, :], ALU.mult, ALU.mult)
                    O.copy(ckvb[0:tp, :], cf[0:tp, :], eng="act")
                    O.tt(rt1[0:tp, 0, :], ps5[0:tp, 256:288], rC, ALU.mult)
                    O.tt(rt2[0:tp, 0, 0:16], ps5[0:tp, 272:288], rS[:, 0:16], ALU.mult)
                    O.tt(rt2[0:tp, 0, 16:32], ps5[0:tp, 256:272], rS[:, 16:32], ALU.mult)
                    O.tt(kf[0:tp, :], rt1[0:tp, 0, :], rt2[0:tp, 0, :], ALU.add)
                    O.copy(kpeb[0:tp, :], kf[0:tp, :], eng="act")
                    O.dma(G.out("ckv", l, si)[gc, :], cf[0:tp, :])
                    O.dma(G.out("kpe", l, si)[gc, :], kf[0:tp, :])
                    for k in range(3):
                        O.tr(psb(ps6)[:, k * tp:(k + 1) * tp], cqn[0:tp, k * 128:(k + 1) * 128], ident_b[0:tp, 0:tp])
                    O.copy(cqT[:, :, 0:tp], psb(ps6)[:, 0:3 * tp].rearrange("p (k n) -> p k n", k=3), eng="act")
                    for k in range(3):
                        O.mm(ps4[0:tp, 0:384], cqT[:, k, 0:tp], Wuq[:, k, :], start=(k == 0), stop=(k == 2))
                    q3 = ps4[0:tp, 0:384].rearrange("p (h d) -> p h d", h=4)
                    O.copy(qtm[0:tp, :, 0:64], q3[:, :, 0:64], eng="act")
                    O.tt(rt1[0:tp], q3[:, :, 64:96], bc(rC.unsqueeze(1), [tp, 4, 32]), ALU.mult)
                    O.tt(rt2[0:tp, :, 0:16], q3[:, :, 80:96], bc(rS[:, 0:16].unsqueeze(1), [tp, 4, 16]), ALU.mult)
                    O.tt(rt2[0:tp, :, 16:32], q3[:, :, 64:80], bc(rS[:, 16:32].unsqueeze(1), [tp, 4, 16]), ALU.mult)
                    O.tt(qtm[0:tp, :, 64:96], rt1[0:tp], rt2[0:tp], ALU.add)
                    for h in range(4):
                        O.tr(psb(ps6)[0:96, h * tp:(h + 1) * tp], qtm[0:tp, h, :], ident_b[0:tp, 0:tp])
                    O.copy(QT[0:96, :, tc], psb(ps6)[0:96, 0:4 * tp].rearrange("p (h n) -> p h n", h=4), eng="dve")
                    kv_tile(ckvb[0:tp, :], kpeb[0:tp, :], NPT + ti, tp)
                n_kt = NPT + (b + 1) * TPB

                def rows_fn(kt):
                    return 128 if kt < NPT else tp

                def q0_fn(kt, b=b):
                    return max(0, (kt - NPT - b * TPB) * 128) if G.prompt else 0

                def mask_fn(kt, Et, c0, b=b):
                    if G.prompt and kt >= b * TPB:
                        O.memset(Et[64:128, c0:c0 + 64], 0.0, eng="pool")
                attention(ar_attn(ar), n_kt,
                          lambda h, kt: KT[0:96, h, kt * 128:kt * 128 + rows_fn(kt)],
                          lambda h: QT[0:96, h, :],
                          lambda h, kt: Vc[0:rows_fn(kt), kt, h, :],
                          96 ** -0.5,
                          lambda hp, bcols=bcols: mixT[:, hp, bcols],
                          TB, q0_fn=q0_fn, mask_fn=mask_fn, rows_fn=rows_fn)

    def p2_fox(G, l):
        tp, TB, TPB, NB, NPT = G.tp, G.TB, G.TPB, G.NB, G.NPT
        NKT = NPT + G.NT
        LK = G.past + G.T
        ar = Arena(S, ARENA0, SB_LIMIT)
        Wf = ar.alloc("Wf", [128, 8, 772], BF16)
        KTf = ar.alloc("KTf", [128, 2, LK], BF16)
        Vf = ar.alloc("Vf", [128, NKT, 4, 64], BF16)
        fqT = ar.alloc("fqT", [128, 2, TB], BF16)
        xb = ar.alloc("xb", [128, 8, TB], BF16)
        fkvf = [ar.alloc(f"fkvf{i}", [128, 512], F32) for i in range(2)]
        lgf = [ar.alloc(f"lgf{i}", [128, 4], F32) for i in range(2)]
        fcum = ar.alloc("fcum", [128, NKT, 4], F32)
        carry = ar.alloc("carry", [128, 4], F32)
        biasF = ar.alloc("biasF", [128, NKT, 4], F32)
        bfb = ar.alloc("bfb", [128, 4], F32)
        ltmp = ar.alloc("ltmp", [128, 4], F32)
        if NPT:
            fkp = ar.alloc("fkp", [128, NPT, 256], BF16)
            lgp = ar.alloc("lgp", [128, NPT, 4], F32)
        wload(Wf[:], kpn(w_in[l, :, C_FQ:C_FQ + 772]))
        O.dma(bfb[:], pbc(fox_bf[l:l + 1, :]))

        def cum_tile(lg_ap, kt, r):
            O.mm(ps[7][0:r, 8:12], triu_f[0:r, 0:r], lg_ap)
            O.mm(ps[7][:, 16:20], ones_f[0:r, :], lg_ap)
            O.tt(fcum[0:r, kt, :], ps[7][0:r, 8:12], carry[0:r, :], ALU.add)
            O.tt(carry[:], ps[7][:, 16:20], carry[:], ALU.add)

        for si in range(G.nseq):
            O.memset(carry[:], 0.0)
            if NPT:
                gs = G.gseq(si)
                wload(fkp[:], ci["fk"][l, gs].rearrange("(t p) n -> p t n", p=128))
                wload(Vf[:, 0:NPT, :, :], ci["fv"][l, gs].rearrange("(t p) (h d) -> p t h d", p=128, h=4))
                O.dma(lgp[:], ci["logf"][l, gs].rearrange("(t p) h -> p t h", p=128))
                for pt in range(NPT):
                    for c in range(2):
                        O.tr(psb(6)[:, c * 128:(c + 1) * 128], fkp[:, pt, c * 128:(c + 1) * 128], ident_b[:])
                    O.copy(KTf[:, :, pt * 128:(pt + 1) * 128], psb(6)[:, 0:256].rearrange("p (c n) -> p c n", c=2),
                           eng=evac_eng())
                    cum_tile(lgp[:, pt, :], pt, 128)
            for b in range(NB):
                x0 = G.col0(si) + b * TB
                bcols = slice(x0, x0 + TB)
                kcols = slice(G.past + b * TB, G.past + (b + 1) * TB)
                O.copy(xb[:], xf[:, :, bcols], eng="dve")
                for c in range(2):
                    for k in range(8):
                        O.mm(ps[4][:, 0:TB], Wf[:, k, c * 128:(c + 1) * 128], xb[:, k, :], start=(k == 0), stop=(k == 7))
                    O.copy(fqT[:, c, :], ps[4][:, 0:TB], eng="act")
                    for k in range(8):
                        O.mm(ps[5][:, 0:TB], Wf[:, k, 256 + c * 128:256 + (c + 1) * 128], xb[:, k, :],
                             start=(k == 0), stop=(k == 7))
                    O.copy(KTf[:, c, kcols], ps[5][:, 0:TB], eng="dve")
                for t in range(TPB):
                    ti = b * TPB + t
                    tc = slice(t * tp, (t + 1) * tp)
                    gc = slice(ti * tp, (ti + 1) * tp)
                    for k in range(8):
                        O.mm(ps[6][0:tp, 0:512], xb[:, k, tc], Wf[:, k, 256:768], start=(k == 0), stop=(k == 7))
                    for k in range(8):
                        O.mm(ps[7][0:tp, 0:4], xb[:, k, tc], Wf[:, k, 768:772], start=(k == 0), stop=(k == 7))
                    ff_ = fkvf[ti % 2]
                    O.copy(ff_[0:tp, :], ps[6][0:tp, 0:512], eng="act")
                    O.copy(Vf[0:tp, NPT + ti, :, :], ps[6][0:tp, 256:512].rearrange("p (h d) -> p h d", h=4), eng="dve")
                    O.dma(G.out("fk", l, si)[gc, :], ff_[0:tp, 0:256])
                    O.dma(G.out("fv", l, si)[gc, :], ff_[0:tp, 256:512])
                    lg = lgf[ti % 2]
                    O.tt(ltmp[0:tp, :], ps[7][0:tp, 0:4], bfb[0:tp, :], ALU.add)
                    O.act(ltmp[0:tp, :], ltmp[0:tp, :], AF.Exp, scale=-1.0)
                    O.act(ltmp[0:tp, :], ltmp[0:tp, :], AF.Ln, bias=1.0)
                    O.ts(lg[0:tp, :], ltmp[0:tp, :], -1.0, None, ALU.mult)
                    O.dma(G.out("logf", l, si)[gc, :], lg[0:tp, :])
                    cum_tile(lg[0:tp, :], NPT + ti, tp)
                n_kt = NPT + (b + 1) * TPB
                O.tt(biasF[:, 0:n_kt, :], bc(carry[:].unsqueeze(1), [128, n_kt, 4]), fcum[:, 0:n_kt, :], ALU.subtract)

                def rows_fn(kt):
                    return 128 if kt < NPT else tp

                def q0_fn(kt, b=b):
                    return max(0, (kt - NPT - b * TPB) * tp)

                def mask_fn(kt, Et, c0, b=b):
                    if kt >= NPT + b * TPB:
                        O.tt(Et[0:tp, c0:c0 + tp], Et[0:tp, c0:c0 + tp], triu_b[0:tp, 0:tp], ALU.mult, eng="pool")
                attention(ar_attn(ar), n_kt,
                          lambda h, kt: KTf[(h % 2) * 64:(h % 2) * 64 + 64, h // 2, kt * 128:kt * 128 + rows_fn(kt)],
                          lambda h: fqT[(h % 2) * 64:(h % 2) * 64 + 64, h // 2, :],
                          lambda h, kt: Vf[0:rows_fn(kt), kt, h, :],
                          64 ** -0.5,
                          lambda hp, bcols=bcols: mixT[:, 6 + hp, bcols],
                          TB, bias_fn=lambda h, kt: biasF[0:rows_fn(kt), kt, h:h + 1], q0_fn=q0_fn, mask_fn=mask_fn,
                          rows_fn=rows_fn)

    def p3_gdn(G, l):
        tp = G.tp
        CH = min(64, tp)
        NCH = tp // CH
        GB = min(256, G.T)
        NGB = G.T // GB
        TPG = GB // tp
        NLEV = 6 if CH == 64 else 5
        ar = Arena(S, ARENA0, SB_LIMIT)
        Wg = ar.alloc("Wg", [128, 8, 2056], BF16)
        halo4 = ar.alloc("halo", [128, 12, 4], F32)
        halo = halo4[:, :, 0:3]
        NR = 3
        pre = [ar.alloc(f"pre{i}", [128, 3 + GB], F32) for i in range(NR)]
        acc = [ar.alloc(f"acc{i}", [128, GB], F32) for i in range(NR)]
        sq16s = [ar.alloc(f"gsq16{i}", [128, GB], BF16) for i in range(NR)]
        rsts = [ar.alloc(f"grst{i}", [128, GB], F32) for i in range(NR)]
        qnTs = [ar.alloc(f"qnT{i}", [128, 4, GB], BF16) for i in range(2)]
        knT = ar.alloc("knT", [128, 4, GB], BF16)
        vT = ar.alloc("vT", [128, 4, GB], BF16)
        ktm = ar.alloc("gktm", [128, TPG, 4, 128], BF16)
        vtm = ar.alloc("gvtm", [128, TPG, 4, 128], BF16)
        zs = ar.alloc("zs", [128, TPG, 4, 128], BF16)
        gg = ar.alloc("gg", [128, TPG, 4], F32)
        bet = ar.alloc("bet", [128, TPG, 4], F32)
        xb = ar.alloc("gxb", [128, 8, GB], BF16)
        dtb = ar.alloc("dtb", [128, 4], F32)
        nega = ar.alloc("nega", [128, 4], F32)
        ngb = ar.alloc("ngb", [128, 128], F32)
        Sf = ar.alloc("Sf", [128, 4, 128], F32)
        Sb = ar.alloc("Sb", [128, 4, 128], BF16)
        Bg = ar.alloc("Bg", [128, 4, 128], F32)
        Dm = ar.alloc("Dm", [128, 4, 128], F32)
        DTm = ar.alloc("DTm", [128, 4, 128], F32)
        tmpf = ar.alloc("tmpf", [128, 4, 128], F32)
        Xm = ar.alloc("Xm", [128, 4, 128], BF16)
        XTm = ar.alloc("XTm", [128, 4, 128], BF16)
        Pm = [ar.alloc(f"Pm{i}", [128, 4, 128], BF16) for i in range(2)]
        PTm = [ar.alloc(f"PTm{i}", [128, 4, 128], BF16) for i in range(2)]
        Ym = ar.alloc("Ym", [128, 4, 128], BF16)
        vb = ar.alloc("vb", [128, 4, 128], BF16)
        kbg = ar.alloc("kbg", [128, 4, 128], BF16)
        kdecs = [ar.alloc(f"kdec{i}", [128, 4, 128], BF16) for i in range(2)]
        usbs = [ar.alloc(f"usb{i}", [128, 4, 128], F32) for i in range(2)]
        wTss = [ar.alloc(f"wTs{i}", [128, 4, 128], BF16) for i in range(2)]
        qkdTs = [ar.alloc(f"qkdT{i}", [128, 4, 128], BF16) for i in range(2)]
        vnew = ar.alloc("vnew", [128, 4, 128], BF16)
        osb = ar.alloc("osb", [128, 4, 128], F32)
        tsb = ar.alloc("tsb", [128, 4, 128], F32)
        obt = ar.alloc("obt", [128, 4, 128], BF16)
        svs = [ar.alloc(f"sv{i}", [128, 32], F32) for i in range(2)]
        epsb = ar.alloc("epsb", [128, 16], F32)
        cst = ar.alloc("cst", [128, 512], F32)
        wload(Wg[:], kpn(w_in[l, :, C_GQ:C_GQ + 2056]))
        O.dma(dtb[:], pbc(dt_bias[l:l + 1, :]))
        O.dma(nega[:], pbc(a_log[l:l + 1, :]))
        O.dma(ngb[:], pbc(norm_g[l:l + 1, :]))
        O.act(nega[:], nega[:], AF.Exp)
        O.ts(nega[:], nega[:], -1.0, None, ALU.mult)
        O.memset(epsb[:], 1e-6)

        def v4(ap, n=128):
            return ap.rearrange("p (h n) -> p h n", h=4)

        for si in range(G.nseq):
            if G.prompt:
                O.memset(halo4[:], 0.0)
                O.memset(Sf[:], 0.0)
                O.memset(Sb[:], 0.0)
            else:
                gs = G.gseq(si)
                O.dma(Sf[:], ci["gdn"][l, gs].rearrange("h k v -> k h v"))
                O.copy(Sb[:], Sf[:], eng="act")
                for grp in range(3):
                    O.dma(cst[0:3, :], ci["conv"][l, gs][:, grp * 512:(grp + 1) * 512])
                    for c4 in range(4):
                        c = grp * 4 + c4
                        O.tr(ps[7][:, c * 3:(c + 1) * 3], cst[0:3, c4 * 128:(c4 + 1) * 128], ident_f[0:3, 0:3])
                O.copy(halo[:], ps[7][:, 0:36].rearrange("p (c j) -> p c j", c=12), eng="act")
            for gbk in range(NGB):
                x0 = G.col0(si) + gbk * GB
                gcols = slice(x0, x0 + GB)
                O.copy(xb[:], xf[:, :, gcols], eng="dve")
                qnT = qnTs[gbk % 2]
                S.tag = "g_conv"
                for c in range(12):
                    pr = pre[c % NR]
                    ac = acc[c % NR]
                    sq16 = sq16s[c % NR]
                    rst = rsts[c % NR]
                    pb = ps[c % 2]
                    for k in range(8):
                        O.mm(pb[:, 0:GB], Wg[:, k, c * 128:(c + 1) * 128], xb[:, k, :], start=(k == 0), stop=(k == 7))
                    O.copy(pr[:, 0:3], halo[:, c, :], eng="dve")
                    O.copy(pr[:, 3:3 + GB], pb[:, 0:GB], eng="act")
                    O.copy(halo[:, c, :], pr[:, GB:GB + 3], eng="dve")
                    O.act(ac[:], pr[:, 0:GB], AF.Copy, scale=cwt[:, l, c, 0:1])
                    for j in range(1, 4):
                        O.stt(ac[:], pr[:, j:j + GB], cwt[:, l, c, j:j + 1], ac[:], ALU.mult, ALU.add)
                    if c < 8:
                        dst = qnT if c < 4 else knT
                        h = c % 4
                        O.act(ac[:], ac[:], AF.Silu)
                        O.act(sq16[:], ac[:], AF.Square)
                        O.mm(ps[2 + c % 2][:, 0:GB], ones_b[:], sq16[:])
                        O.act(rst[:], ps[2 + c % 2][:, 0:GB], AF.Sqrt, bias=epsb[:, 0:1])
                        O.recip(rst[:], rst[:])
                        if c < 4:
                            O.stt(dst[:, h, :], ac[:], GDN_DK ** -0.5, rst[:], ALU.mult, ALU.mult)
                        else:
                            O.tt(dst[:, h, :], ac[:], rst[:], ALU.mult)
                    else:
                        O.act(vT[:, c - 8, :], ac[:], AF.Silu)
                for t in range(TPG):
                    ti = gbk * TPG + t
                    tc = slice(t * tp, (t + 1) * tp)
                    gc = slice(x0 + t * tp, x0 + (t + 1) * tp)
                    R = slice(0, tp)
                    idb = ident_b[0:tp, 0:tp]
                    kdec, usb, wTs, qkdT, sv = (x_[ti % 2] for x_ in (kdecs, usbs, wTss, qkdTs, svs))
                    S.tag = "g_prep"
                    for h in range(4):
                        O.tr(psb(3)[R, h * 128:(h + 1) * 128], knT[:, h, tc], ident_b[:])
                    O.copy(ktm[R, t, :, :], v4(psb(3)[R, 0:512]), eng="act")
                    for h in range(4):
                        O.tr(psb(4)[R, h * 128:(h + 1) * 128], vT[:, h, tc], ident_b[:])
                    O.copy(vtm[R, t, :, :], v4(psb(4)[R, 0:512]), eng="act")
                    for k in range(8):
                        O.mm(ps[5][R, 0:512], xb[:, k, tc], Wg[:, k, 1536:2048], start=(k == 0), stop=(k == 7))
                    for k in range(8):
                        O.mm(ps[6][R, 0:8], xb[:, k, tc], Wg[:, k, 2048:2056], start=(k == 0), stop=(k == 7))
                    O.act(zs[R, t, :, :], v4(ps[5][R, 0:512]), AF.Silu)
                    g_t = gg[R, t, :]
                    b_t = bet[R, t, :]
                    O.tt(g_t, ps[6][R, 0:4], dtb[R, :], ALU.add)
                    O.ts(g_t, g_t, 30.0, None, ALU.min)
                    O.act(g_t, g_t, AF.Exp)
                    O.act(g_t, g_t, AF.Ln, bias=1.0)
                    O.tt(g_t, g_t, nega[R, :], ALU.mult)
                    O.act(b_t, ps[6][R, 4:8], AF.Sigmoid)
                    mle = Mle[R, R]
                    mgt = Mgt[R, R]
                    O.mm(ps[6][R, 16:20], mle, g_t)
                    O.mm(ps[6][R, 24:28], mgt, g_t)
                    O.mm(ps[6][:, 32:36], Sel0[R, :], g_t)
                    if NCH == 2:
                        O.mm(ps[6][:, 36:40], Sel1[R, :], g_t)
                    O.act(sv[R, 0:4], ps[6][R, 16:20], AF.Exp)
                    O.act(sv[R, 4:8], ps[6][R, 24:28], AF.Exp)
                    O.act(sv[:, 8:8 + 4 * NCH], ps[6][:, 32:32 + 4 * NCH], AF.Exp)
                    O.tt(sv[R, 16:20], sv[R, 0:4], b_t, ALU.mult)
                    O.ts(sv[R, 20:24], b_t, -1.0, None, ALU.mult)
                    O.tt(Bg[R, :, R], bc(mgt.unsqueeze(1), [tp, 4, tp]), bc(g_t.unsqueeze(2), [tp, 4, tp]), ALU.mult)
                    for h in range(4):
                        O.mm(ps[7][R, h * 128:h * 128 + tp], mle, Bg[R, h, R])
                    O.act(Dm[R, :, R], v4(ps[7][R, 0:512])[:, :, R], AF.Exp)
                    O.tt(Dm[R, :, R], Dm[R, :, R], bc(mgt.unsqueeze(1), [tp, 4, tp]), ALU.mult, eng="pool")
                    for h in range(4):
                        O.mm(ps[7][R, h * 128:h * 128 + tp], Bg[R, h, R], mle)
                    O.act(DTm[R, :, R], v4(ps[7][R, 0:512])[:, :, R], AF.Exp)
                    O.tt(DTm[R, :, R], DTm[R, :, R], bc(mle.unsqueeze(1), [tp, 4, tp]), ALU.mult, eng="pool")
                    for h in range(4):
                        O.mm(ps[0][R, h * 128:h * 128 + tp], knT[:, h, tc], knT[:, h, tc])
                    O.tt(tmpf[R, :, R], v4(ps[0][R, 0:512])[:, :, R], Dm[R, :, R], ALU.mult)
                    O.tt(Xm[R, :, R], tmpf[R, :, R], bc(sv[R, 20:24].unsqueeze(2), [tp, 4, tp]), ALU.mult)
                    for h in range(4):
                        O.mm(ps[1][R, h * 128:h * 128 + tp], knT[:, h, tc], qnT[:, h, tc])
                    O.tt(qkdT[R, :, R], v4(ps[1][R, 0:512])[:, :, R], DTm[R, :, R], ALU.mult)
                    for h in range(4):
                        O.tr(psb(3)[R, h * 128:h * 128 + tp], Xm[R, h, R], idb)
                    O.copy(XTm[R, :, R], v4(psb(3)[R, 0:512])[:, :, R], eng="act")
                    O.tt(Ym[R, :, R], XTm[R, :, R], bc(idb.unsqueeze(1), [tp, 4, tp]), ALU.add)
                    P_, PT_ = XTm, Xm
                    for lev in range(1, NLEV):
                        Pn, PTn = Pm[lev % 2], PTm[lev % 2]
                        lastl = lev == NLEV - 1
                        if not lastl:
                            for h in range(4):
                                O.mm(ps[2][R, h * 128:h * 128 + tp], PT_[R, h, R], P_[R, h, R])
                        for h in range(4):
                            O.mm(ps[4][R, h * 128:h * 128 + tp], P_[R, h, R], PT_[R, h, R])
                        if not lastl:
                            O.copy(Pn[R, :, R], v4(ps[2][R, 0:512])[:, :, R], eng="act")
                        O.copy(PTn[R, :, R], v4(ps[4][R, 0:512])[:, :, R], eng="act")
                        for h in range(4):
                            O.mm(ps[5][R, h * 128:h * 128 + tp], PTn[R, h, R], Ym[R, h, R])
                        O.tt(Ym[R, :, R], v4(ps[5][R, 0:512])[:, :, R], Ym[R, :, R], ALU.add)
                        P_, PT_ = Pn, PTn
                    O.tt(vb[R], vtm[R, t, :, :], bc(b_t.unsqueeze(2), [tp, 4, 128]), ALU.mult, eng="pool")
                    O.tt(kbg[R], ktm[R, t, :, :], bc(sv[R, 16:20].unsqueeze(2), [tp, 4, 128]), ALU.mult, eng="pool")
                    O.tt(kdec[R], ktm[R, t, :, :], bc(sv[R, 4:8].unsqueeze(2), [tp, 4, 128]), ALU.mult, eng="pool")
                    for h in range(4):
                        O.mm(ps[0][R, h * 128:(h + 1) * 128], Ym[R, h, R], vb[R, h, :])
                    O.copy(usb[R], v4(ps[0][R, 0:512]), eng="act")
                    for h in range(4):
                        O.mm(ps[1][:, h * 128:h * 128 + tp], kbg[R, h, :], Ym[R, h, R])
                    O.copy(wTs[:, :, R], v4(ps[1][:, 0:512])[:, :, R], eng="act")
                    S.tag = "g_rec"
                    for cch in range(NCH):
                        rows = slice(cch * CH, (cch + 1) * CH)
                        ccols = slice(t * tp + cch * CH, t * tp + (cch + 1) * CH)
                        for h in range(4):
                            O.mm(ps[2][rows, h * 128:(h + 1) * 128], wTs[:, h, rows], Sb[:, h, :])
                        O.tt(vnew[rows], usb[rows], v4(ps[2][rows, 0:512]), ALU.subtract)
                        for h in range(4):
                            O.mm(ps[3][rows, h * 128:(h + 1) * 128], qnT[:, h, ccols], Sb[:, h, :])
                        for h in range(4):
                            O.mm(ps[4][rows, h * 128:(h + 1) * 128], qkdT[rows, h, rows], vnew[rows, h, :])
                        for h in range(4):
                            O.mm(ps[5][:, h * 128:(h + 1) * 128], kdec[rows, h, :], vnew[rows, h, :])
                        O.tt(tsb[rows], v4(ps[3][rows, 0:512]), bc(sv[rows, 0:4].unsqueeze(2), [CH, 4, 128]), ALU.mult)
                        O.tt(osb[rows], tsb[rows], v4(ps[4][rows, 0:512]), ALU.add)
                        O.tt(Sf[:], Sf[:], bc(sv[:, 8 + 4 * cch:12 + 4 * cch].unsqueeze(2), [128, 4, 128]), ALU.mult)
                        O.tt(Sf[:], Sf[:], v4(ps[5][:, 0:512]), ALU.add)
                        O.copy(Sb[:], Sf[:], eng="act")
                    S.tag = "g_out"
                    for h in range(4):
                        O.act(tsb[R, h, :], osb[R, h, :], AF.Square, accum=sv[R, 24 + h:25 + h])
                    O.ts(sv[R, 24:28], sv[R, 24:28], 1.0 / GDN_DV, RMS_EPS, ALU.mult, ALU.add)
                    O.act(sv[R, 24:28], sv[R, 24:28], AF.Sqrt)
                    O.recip(sv[R, 24:28], sv[R, 24:28])
                    O.tt(tsb[R], osb[R], bc(sv[R, 24:28].unsqueeze(2), [tp, 4, 128]), ALU.mult)
                    O.tt(tsb[R], tsb[R], bc(ngb[R, :].unsqueeze(1), [tp, 4, 128]), ALU.mult)
                    O.tt(obt[R], tsb[R], zs[R, t, :, :], ALU.mult)
                    for h in range(4):
                        O.tr(psb(6)[:, h * tp:(h + 1) * tp], obt[R, h, :], idb)
                    O.copy(mixT[:, 2:6, gc], psb(6)[:, 0:4 * tp].rearrange("p (h n) -> p h n", h=4), eng="act")
            O.dma(G.out("gdn", l, si).rearrange("h k v -> k h v"), Sf[:])
            for grp in range(3):
                for c4 in range(4):
                    c = grp * 4 + c4
                    O.tr(ps[7][0:3, c4 * 128:(c4 + 1) * 128], halo[:, c, :], ident_f[:])
                O.copy(cst[0:3, :], ps[7][0:3, 0:512], eng="act")
                O.dma(G.out("conv", l, si)[:, grp * 512:(grp + 1) * 512], cst[0:3, :])

    def dense_blocks(G):
        tot = G.nseq * G.T
        bs = min(512, tot)
        return [slice(i, i + bs) for i in range(0, tot, bs)]

    def p4_out(G, l):
        ar = Arena(S, ARENA0, SB_LIMIT)
        Wo = ar.alloc("Wo", [128, 8, D], BF16)
        wload(Wo[:], kpn(w_out[l]))
        for bcols in dense_blocks(G):
            n = bcols.stop - bcols.start
            for oc in range(8):
                pb = ps[oc % 4]
                for k in range(8):
                    O.mm(pb[:, 0:n], Wo[:, k, oc * 128:(oc + 1) * 128], mixT[:, k, bcols], start=(k == 0), stop=(k == 7))
                O.stt(xf[:, oc, bcols], xf[:, oc, bcols], ALPHA, pb[:, 0:n], ALU.mult, ALU.add)
            ln_block(ar_attn(ar), bcols, lnp[:, l, 0, 0, :], lnp[:, l, 0, 1, :])

    def p5_mem(G, l):
        TB, NB = G.TB, G.NB
        ar = Arena(S, ARENA0, SB_LIMIT)
        Wq = ar.alloc("Wq", [128, 8, 512], BF16)
        Wxo = ar.alloc("Wxo", [128, 4, D], BF16)
        KTm = ar.alloc("KTm", [128, 4, N_MEM], BF16)
        Vm = ar.alloc("Vm", [128, 2, 512], BF16)
        qmTs = [ar.alloc(f"qmT{i}", [128, 4, TB], BF16) for i in range(2)]
        omTs = [ar.alloc(f"omT{i}", [128, 4, TB], BF16) for i in range(2)]
        xbs = [ar.alloc(f"xb{i}", [128, 8, TB], BF16) for i in range(2)]
        if G.prompt:
            Wk = ar.alloc("Wk", [128, 8, 512], BF16)
            Wv = ar.alloc("Wv", [128, 8, 512], BF16)
            mst = [ar.alloc(f"mst{i}", [128, 512], F32) for i in range(2)]
            wload(Wk[:], kpn(w_mk[l]))
            wload(Wv[:], kpn(w_mv[l]))
        else:
            kmp = ar.alloc("kmp", [128, 2, 512], BF16)
        wload(Wq[:], kpn(w_xq[l]))
        wload(Wxo[:], kpn(w_xo[l]))
        for si in range(G.nseq):
            gs = G.gseq(si)
            if G.prompt:
                for h in range(4):
                    for k in range(8):
                        O.mm(ps[4][:, 0:N_MEM], Wk[:, k, h * 128:(h + 1) * 128], memT[:, k, :], start=(k == 0), stop=(k == 7))
                    O.copy(KTm[:, h, :], ps[4][:, 0:N_MEM], eng=evac_eng())
                for mt in range(2):
                    mc = slice(mt * 128, (mt + 1) * 128)
                    for wi, (W_, o_) in enumerate(((Wk, o_mk), (Wv, o_mv))):
                        pb = ps[5 + wi]
                        for k in range(8):
                            O.mm(pb[:, 0:512], memT[:, k, mc], W_[:, k, :], start=(k == 0), stop=(k == 7))
                        st = mst[wi]
                        O.copy(st[:], pb[:, 0:512], eng="act")
                        if wi == 1:
                            O.copy(Vm[:, mt, :], pb[:, 0:512], eng="dve")
                        O.dma(o_[l, gs, mc, :], st[:])
            else:
                wload(kmp[:], ci["mk"][l, gs].rearrange("(t p) n -> p t n", p=128))
                wload(Vm[:], ci["mv"][l, gs].rearrange("(t p) n -> p t n", p=128))
                for mt in range(2):
                    for h in range(4):
                        O.tr(psb(4)[:, h * 128:(h + 1) * 128], kmp[:, mt, h * 128:(h + 1) * 128], ident_b[:])
                    O.copy(KTm[:, :, mt * 128:(mt + 1) * 128], psb(4)[:, 0:512].rearrange("p (h n) -> p h n", h=4),
                           eng=evac_eng())
            for b in range(NB):
                x0 = G.col0(si) + b * TB
                bcols = slice(x0, x0 + TB)
                xb, qmT, omT = xbs[b % 2], qmTs[b % 2], omTs[b % 2]
                O.copy(xb[:], xf[:, :, bcols], eng="dve")
                for h in range(4):
                    for k in range(8):
                        O.mm(ps[4 + h % 2][:, 0:TB], Wq[:, k, h * 128:(h + 1) * 128], xb[:, k, :], start=(k == 0), stop=(k == 7))
                    O.copy(qmT[:, h, :], ps[4 + h % 2][:, 0:TB], eng=evac_eng())
                attention(ar_attn(ar), 2,
                          lambda h, kt: KTm[:, h, kt * 128:(kt + 1) * 128],
                          lambda h, qmT=qmT: qmT[:, h, :],
                          lambda h, kt: Vm[:, kt, h * 128:(h + 1) * 128],
                          128 ** -0.5,
                          lambda hp, omT=omT: omT[:, hp, :],
                          TB, dv=128)
                for oc in range(8):
                    pb = ps[4 + oc % 2]
                    for k in range(4):
                        O.mm(pb[:, 0:TB], Wxo[:, k, oc * 128:(oc + 1) * 128], omT[:, k, :], start=(k == 0), stop=(k == 3))
                    O.stt(xf[:, oc, bcols], xf[:, oc, bcols], ALPHA, pb[:, 0:TB], ALU.mult, ALU.add)
                ln_block(ar_attn(ar), bcols, lnp[:, l, 1, 0, :], lnp[:, l, 1, 1, :])

    def p6_ffn(G, l):
        ar = Arena(S, ARENA0, SB_LIMIT)
        blocks = dense_blocks(G)
        n = blocks[0].stop - blocks[0].start
        xbf = mixT
        W1 = [ar.alloc(f"W1_{i}", [128, 8, 1024], BF16) for i in range(2)]
        W2 = [ar.alloc(f"W2_{i}", [128, 8, 1024], BF16) for i in range(2)]
        hb = [ar.alloc(f"hb{i}", [128, 8, n], BF16) for i in range(2)]
        relu = [ar.alloc(f"relu{i}", [128, n], F32) for i in range(4)]
        for bcols in blocks:
            O.copy(xbf[:, :, bcols], xf[:, :, bcols], eng="act")
            O.ts(xf[:, :, bcols], xf[:, :, bcols], ALPHA, None, ALU.mult)
        for q in range(4):
            w1, w2 = W1[q % 2], W2[q % 2]
            wload(w1[:], kpn(w_ff1[l, :, q * 1024:(q + 1) * 1024]))
            wload(w2[:], kpn(w_ff2[l, q * 1024:(q + 1) * 1024, :]))
            for bi, bcols in enumerate(blocks):
                hh = hb[bi % 2]
                for hc in range(8):
                    pb = ps[hc % 4]
                    for k in range(8):
                        O.mm(pb[:, 0:n], w1[:, k, hc * 128:(hc + 1) * 128], xbf[:, k, bcols], start=(k == 0), stop=(k == 7))
                    rl = relu[hc % 4]
                    O.act(rl[:], pb[:, 0:n], AF.Relu)
                    if hc % 2 == 0:
                        O.tt(hh[:, hc, :], rl[:], rl[:], ALU.mult)
                    else:
                        O.act(hh[:, hc, :], rl[:], AF.Square)
                for oc in range(8):
                    pb = ps[4 + oc % 4]
                    for k in range(8):
                        O.mm(pb[:, 0:n], w2[:, k, oc * 128:(oc + 1) * 128], hh[:, k, :], start=(k == 0), stop=(k == 7))
                    O.tt(xf[:, oc, bcols], xf[:, oc, bcols], pb[:, 0:n], ALU.add)
        for bcols in blocks:
            ln_block(Arena(S, ARENA0, SB_LIMIT), bcols, lnp[:, l, 2, 0, :], lnp[:, l, 2, 1, :])

    groups = []
    for s in range(NP):
        G = Grp()
        G.prompt, G.nseq, G.T, G.tp, G.past = True, 1, T, 128, 0
        G.gseq = lambda si, s=s: s
        G.x_in = lambda si, s=s: x_p[s]
        G.y_out = lambda si, s=s: y_p[s]
        G.out = lambda name, l, si, s=s: po[name][l, s]
        groups.append(G)
    if NS:
        G = Grp()
        G.prompt, G.nseq, G.T, G.tp, G.past = False, NS, TS, TS, PAST
        G.gseq = lambda si: si
        G.x_in = lambda si: x_s[si]
        G.y_out = lambda si: y_s[si]
        G.out = lambda name, l, si: so[name][l, si]
        G.rope = lambda ti: (ropeCs[0:TS, 0, :], ropeSs[0:TS, 0, :])
        groups.append(G)
    for G in groups:
        G.NT = G.T // G.tp
        G.TB = min(512, G.T)
        G.NB = G.T // G.TB
        G.TPB = G.TB // G.tp
        G.NPT = G.past // 128
        G.col0 = lambda si, G=G: si * G.T
        S.tag = "input"
        input_stage(G)
        for l in range(L if cfg.upto >= 1 else 0):
            S.tag = "p1_mla"
            p1_mla(G, l)
            if cfg.upto < 2:
                continue
            S.tag = "p2_fox"
            p2_fox(G, l)
            if cfg.upto < 3:
                continue
            S.tag = "p3_gdn"
            if cfg.gdn:
                p3_gdn(G, l)
            else:
                O.memset(mixT[:, 2:6, 0:G.nseq * G.T], 0.0, eng="pool")
            if cfg.upto < 4:
                continue
            S.tag = "p4_out"
            p4_out(G, l)
            if cfg.upto < 5:
                continue
            S.tag = "p5_mem"
            p5_mem(G, l)
            if cfg.upto < 6:
                continue
            S.tag = "p6_ffn"
            p6_ffn(G, l)
        S.tag = "final"
        final_stage(G)

    if cfg.resched:
        S.reschedule(window=cfg.window)
    S.emit()
    return nc, S


def ar_attn(ar):
    return Arena(ar.S, ar.off, ar.limit)


def _fm(v):
    return np.ascontiguousarray(np.asarray(v, np.float32).reshape(8, 128).T)


def prep_core_inputs(cfg, inp, core, consts):
    NP, NS, L = cfg.n_prompt, cfg.n_sample, cfg.depth
    m = {}
    if NP:
        sl = slice(core * NP, (core + 1) * NP)
        m["x_prompt"] = np.ascontiguousarray(inp["x_prompt"][sl, :cfg.T])
        m["mem_prompt"] = np.ascontiguousarray(inp["mem_prompt"][sl])
    if NS:
        ss = slice(core * NS, (core + 1) * NS)
        m["x_sample"] = np.ascontiguousarray(inp["x_sample"][ss])
        for k in ("cache_mla_ckv", "cache_mla_kpe", "cache_fox_logf", "state_gdn", "state_gdn_conv"):
            m[k] = np.ascontiguousarray(np.asarray(inp[k], np.float32)[:L, ss])
        for k in ("cache_fox_k", "cache_fox_v"):
            a = np.asarray(inp[k], np.float32)[:L, ss]
            m[k] = np.ascontiguousarray(a.reshape(a.shape[0], a.shape[1], a.shape[2], 256))
        for k in ("cache_mem_k", "cache_mem_v"):
            a = np.asarray(inp[k], np.float32)[:L, ss]
            m[k] = np.ascontiguousarray(a.reshape(a.shape[0], a.shape[1], N_MEM, 512))
        m["ropeCs"] = consts["ropeCs"]
        m["ropeSs"] = consts["ropeSs"]
    for k in ("w_in", "w_uq", "w_ukv", "w_out", "w_xq", "w_mk", "w_mv", "w_xo", "w_ff1", "w_ff2",
              "qa_g", "kva_g", "fox_bf"):
        m[k] = np.ascontiguousarray(np.asarray(inp[k], np.float32)[:L])
    m["ln_in_fm"] = np.ascontiguousarray(np.stack([_fm(inp["ln_in_g"]), _fm(inp["ln_in_b"])], 1))
    lg = np.asarray(inp["ln_g"], np.float32)[:L]
    lb = np.asarray(inp["ln_b"], np.float32)[:L]
    arr = np.zeros((128, L, 3, 2, 8), np.float32)
    for l in range(L):
        for w in range(3):
            arr[:, l, w, 0, :] = _fm(lg[l, w])
            arr[:, l, w, 1, :] = _fm(lb[l, w])
    m["ln_fm"] = arr
    cw = np.asarray(inp["gdn_conv_w"], np.float32)[:L]
    m["conv_fm"] = np.ascontiguousarray(cw.reshape(L, 4, 12, 128).transpose(3, 0, 2, 1))
    for k in ("gdn_a_log", "gdn_dt_bias", "gdn_norm_g"):
        m[k] = np.ascontiguousarray(np.asarray(inp[k], np.float32)[:L])
    for k in ("ident_f", "triu_f", "ones_f", "ropeC", "ropeS", "Mle_g", "Mgt_g", "Sel0_g", "Sel1_g"):
        m[k] = consts[k]
    return m


N_CORES = 8
_OUT_ORDER = ["y_prompt", "y_sample", "p_mla_ckv", "p_mla_kpe", "p_fox_k", "p_fox_v", "p_fox_logf", "p_gdn",
              "p_gdn_conv", "p_mem_k", "p_mem_v", "s_mla_ckv", "s_mla_kpe", "s_fox_k", "s_fox_v", "s_fox_logf",
              "s_gdn", "s_gdn_conv"]


def kernel(**inputs):
    inp = {k: np.asarray(v) for k, v in inputs.items()}
    B, T = inp["x_prompt"].shape[:2]
    BS, TS = inp["x_sample"].shape[:2]
    past = inp["cache_mla_ckv"].shape[2]
    cfg = Cfg(n_prompt=B // N_CORES, T=T, n_sample=BS // N_CORES, TS=TS, past=past, depth=inp["w_in"].shape[0])
    nc, _ = build(cfg)
    consts = host_consts(cfg)
    in_maps = [prep_core_inputs(cfg, inp, c, consts) for c in range(N_CORES)]
    res = run_bass_kernel_spmd(nc, in_maps, core_ids=list(range(N_CORES)))
    R = res.results
    outs = []
    for name in _OUT_ORDER:
        if name in ("y_prompt", "y_sample"):
            a = np.concatenate([r[name] for r in R], axis=0)
        else:
            a = np.concatenate([r[name] for r in R], axis=1)
        if name in ("p_fox_k", "p_fox_v", "s_fox_k", "s_fox_v"):
            a = a.reshape(a.shape[0], a.shape[1], a.shape[2], FOX_H, FOX_HD)
        elif name in ("p_mem_k", "p_mem_v"):
            a = a.reshape(a.shape[0], a.shape[1], N_MEM, MEM_H, MEM_HD)
        outs.append(np.ascontiguousarray(a, dtype=np.float32))
    return tuple(outs)
```

```python
import math
import numpy as np
import concourse.bass as bass
import concourse.mybir as mybir
from concourse.bass_utils import run_bass_kernel_spmd

F32 = mybir.dt.float32
BF16 = mybir.dt.bfloat16
AF = mybir.ActivationFunctionType
ALU = mybir.AluOpType
AX = mybir.AxisListType

D = 1024
DEPTH = 4
MLA_H, MLA_NOPE, MLA_ROPE, MLA_V, MLA_QL, MLA_KVL = 4, 64, 32, 64, 384, 256
GDN_H, GDN_DK, GDN_DV, CONV_W = 4, 128, 128, 4
FOX_H, FOX_HD = 4, 64
N_MEM, MEM_H, MEM_HD = 256, 4, 128
D_FF = 4096
ALPHA = (2 * DEPTH) ** 0.25
LN_EPS = 1e-5
RMS_EPS = 1e-6
IN_DIM = 3500
C_CQ, C_KVA, C_GQ, C_GK, C_GV, C_GZ, C_GA, C_GB, C_FQ, C_FK, C_FV, C_FF = (
    0, 384, 672, 1184, 1696, 2208, 2720, 2724, 2728, 2984, 3240, 3496)

ENGS = ["pe", "act", "dve", "pool", "sp"]
CELL = 16
N_SB = 229376 // CELL
N_PS = 8
NCELL = N_SB + N_PS
DMA_RING = 20
DMA_RING_Q = {"pool": 6}
SB_BASE = 16512
SB_LIMIT = 229248


def _dsize(dt):
    s = str(dt)
    if "32" in s:
        return 4
    if "16" in s:
        return 2
    if "8" in s:
        return 1
    raise ValueError(s)


class Sched:
    def __init__(self, nc):
        self.nc = nc
        self.ops = []
        self.lw = np.full(NCELL, -1, np.int64)
        self.lr = {e: np.full(NCELL, -1, np.int64) for e in ENGS}
        self.lr_dma = np.full(NCELL, -1, np.int64)
        self.n_alloc = 0
        self.tag = ""
        self.psum = [nc.alloc_psum_tensor(f"psb{i}", [128, 512], F32) for i in range(N_PS)]
        self.psum_names = {p.name: i for i, p in enumerate(self.psum)}
        self.big = nc.alloc_sbuf_tensor_at("arena", [128, (SB_LIMIT - SB_BASE) // 2], BF16, offset=SB_BASE)

    def alloc(self, name, shape, dtype, off):
        n = int(np.prod(shape[1:]))
        nbytes = n * _dsize(dtype)
        assert off % 64 == 0 and off >= SB_BASE and off + nbytes <= SB_LIMIT, (name, off, nbytes)
        e0 = (off - SB_BASE) // 2
        v = self.big[0:shape[0], e0:e0 + nbytes // 2]
        if _dsize(dtype) == 4:
            v = v.bitcast(dtype)
        elif str(dtype) != str(BF16):
            raise ValueError(dtype)
        if len(shape) == 3:
            v = v.rearrange("p (a b) -> p a b", a=shape[1])
        elif len(shape) == 4:
            v = v.rearrange("p (a b c) -> p a b c", a=shape[1], b=shape[2])
        elif len(shape) == 5:
            v = v.rearrange("p (a b c d) -> p a b c d", a=shape[1], b=shape[2], c=shape[3])
        return v

    def _cells(self, ap):
        sp = str(ap.space)
        if sp == "PSUM":
            b = self.psum_names[ap.tensor.name]
            return N_SB + b, N_SB + b + 1
        pat = ap.ap
        pstride = pat[0][0]
        off = int(ap.offset)
        e0 = off % pstride if pstride > 0 else off
        span = 1
        for st, cnt in pat[1:]:
            span += abs(st) * (cnt - 1)
        esz = _dsize(ap.dtype)
        b0 = ap.tensor.manual_sbuf_range[0] + e0 * esz
        b1 = b0 + span * esz
        return b0 // CELL, (b1 + CELL - 1) // CELL

    def op(self, eng, fn, reads=(), writes=(), dma=False, est=None, lat=None):
        idx = len(self.ops)
        deps = set()
        oo = set()
        rc = [self._cells(a) for a in reads if a is not None and str(a.space) != "DRAM"]
        wc = [self._cells(a) for a in writes if a is not None and str(a.space) != "DRAM"]
        lw = self.lw

        def grab(arr, a, b):
            if b - a == 1:
                v = arr[a]
                if v >= 0:
                    deps.add(int(v))
                return
            sl = arr[a:b]
            mx = sl.max()
            if mx < 0:
                return
            if sl.min() == mx:
                deps.add(int(mx))
                return
            for v in np.unique(sl):
                if v >= 0:
                    deps.add(int(v))
        for a, b in rc:
            grab(lw, a, b)
            if a >= N_SB:
                for e in ENGS:
                    if e != eng:
                        grab(self.lr[e], a, b)
        for a, b in wc:
            grab(lw, a, b)
            for e in ENGS:
                grab(self.lr[e], a, b)
            grab(self.lr_dma, a, b)
        for a, b in rc:
            if dma:
                grab(self.lr_dma, a, b)
                self.lr_dma[a:b] = idx
            else:
                sl = self.lr[eng][a:b]
                if sl.max() >= 0:
                    for v in np.unique(sl):
                        if v >= 0:
                            oo.add(int(v))
                self.lr[eng][a:b] = idx
        for a, b in wc:
            lw[a:b] = idx
            for e in ENGS:
                self.lr[e][a:b] = -1
            self.lr_dma[a:b] = -1
        deps.discard(idx)
        oo -= deps
        oo.discard(idx)
        self.ops.append(dict(eng=eng, fn=fn, deps=deps, oo=oo, dma=dma, tag=self.tag, est=est if est is not None else 0.3,
                             lat=lat if lat is not None else 0.0))
        return idx

    def reschedule(self, window=6000, xlat=0.2):
        import heapq
        ops = self.ops
        n = len(ops)
        succ = [[] for _ in range(n)]
        indeg = [0] * n
        for i, o in enumerate(ops):
            for d in o["deps"]:
                succ[d].append(i)
            for d in o["oo"]:
                succ[d].append(i)
            indeg[i] = len(o["deps"]) + len(o["oo"])
        eng_id = {e: k for k, e in enumerate(ENGS)}
        eng = [eng_id[o["eng"]] for o in ops]
        bl = [0.0] * n
        for i in range(n - 1, -1, -1):
            m_ = 0.0
            for j in succ[i]:
                if bl[j] > m_:
                    m_ = bl[j]
            bl[i] = m_ + ops[i]["est"] + ops[i]["lat"] + 0.1
        self.use_bl = getattr(self, "use_bl", True)
        pri = (lambda i: (-bl[i], i)) if self.use_bl else (lambda i: (0.0, i))
        finish = [0.0] * n
        readyt = [0.0] * n
        pending = [[] for _ in ENGS]
        ready = [[] for _ in ENGS]
        future = []
        free_at = [0.0] * len(ENGS)
        placed = [False] * n
        critp = [None] * n
        last_on = [None] * len(ENGS)
        lo = 0
        order = []

        def release(i):
            if i >= lo + window:
                heapq.heappush(future, i)
            else:
                heapq.heappush(pending[eng[i]], (readyt[i], i))
        for i in range(n):
            if indeg[i] == 0:
                release(i)
        while len(order) < n:
            best = None
            for e in range(len(ENGS)):
                pe_, re_ = pending[e], ready[e]
                fa = free_at[e]
                while pe_ and pe_[0][0] <= fa:
                    k_ = heapq.heappop(pe_)[1]
                    heapq.heappush(re_, (pri(k_), k_))
                if re_:
                    cand = (fa, re_[0][1], e, True)
                elif pe_:
                    cand = (pe_[0][0], pe_[0][1], e, False)
                else:
                    continue
                if best is None or cand[:2] < best[:2]:
                    best = cand
            if best is None:
                i = heapq.heappop(future)
                heapq.heappush(pending[eng[i]], (readyt[i], i))
                continue
            st, i, e, from_ready = best
            if from_ready:
                heapq.heappop(ready[e])
            else:
                heapq.heappop(pending[e])
            o = ops[i]
            free_at[e] = st + o["est"]
            finish[i] = st + o["est"] + o["lat"]
            placed[i] = True
            o["t0"], o["t1"] = st, finish[i]
            order.append(i)
            o["why"] = ("dep", critp[i]) if (critp[i] is not None and readyt[i] >= st - 1e-9) else ("eng", last_on[e])
            last_on[e] = i
            for j in succ[i]:
                t = finish[i] + (xlat if eng[j] != e or o["dma"] else 0.05)
                if t > readyt[j]:
                    readyt[j] = t
                    critp[j] = i
                indeg[j] -= 1
                if indeg[j] == 0:
                    release(j)
            while lo < n and placed[lo]:
                lo += 1
            while future and future[0] < lo + window:
                k = heapq.heappop(future)
                heapq.heappush(pending[eng[k]], (readyt[k], k))
        newpos = [0] * n
        for p_, i in enumerate(order):
            newpos[i] = p_
        for o in ops:
            k_, w_ = o.get("why", ("eng", None))
            o["why"] = (k_, newpos[w_] if w_ is not None else None)
        new_ops = []
        for i in order:
            o = ops[i]
            o["deps"] = {newpos[d] for d in o["deps"]}
            o["oo"] = {newpos[d] for d in o["oo"]}
            new_ops.append(o)
        self.ops = new_ops
        self.sim_time = max(finish) if n else 0.0

    def emit(self):
        nc = self.nc
        ops = self.ops
        eng_of = [o["eng"] for o in ops]
        is_dma = [o["dma"] for o in ops]
        signal = [False] * len(ops)
        for o in ops:
            nd = {}
            dm = []
            for d in o["deps"]:
                if is_dma[d]:
                    dm.append(d)
                else:
                    e = eng_of[d]
                    if e == "pe" and o["eng"] == "pe" and not o["dma"]:
                        continue
                    if nd.get(e, -1) < d:
                        nd[e] = d
            o["cdeps"] = nd
            o["ddeps"] = dm
            for d in nd.values():
                signal[d] = True
        cnt = {}
        run = {e: 0 for e in ENGS}
        dma_n = {e: 0 for e in ENGS}
        dma_slot = {}
        for i, o in enumerate(ops):
            if o["dma"]:
                n = dma_n[o["eng"]]
                rq = DMA_RING_Q.get(o["eng"], DMA_RING)
                dma_slot[i] = (o["eng"], n % rq, n // rq + 1)
                dma_n[o["eng"]] = n + 1
            elif signal[i]:
                run[o["eng"]] += 1
                cnt[i] = run[o["eng"]]
        dma_engs = [e for e in ENGS if dma_n[e] > 0]
        sems = {e: nc.alloc_semaphore(f"s_{e}") for e in ENGS}
        dsem = {e: [nc.alloc_semaphore(f"d_{e}{k}") for k in range(DMA_RING)] for e in dma_engs}
        by_eng = {e: [i for i, o in enumerate(ops) if o["eng"] == e] for e in ENGS}
        self.stats = {e: len(by_eng[e]) for e in ENGS}

        def run_engine(e, eng):
            waited = {}
            nwait = 0
            for i in by_eng[e]:
                o = ops[i]
                for pe_, d in o["cdeps"].items():
                    c = cnt[d]
                    if waited.get(pe_, 0) < c:
                        eng.wait_ge(sems[pe_], c)
                        waited[pe_] = c
                        nwait += 1
                for d in o["ddeps"]:
                    qe, slot, use = dma_slot[d]
                    key = (qe, slot)
                    if waited.get(key, 0) < use:
                        eng.wait_ge(dsem[qe][slot], 16 * use)
                        waited[key] = use
                        nwait += 1
                if o["dma"]:
                    qe, slot, use = dma_slot[i]
                    key = (qe, slot)
                    if use > 1 and waited.get(key, 0) < use - 1:
                        eng.wait_ge(dsem[qe][slot], 16 * (use - 1))
                        waited[key] = use - 1
                        nwait += 1
                    o["fn"](eng).then_inc(dsem[qe][slot], 16)
                else:
                    ins = o["fn"](eng)
                    if signal[i]:
                        ins.then_inc(sems[e], 1)
            if e in dma_engs:
                n = dma_n[e]
                rq = DMA_RING_Q.get(e, DMA_RING)
                for slot in range(min(n, rq)):
                    use = (n - 1 - slot) // rq + 1
                    eng.wait_ge(dsem[e][slot], 16 * use)
            self.stats[e + "_waits"] = nwait

        with nc.Block() as block:
            @block.tensor
            def _(eng):
                run_engine("pe", eng)

            @block.scalar
            def _(eng):
                run_engine("act", eng)

            @block.vector
            def _(eng):
                run_engine("dve", eng)

            @block.gpsimd
            def _(eng):
                run_engine("pool", eng)

            @block.sync
            def _(eng):
                run_engine("sp", eng)


class Arena:
    def __init__(self, S, base, limit):
        self.S, self.base, self.off, self.limit = S, base, base, limit

    def alloc(self, name, shape, dtype):
        nbytes = int(np.prod(shape[1:])) * _dsize(dtype)
        off = self.off
        self.off = (off + nbytes + 63) // 64 * 64
        assert self.off <= self.limit, (name, self.off, self.limit)
        return self.S.alloc(name, shape, dtype, off)


class Ops:
    def __init__(self, S):
        self.S = S

    def mm(self, out, lhsT, rhs, start=True, stop=True):
        cols = _free(rhs)
        est = 0.05 + max(cols, 64) / 2400.0 * (4.0 if _dsize(rhs.dtype) == 4 else 1.0)
        self.S.op("pe", lambda e: e.matmul(out, lhsT, rhs, start=start, stop=stop, skip_group_check=True),
                  reads=[lhsT, rhs], writes=[out], est=est, lat=0.1)

    def tr(self, out, in_, ident):
        self.S.op("pe", lambda e: e.transpose(out, in_, ident), reads=[in_, ident], writes=[out],
                  est=0.12 * (4.0 if _dsize(in_.dtype) == 4 else 1.0), lat=0.1)

    def act(self, out, in_, func, bias=None, scale=1.0, accum=None, eng="act"):
        kw = {}
        rd = [in_]
        if bias is not None:
            kw["bias"] = bias
            if not isinstance(bias, (int, float)):
                rd.append(bias)
        if not isinstance(scale, (int, float)):
            rd.append(scale)
        kw["scale"] = scale
        wr = [out]
        if accum is not None:
            kw["accum_out"] = accum
            wr.append(accum)
        self.S.op("act", lambda e: e.activation(out, in_, func, **kw), reads=rd, writes=wr,
                  est=0.25 + _free(out) / 1200.0 + (0.1 if accum is not None else 0.0))

    def tt(self, out, in0, in1, op, eng="dve"):
        self.S.op(eng, lambda e: e.tensor_tensor(out, in0, in1, op), reads=[in0, in1], writes=[out], est=_vest(eng, out))

    def ts(self, out, in0, s1, s2, op0, op1=None, eng="dve", accum=None):
        rd = [in0] + [s for s in (s1, s2) if s is not None and not isinstance(s, (int, float))]
        wr = [out] + ([accum] if accum is not None else [])
        if op1 is None:
            self.S.op(eng, lambda e: e.tensor_scalar(out, in0, s1, None, op0), reads=rd, writes=wr, est=_vest(eng, out))
        elif accum is None:
            self.S.op(eng, lambda e: e.tensor_scalar(out, in0, s1, s2, op0, op1), reads=rd, writes=wr, est=_vest(eng, out))
        else:
            self.S.op(eng, lambda e: e.tensor_scalar(out, in0, s1, s2, op0, op1, accum_out=accum),
                      reads=rd, writes=wr)

    def stt(self, out, in0, scalar, in1, op0, op1, eng="dve"):
        rd = [in0, in1] + ([scalar] if not isinstance(scalar, (int, float)) else [])
        self.S.op(eng, lambda e: e.scalar_tensor_tensor(out, in0, scalar, in1, op0, op1), reads=rd, writes=[out],
                  est=_vest(eng, out))

    def copy(self, out, in_, eng="dve"):
        if eng == "act":
            self.S.op("act", lambda e: e.copy(out, in_), reads=[in_], writes=[out], est=0.25 + _free(out) / 1200.0)
        else:
            self.S.op(eng, lambda e: e.tensor_copy(out, in_), reads=[in_], writes=[out], est=_vest(eng, out))

    def memset(self, out, val, eng="dve"):
        self.S.op(eng, lambda e: e.memset(out, val), writes=[out], est=_vest(eng, out))

    def recip(self, out, in_):
        self.S.op("dve", lambda e: e.reciprocal(out, in_), reads=[in_], writes=[out], est=_vest("dve", out))

    def reduce(self, out, in_, op, axis=AX.X):
        self.S.op("dve", lambda e: e.tensor_reduce(out, in_, axis, op), reads=[in_], writes=[out], est=_vest("dve", in_))

    def dma(self, out, in_, q="sp"):
        nbytes = int(np.prod(out.shape)) * 4
        self.S.op(q, lambda e: e.dma_start(out=out, in_=in_), reads=[in_], writes=[out], dma=True,
                  est=0.1 + nbytes / 300e3 * 0.5, lat=2.0 + nbytes / 300e3 * 0.5)


def _free(ap):
    n = 1
    for d in ap.shape[1:]:
        n *= int(d)
    return n


def _vest(eng, ap):
    n = _free(ap)
    return (0.16 + n / 960.0) if eng == "dve" else (0.35 + n / 400.0)


def bc(ap, shape):
    return ap.broadcast_to(list(shape))


class Cfg:
    def __init__(self, n_prompt=4, T=2048, n_sample=2, TS=32, past=1024, depth=DEPTH, gdn=True, upto=9):
        self.upto = upto
        self.resched = True
        self.window = 6000
        self.n_prompt, self.T, self.n_sample, self.TS, self.past, self.depth, self.gdn = (
            n_prompt, T, n_sample, TS, past, depth, gdn)


def host_consts(cfg):
    c = {}
    c["ident_f"] = np.eye(128, dtype=np.float32)
    m = np.arange(128)
    c["triu_f"] = (m[:, None] <= m[None, :]).astype(np.float32)
    c["ones_f"] = np.ones((128, 128), np.float32)
    for nm, ch in (("g", 64), ("gs", 32)):
        same = (m[:, None] // ch) == (m[None, :] // ch)
        c["Mle_" + nm] = ((m[:, None] <= m[None, :]) & same).astype(np.float32)
        c["Mgt_" + nm] = ((m[:, None] > m[None, :]) & same).astype(np.float32)
        c["Sel0_" + nm] = np.repeat(((m // ch) == 0).astype(np.float32)[:, None], 128, 1)
        c["Sel1_" + nm] = np.repeat(((m // ch) == 1).astype(np.float32)[:, None], 128, 1)
    half = MLA_ROPE // 2
    inv = (10000.0 ** (-np.arange(half, dtype=np.float32) * (2.0 / MLA_ROPE))).astype(np.float32)

    def rope_tab(pos):
        ang = pos.astype(np.float32)[:, None] * inv[None, :]
        cs, sn = np.cos(ang).astype(np.float32), np.sin(ang).astype(np.float32)
        C = np.concatenate([cs, cs], 1)
        Sg = np.concatenate([-sn, sn], 1)
        return C, Sg
    C, Sg = rope_tab(np.arange(max(cfg.T, 128)))
    nt = C.shape[0] // 128
    c["ropeC"] = np.ascontiguousarray(C.reshape(nt, 128, 32).transpose(1, 0, 2))
    c["ropeS"] = np.ascontiguousarray(Sg.reshape(nt, 128, 32).transpose(1, 0, 2))
    C, Sg = rope_tab(cfg.past + np.arange(cfg.TS))
    c["ropeCs"] = C
    c["ropeSs"] = Sg
    return c


class Grp:
    pass


def build(cfg):
    nc = bass.Bass("TRN2", target_bir_lowering=False)
    S = Sched(nc)
    O = Ops(S)
    L = cfg.depth
    T = cfg.T
    NP = cfg.n_prompt
    NS = cfg.n_sample
    TS = cfg.TS
    PAST = cfg.past
    NTP = max(T // 128, 1)

    def din(name, shape):
        return nc.dram_tensor(name, list(shape), F32, kind="ExternalInput").ap()

    def dout(name, shape):
        return nc.dram_tensor(name, list(shape), F32, kind="ExternalOutput").ap()

    w_in = din("w_in", [L, D, IN_DIM])
    w_uq = din("w_uq", [L, MLA_QL, 384])
    w_ukv = din("w_ukv", [L, MLA_KVL, 512])
    w_out = din("w_out", [L, D, D])
    w_xq = din("w_xq", [L, D, 512])
    w_mk = din("w_mk", [L, D, 512])
    w_mv = din("w_mv", [L, D, 512])
    w_xo = din("w_xo", [L, 512, D])
    w_ff1 = din("w_ff1", [L, D, D_FF])
    w_ff2 = din("w_ff2", [L, D_FF, D])
    qa_g = din("qa_g", [L, 384])
    kva_g = din("kva_g", [L, 256])
    fox_bf = din("fox_bf", [L, 4])
    ln_in_fm = din("ln_in_fm", [128, 2, 8])
    ln_fm = din("ln_fm", [128, L, 3, 2, 8])
    conv_fm = din("conv_fm", [128, L, 12, 4])
    a_log = din("gdn_a_log", [L, 4])
    dt_bias = din("gdn_dt_bias", [L, 4])
    norm_g = din("gdn_norm_g", [L, 128])
    c_Mle = din("Mle_g", [128, 128])
    c_Mgt = din("Mgt_g", [128, 128])
    c_Sel0 = din("Sel0_g", [128, 128])
    c_Sel1 = din("Sel1_g", [128, 128])
    c_ident_f = din("ident_f", [128, 128])
    c_triu_f = din("triu_f", [128, 128])
    c_ones_f = din("ones_f", [128, 128])
    c_ropeC = din("ropeC", [128, NTP, 32])
    c_ropeS = din("ropeS", [128, NTP, 32])

    if NP:
        x_p = din("x_prompt", [NP, T, D])
        mem_p = din("mem_prompt", [NP, N_MEM, D])
        y_p = dout("y_prompt", [NP, T, D])
        po = dict(ckv=dout("p_mla_ckv", [L, NP, T, 256]), kpe=dout("p_mla_kpe", [L, NP, T, 32]),
                  fk=dout("p_fox_k", [L, NP, T, 256]), fv=dout("p_fox_v", [L, NP, T, 256]),
                  logf=dout("p_fox_logf", [L, NP, T, 4]), gdn=dout("p_gdn", [L, NP, 4, 128, 128]),
                  conv=dout("p_gdn_conv", [L, NP, 3, 1536]))
        o_mk = dout("p_mem_k", [L, NP, N_MEM, 512])
        o_mv = dout("p_mem_v", [L, NP, N_MEM, 512])
    if NS:
        x_s = din("x_sample", [NS, TS, D])
        y_s = dout("y_sample", [NS, TS, D])
        ci = dict(ckv=din("cache_mla_ckv", [L, NS, PAST, 256]), kpe=din("cache_mla_kpe", [L, NS, PAST, 32]),
                  fk=din("cache_fox_k", [L, NS, PAST, 256]), fv=din("cache_fox_v", [L, NS, PAST, 256]),
                  logf=din("cache_fox_logf", [L, NS, PAST, 4]), gdn=din("state_gdn", [L, NS, 4, 128, 128]),
                  conv=din("state_gdn_conv", [L, NS, 3, 1536]),
                  mk=din("cache_mem_k", [L, NS, N_MEM, 512]), mv=din("cache_mem_v", [L, NS, N_MEM, 512]))
        so = dict(ckv=dout("s_mla_ckv", [L, NS, TS, 256]), kpe=dout("s_mla_kpe", [L, NS, TS, 32]),
                  fk=dout("s_fox_k", [L, NS, TS, 256]), fv=dout("s_fox_v", [L, NS, TS, 256]),
                  logf=dout("s_fox_logf", [L, NS, TS, 4]), gdn=dout("s_gdn", [L, NS, 4, 128, 128]),
                  conv=dout("s_gdn_conv", [L, NS, 3, 1536]))
        c_ropeCs = din("ropeCs", [TS, 32])
        c_ropeSs = din("ropeSs", [TS, 32])

    main = Arena(S, SB_BASE, SB_LIMIT)
    ident_f = main.alloc("ident_f", [128, 128], F32)
    ident_b = main.alloc("ident_b", [128, 128], BF16)
    ones_b = main.alloc("ones_b", [128, 128], BF16)
    triu_f = main.alloc("triu_f", [128, 128], F32)
    triu_b = main.alloc("triu_b", [128, 128], BF16)
    ones_f = main.alloc("ones_f", [128, 128], F32)
    ropeCs = main.alloc("ropeCs", [128, 1, 32], F32)
    ropeSs = main.alloc("ropeSs", [128, 1, 32], F32)
    Mle = main.alloc("Mle", [128, 128], F32)
    Mgt = main.alloc("Mgt", [128, 128], F32)
    Sel0 = main.alloc("Sel0", [128, 128], F32)
    Sel1 = main.alloc("Sel1", [128, 128], F32)
    cwt = main.alloc("cwt", [128, L, 12, 4], F32)
    lnin = main.alloc("lnin", [128, 2, 8], F32)
    lnp = main.alloc("lnp", [128, L, 3, 2, 8], F32)
    TX = max(T if NP else 0, NS * TS)
    xf = main.alloc("xf", [128, 8, TX], F32)
    mixT = main.alloc("mixT", [128, 8, TX], BF16)
    memT = main.alloc("memT", [128, 8, N_MEM], BF16)
    ARENA0 = main.off

    O.dma(ident_f[:], c_ident_f)
    O.dma(triu_f[:], c_triu_f)
    O.dma(ones_f[:], c_ones_f)
    if NS:
        O.dma(ropeCs[0:TS, 0, :], c_ropeCs)
        O.dma(ropeSs[0:TS, 0, :], c_ropeSs)
    O.dma(Mle[:], c_Mle)
    O.dma(Mgt[:], c_Mgt)
    O.dma(Sel0[:], c_Sel0)
    O.dma(Sel1[:], c_Sel1)
    O.dma(cwt[:], conv_fm)
    O.dma(lnin[:], ln_in_fm)
    O.dma(lnp[:], ln_fm)
    O.copy(ident_b[:], ident_f[:])
    O.copy(triu_b[:], triu_f[:])
    O.copy(ones_b[:], ones_f[:])

    ps = S.psum

    def psb(i):
        return ps[i][:].bitcast(BF16)

    rr = {"ev": 0}

    def evac_eng():
        rr["ev"] ^= 1
        return "act" if rr["ev"] else "dve"

    def wload(dst, src):
        O.dma(dst, src, q="pool")

    def kpn(src):
        return src.rearrange("(k p) n -> p k n", p=128)

    def pbc(src_row):
        return src_row.partition_broadcast(128).rearrange("p o n -> p (o n)")

    def ln_block(ar, cols, g_ap, b_ap):
        n = cols.stop - cols.start
        v16 = ar.alloc("ln_v16", [128, 8, n], BF16)
        sq16 = ar.alloc("ln_sq16", [128, 8, n], BF16)
        mean = ar.alloc("ln_mean", [128, n], F32)
        msq = ar.alloc("ln_msq", [128, n], F32)
        rstd = ar.alloc("ln_rstd", [128, n], F32)
        xv = xf[:, :, cols]
        O.copy(v16[:], xv, eng="act")
        O.act(sq16[:], xv, AF.Square)
        for c in range(8):
            O.mm(ps[6][:, 0:n], ones_b[:], v16[:, c, :], start=(c == 0), stop=(c == 7))
        for c in range(8):
            O.mm(ps[7][:, 0:n], ones_b[:], sq16[:, c, :], start=(c == 0), stop=(c == 7))
        O.act(mean[:], ps[6][:, 0:n], AF.Copy, scale=1.0 / D)
        O.tt(msq[:], mean[:], mean[:], ALU.mult)
        O.stt(rstd[:], ps[7][:, 0:n], 1.0 / D, msq[:], ALU.mult, ALU.subtract)
        O.ts(rstd[:], rstd[:], LN_EPS, None, ALU.add)
        O.act(rstd[:], rstd[:], AF.Sqrt)
        O.recip(rstd[:], rstd[:])
        for hf in range(2):
            xh = xf[:, hf * 4:hf * 4 + 4, cols]
            O.tt(xh, xh, bc(mean[:].unsqueeze(1), [128, 4, n]), ALU.subtract)
            O.tt(xh, xh, bc(rstd[:].unsqueeze(1), [128, 4, n]), ALU.mult)
            for c in range(hf * 4, hf * 4 + 4):
                O.act(xf[:, c, cols], xf[:, c, cols], AF.Identity, bias=b_ap[:, c:c + 1], scale=g_ap[:, c:c + 1])

    def attention(ar, n_kt, kt_ops, qt_ops, v_ops, scale, out_dst, nq, bias_fn=None, mask_fn=None,
                  q0_fn=None, heads=4, dv=64, rows_fn=None):
        E = [ar.alloc(f"attE{i}", [128, nq], BF16) for i in range(4)]
        rden = ar.alloc("att_rden", [128, nq], F32)
        ei = 0
        group = 2 if dv == 64 else 1
        for h0 in range(0, heads, group):
            hs = list(range(h0, h0 + group))
            first = {h: True for h in hs}
            last_kt = n_kt - 1
            for kt in range(n_kt):
                c0 = q0_fn(kt) if q0_fn else 0
                if c0 >= nq:
                    continue
                r = rows_fn(kt) if rows_fn else 128
                for h in hs:
                    sb = ei % 2
                    Et = E[ei % 4]
                    ei += 1
                    O.mm(ps[sb][0:r, c0:nq], kt_ops(h, kt), qt_ops(h)[:, c0:nq])
                    b_ap = bias_fn(h, kt) if bias_fn else None
                    O.act(Et[0:r, c0:nq], ps[sb][0:r, c0:nq], AF.Exp, bias=b_ap, scale=scale)
                    if mask_fn:
                        mask_fn(kt, Et, c0)
                    po_ = (h - h0) * 64 if dv == 64 else 0
                    pn = 64 if dv == 64 else 128
                    O.mm(ps[2][po_:po_ + pn, c0:nq], v_ops(h, kt), Et[0:r, c0:nq], start=first[h], stop=(kt == last_kt))
                    O.mm(ps[3][po_:po_ + pn, c0:nq], ones_b[0:r, 0:pn], Et[0:r, c0:nq], start=first[h], stop=(kt == last_kt))
                    first[h] = False
            O.recip(rden[:], ps[3][:, 0:nq])
            O.tt(out_dst(h0 // group), ps[2][:, 0:nq], rden[:], ALU.mult)

    def load_ln_transpose(src_rows, tp, dst_cols, scale_ap, bias_ap, xt, scratch, normalize=True):
        st6, mv, rs = scratch
        O.dma(xt[0:tp, :], src_rows)
        if normalize:
            for j in range(2):
                S.op("dve", lambda e, j=j: e.bn_stats(st6[0:tp, j, :], xt[0:tp, j * 512:(j + 1) * 512]),
                     reads=[xt[0:tp, j * 512:(j + 1) * 512]], writes=[st6[0:tp, j, :]])
            S.op("dve", lambda e: e.bn_aggr(mv[0:tp, :], st6[0:tp].rearrange("p a b -> p (a b)")),
                 reads=[st6[0:tp]], writes=[mv[0:tp, :]])
            O.ts(rs[0:tp, :], mv[0:tp, 1:2], LN_EPS, None, ALU.add)
            O.act(rs[0:tp, :], rs[0:tp, :], AF.Sqrt)
            O.recip(rs[0:tp, :], rs[0:tp, :])
            O.ts(xt[0:tp, :], xt[0:tp, :], mv[0:tp, 0:1], rs[0:tp, 0:1], ALU.subtract, ALU.mult)
        for half in range(2):
            pb = ps[4 + half]
            for c4 in range(4):
                c = half * 4 + c4
                O.tr(pb[:, c4 * tp:(c4 + 1) * tp], xt[0:tp, c * 128:(c + 1) * 128], ident_f[0:tp, 0:tp])
            if normalize:
                for c4 in range(4):
                    c = half * 4 + c4
                    O.act(xf[:, c, dst_cols], pb[:, c4 * tp:(c4 + 1) * tp], AF.Identity,
                          bias=bias_ap[:, c:c + 1], scale=scale_ap[:, c:c + 1])
            else:
                O.copy(memT[:, half * 4:half * 4 + 4, dst_cols],
                       pb[:, 0:4 * tp].rearrange("p (c n) -> p c n", c=4), eng="act")

    def input_stage(G):
        ar = Arena(S, ARENA0, SB_LIMIT)
        xin = [ar.alloc(f"xin{i}", [128, D], F32) for i in range(2)]
        scratch = (ar.alloc("st6", [128, 2, 6], F32), ar.alloc("mv", [128, 2], F32), ar.alloc("rs", [128, 1], F32))
        n = 0
        for si in range(G.nseq):
            for ti in range(G.NT):
                c0 = G.col0(si) + ti * G.tp
                load_ln_transpose(G.x_in(si)[ti * G.tp:(ti + 1) * G.tp, :], G.tp, slice(c0, c0 + G.tp),
                                  lnin[:, 0, :], lnin[:, 1, :], xin[n % 2], scratch)
                n += 1
            if G.prompt:
                for mt in range(2):
                    load_ln_transpose(mem_p[G.gseq(si), mt * 128:(mt + 1) * 128, :], 128, slice(mt * 128, (mt + 1) * 128),
                                      None, None, xin[n % 2], scratch, normalize=False)
                    n += 1

    def final_stage(G):
        ar = Arena(S, ARENA0, SB_LIMIT)
        yst = [ar.alloc(f"yst{i}", [128, D], F32) for i in range(2)]
        n = 0
        tp = G.tp
        for si in range(G.nseq):
            for ti in range(G.NT):
                yt = yst[n % 2]
                n += 1
                c0 = G.col0(si) + ti * tp
                for half in range(2):
                    pb = ps[4 + half]
                    for c4 in range(4):
                        c = half * 4 + c4
                        O.tr(pb[0:tp, c4 * 128:(c4 + 1) * 128], xf[:, c, c0:c0 + tp], ident_f[:])
                    O.copy(yt[0:tp, half * 512:(half + 1) * 512], pb[0:tp, 0:512], eng=("act" if half else "dve"))
                O.dma(G.y_out(si)[ti * tp:(ti + 1) * tp, :], yt[0:tp, :])

    def p1_mla(G, l):
        tp, TB, TPB, NB, NPT = G.tp, G.TB, G.TPB, G.NB, G.NPT
        NKT = NPT + G.NT
        LK = G.past + G.T
        ar = Arena(S, ARENA0, SB_LIMIT)
        Wm = ar.alloc("Wm", [128, 8, 672], BF16)
        Wuq = ar.alloc("Wuq", [128, 3, 384], BF16)
        Wukv = ar.alloc("Wukv", [128, 2, 512], BF16)
        qag = ar.alloc("qag", [128, 384], F32)
        kvag = ar.alloc("kvag", [128, 256], F32)
        KT = ar.alloc("KT", [128, 4, LK], BF16)
        Vc = ar.alloc("Vc", [128, NKT, 4, 64], BF16)
        QTs = [ar.alloc(f"QT{i}", [128, 4, TB], BF16) for i in range(2)]
        xbs1 = [ar.alloc(f"xb{i}", [128, 8, TB], BF16) for i in range(2)]
        ckvf = [ar.alloc(f"ckvf{i}", [128, 256], F32) for i in range(2)]
        kpef = [ar.alloc(f"kpef{i}", [128, 32], F32) for i in range(2)]
        TB2 = [dict(cqn=ar.alloc(f"cqn{i}", [128, 384], BF16), ckvb=ar.alloc(f"ckvb{i}", [128, 256], BF16),
                    kpeb=ar.alloc(f"kpeb{i}", [128, 32], BF16), rt1=ar.alloc(f"rt1{i}", [128, 4, 32], F32),
                    rt2=ar.alloc(f"rt2{i}", [128, 4, 32], F32), cqT=ar.alloc(f"cqT{i}", [128, 3, 128], BF16),
                    ckvT=ar.alloc(f"ckvT{i}", [128, 2, 128], BF16), qtm=ar.alloc(f"qtm{i}", [128, 4, 96], BF16),
                    ktm=ar.alloc(f"ktm{i}", [128, 4, 96], BF16), ssq=ar.alloc(f"ssq{i}", [128, 16], F32),
                    junk=ar.alloc(f"junk{i}", [128, 384], BF16)) for i in range(2)]
        if NPT:
            ckvp = ar.alloc("ckvp", [128, NPT, 256], BF16)
            kpep = ar.alloc("kpep", [128, NPT, 32], BF16)
        if G.prompt:
            ropeC = ar.alloc("ropeC", [128, NTP, 32], F32)
            ropeS = ar.alloc("ropeS", [128, NTP, 32], F32)
            O.dma(ropeC[:], c_ropeC)
            O.dma(ropeS[:], c_ropeS)
            G.rope = lambda ti: (ropeC[:, ti, :], ropeS[:, ti, :])
        wload(Wm[:], kpn(w_in[l, :, C_CQ:C_CQ + 672]))
        wload(Wuq[:], kpn(w_uq[l]))
        wload(Wukv[:], kpn(w_ukv[l]))
        O.dma(qag[:], pbc(qa_g[l:l + 1, :]))
        O.dma(kvag[:], pbc(kva_g[l:l + 1, :]))

        def kv_tile(ckvb_ap, kpeb_ap, kt, r):
            kc = slice(kt * 128, kt * 128 + r)
            B_ = TB2[kt % 2]
            ckvT, ktm = B_["ckvT"], B_["ktm"]
            b5, b7 = (5, 7) if kt % 2 == 0 else (4, 6)
            for k in range(2):
                O.tr(psb(b7)[:, k * r:(k + 1) * r], ckvb_ap[:, k * 128:(k + 1) * 128], ident_b[0:r, 0:r])
            O.copy(ckvT[:, :, 0:r], psb(b7)[:, 0:2 * r].rearrange("p (k n) -> p k n", k=2), eng="dve")
            for k in range(2):
                O.mm(ps[b5][0:r, 0:512], ckvT[:, k, 0:r], Wukv[:, k, :], start=(k == 0), stop=(k == 1))
            kv3 = ps[b5][0:r, 0:512].rearrange("p (h d) -> p h d", h=4)
            O.copy(ktm[0:r, :, 0:64], kv3[:, :, 0:64], eng="act")
            O.copy(ktm[0:r, :, 64:96], bc(kpeb_ap.unsqueeze(1), [r, 4, 32]), eng="dve")
            O.copy(Vc[0:r, kt, :, :], kv3[:, :, 64:128], eng="act")
            for h in range(4):
                O.tr(psb(b7)[0:96, h * r:(h + 1) * r], ktm[0:r, h, :], ident_b[0:r, 0:r])
            O.copy(KT[0:96, :, kc], psb(b7)[0:96, 0:4 * r].rearrange("p (h n) -> p h n", h=4), eng="act")

        for si in range(G.nseq):
            if NPT:
                wload(ckvp[:], ci["ckv"][l, G.gseq(si)].rearrange("(t p) n -> p t n", p=128))
                wload(kpep[:], ci["kpe"][l, G.gseq(si)].rearrange("(t p) n -> p t n", p=128))
                for pt in range(NPT):
                    kv_tile(ckvp[:, pt, :], kpep[:, pt, :], pt, 128)
            for b in range(NB):
                x0 = G.col0(si) + b * TB
                bcols = slice(x0, x0 + TB)
                xb, QT = xbs1[b % 2], QTs[b % 2]
                O.copy(xb[:], xf[:, :, bcols], eng="dve")
                for t in range(TPB):
                    ti = b * TPB + t
                    tc = slice(t * tp, (t + 1) * tp)
                    gc = slice(ti * tp, (ti + 1) * tp)
                    rC, rS = G.rope(ti)
                    B_ = TB2[(NPT + ti) % 2]
                    cqn, ckvb, kpeb, rt1, rt2, cqT, qtm, ssq, junk = (B_[k_] for k_ in (
                        "cqn", "ckvb", "kpeb", "rt1", "rt2", "cqT", "qtm", "ssq", "junk"))
                    ps4, ps5, ps6 = (ps[4], ps[5], 6) if (NPT + ti) % 2 == 0 else (ps[6], ps[4], 5)
                    for k in range(8):
                        O.mm(ps4[0:tp, 0:384], xb[:, k, tc], Wm[:, k, 0:384], start=(k == 0), stop=(k == 7))
                    for k in range(8):
                        O.mm(ps5[0:tp, 0:288], xb[:, k, tc], Wm[:, k, 384:672], start=(k == 0), stop=(k == 7))
                    O.act(junk[0:tp, 0:384], ps4[0:tp, 0:384], AF.Square, accum=ssq[0:tp, 0:1])
                    O.act(junk[0:tp, 0:256], ps5[0:tp, 0:256], AF.Square, accum=ssq[0:tp, 1:2])
                    O.ts(ssq[0:tp, 0:1], ssq[0:tp, 0:1], 1.0 / 384, RMS_EPS, ALU.mult, ALU.add)
                    O.ts(ssq[0:tp, 1:2], ssq[0:tp, 1:2], 1.0 / 256, RMS_EPS, ALU.mult, ALU.add)
                    O.act(ssq[0:tp, 0:2], ssq[0:tp, 0:2], AF.Sqrt)
                    O.recip(ssq[0:tp, 0:2], ssq[0:tp, 0:2])
                    O.stt(cqn[0:tp, :], ps4[0:tp, 0:384], ssq[0:tp, 0:1], qag[0:tp, :], ALU.mult, ALU.mult)
                    cf = ckvf[ti % 2]
                    kf = kpef[ti % 2]
                    O.stt(cf[0:tp, :], ps5[0:tp, 0:256], ssq[0:tp, 1:2], kvag[0:tp, :], ALU.mult, ALU.mult)
                    O.copy(ckvb[0:tp, :], cf[0:tp, :], eng="act")
                    O.tt(rt1[0:tp, 0, :], ps5[0:tp, 256:288], rC, ALU.mult)
                    O.tt(rt2[0:tp, 0, 0:16], ps5[0:tp, 272:288], rS[:, 0:16], ALU.mult)
                    O.tt(rt2[0:tp, 0, 16:32], ps5[0:tp, 256:272], rS[:, 16:32], ALU.mult)
                    O.tt(kf[0:tp, :], rt1[0:tp, 0, :], rt2[0:tp, 0, :], ALU.add)
                    O.copy(kpeb[0:tp, :], kf[0:tp, :], eng="act")
                    O.dma(G.out("ckv", l, si)[gc, :], cf[0:tp, :])
                    O.dma(G.out("kpe", l, si)[gc, :], kf[0:tp, :])
                    for k in range(3):
                        O.tr(psb(ps6)[:, k * tp:(k + 1) * tp], cqn[0:tp, k * 128:(k + 1) * 128], ident_b[0:tp, 0:tp])
                    O.copy(cqT[:, :, 0:tp], psb(ps6)[:, 0:3 * tp].rearrange("p (k n) -> p k n", k=3), eng="act")
                    for k in range(3):
                        O.mm(ps4[0:tp, 0:384], cqT[:, k, 0:tp], Wuq[:, k, :], start=(k == 0), stop=(k == 2))
                    q3 = ps4[0:tp, 0:384].rearrange("p (h d) -> p h d", h=4)
                    O.copy(qtm[0:tp, :, 0:64], q3[:, :, 0:64], eng="act")
                    O.tt(rt1[0:tp], q3[:, :, 64:96], bc(rC.unsqueeze(1), [tp, 4, 32]), ALU.mult)
                    O.tt(rt2[0:tp, :, 0:16], q3[:, :, 80:96], bc(rS[:, 0:16].unsqueeze(1), [tp, 4, 16]), ALU.mult)
                    O.tt(rt2[0:tp, :, 16:32], q3[:, :, 64:80], bc(rS[:, 16:32].unsqueeze(1), [tp, 4, 16]), ALU.mult)
                    O.tt(qtm[0:tp, :, 64:96], rt1[0:tp], rt2[0:tp], ALU.add)
                    for h in range(4):
                        O.tr(psb(ps6)[0:96, h * tp:(h + 1) * tp], qtm[0:tp, h, :], ident_b[0:tp, 0:tp])
                    O.copy(QT[0:96, :, tc], psb(ps6)[0:96, 0:4 * tp].rearrange("p (h n) -> p h n", h=4), eng="dve")
                    kv_tile(ckvb[0:tp, :], kpeb[0:tp, :], NPT + ti, tp)
                n_kt = NPT + (b + 1) * TPB

                def rows_fn(kt):
                    return 128 if kt < NPT else tp

                def q0_fn(kt, b=b):
                    return max(0, (kt - NPT - b * TPB) * 128) if G.prompt else 0

                def mask_fn(kt, Et, c0, b=b):
                    if G.prompt and kt >= b * TPB:
                        O.memset(Et[64:128, c0:c0 + 64], 0.0, eng="pool")
                attention(ar_attn(ar), n_kt,
                          lambda h, kt: KT[0:96, h, kt * 128:kt * 128 + rows_fn(kt)],
                          lambda h, QT=QT: QT[0:96, h, :],
                          lambda h, kt: Vc[0:rows_fn(kt), kt, h, :],
                          96 ** -0.5,
                          lambda hp, bcols=bcols: mixT[:, hp, bcols],
                          TB, q0_fn=q0_fn, mask_fn=mask_fn, rows_fn=rows_fn)

    def p2_fox(G, l):
        tp, TB, TPB, NB, NPT = G.tp, G.TB, G.TPB, G.NB, G.NPT
        NKT = NPT + G.NT
        LK = G.past + G.T
        ar = Arena(S, ARENA0, SB_LIMIT)
        Wf = ar.alloc("Wf", [128, 8, 772], BF16)
        KTf = ar.alloc("KTf", [128, 2, LK], BF16)
        Vf = ar.alloc("Vf", [128, NKT, 4, 64], BF16)
        fqTs = [ar.alloc(f"fqT{i}", [128, 2, TB], BF16) for i in range(2)]
        xbs2 = [ar.alloc(f"xb{i}", [128, 8, TB], BF16) for i in range(2)]
        fkvf = [ar.alloc(f"fkvf{i}", [128, 512], F32) for i in range(2)]
        lgf = [ar.alloc(f"lgf{i}", [128, 4], F32) for i in range(2)]
        fcum = ar.alloc("fcum", [128, NKT, 4], F32)
        carry = ar.alloc("carry", [128, 4], F32)
        biasF = ar.alloc("biasF", [128, NKT, 4], F32)
        bfb = ar.alloc("bfb", [128, 4], F32)
        ltmp = ar.alloc("ltmp", [128, 4], F32)
        if NPT:
            fkp = ar.alloc("fkp", [128, NPT, 256], BF16)
            lgp = ar.alloc("lgp", [128, NPT, 4], F32)
        wload(Wf[:], kpn(w_in[l, :, C_FQ:C_FQ + 772]))
        O.dma(bfb[:], pbc(fox_bf[l:l + 1, :]))

        def cum_tile(lg_ap, kt, r):
            O.mm(ps[7][0:r, 8:12], triu_f[0:r, 0:r], lg_ap)
            O.mm(ps[7][:, 16:20], ones_f[0:r, :], lg_ap)
            O.tt(fcum[0:r, kt, :], ps[7][0:r, 8:12], carry[0:r, :], ALU.add)
            O.tt(carry[:], ps[7][:, 16:20], carry[:], ALU.add)

        for si in range(G.nseq):
            O.memset(carry[:], 0.0)
            if NPT:
                gs = G.gseq(si)
                wload(fkp[:], ci["fk"][l, gs].rearrange("(t p) n -> p t n", p=128))
                wload(Vf[:, 0:NPT, :, :], ci["fv"][l, gs].rearrange("(t p) (h d) -> p t h d", p=128, h=4))
                O.dma(lgp[:], ci["logf"][l, gs].rearrange("(t p) h -> p t h", p=128))
                for pt in range(NPT):
                    for c in range(2):
                        O.tr(psb(6)[:, c * 128:(c + 1) * 128], fkp[:, pt, c * 128:(c + 1) * 128], ident_b[:])
                    O.copy(KTf[:, :, pt * 128:(pt + 1) * 128], psb(6)[:, 0:256].rearrange("p (c n) -> p c n", c=2),
                           eng=evac_eng())
                    cum_tile(lgp[:, pt, :], pt, 128)
            for b in range(NB):
                x0 = G.col0(si) + b * TB
                bcols = slice(x0, x0 + TB)
                kcols = slice(G.past + b * TB, G.past + (b + 1) * TB)
                xb, fqT = xbs2[b % 2], fqTs[b % 2]
                O.copy(xb[:], xf[:, :, bcols], eng="dve")
                for c in range(2):
                    for k in range(8):
                        O.mm(ps[4][:, 0:TB], Wf[:, k, c * 128:(c + 1) * 128], xb[:, k, :], start=(k == 0), stop=(k == 7))
                    O.copy(fqT[:, c, :], ps[4][:, 0:TB], eng="act")
                    for k in range(8):
                        O.mm(ps[5][:, 0:TB], Wf[:, k, 256 + c * 128:256 + (c + 1) * 128], xb[:, k, :],
                             start=(k == 0), stop=(k == 7))
                    O.copy(KTf[:, c, kcols], ps[5][:, 0:TB], eng="dve")
                for t in range(TPB):
                    ti = b * TPB + t
                    tc = slice(t * tp, (t + 1) * tp)
                    gc = slice(ti * tp, (ti + 1) * tp)
                    for k in range(8):
                        O.mm(ps[6][0:tp, 0:512], xb[:, k, tc], Wf[:, k, 256:768], start=(k == 0), stop=(k == 7))
                    for k in range(8):
                        O.mm(ps[7][0:tp, 0:4], xb[:, k, tc], Wf[:, k, 768:772], start=(k == 0), stop=(k == 7))
                    ff_ = fkvf[ti % 2]
                    O.copy(ff_[0:tp, :], ps[6][0:tp, 0:512], eng="act")
                    O.copy(Vf[0:tp, NPT + ti, :, :], ps[6][0:tp, 256:512].rearrange("p (h d) -> p h d", h=4), eng="dve")
                    O.dma(G.out("fk", l, si)[gc, :], ff_[0:tp, 0:256])
                    O.dma(G.out("fv", l, si)[gc, :], ff_[0:tp, 256:512])
                    lg = lgf[ti % 2]
                    O.tt(ltmp[0:tp, :], ps[7][0:tp, 0:4], bfb[0:tp, :], ALU.add)
                    O.act(ltmp[0:tp, :], ltmp[0:tp, :], AF.Exp, scale=-1.0)
                    O.act(ltmp[0:tp, :], ltmp[0:tp, :], AF.Ln, bias=1.0)
                    O.ts(lg[0:tp, :], ltmp[0:tp, :], -1.0, None, ALU.mult)
                    O.dma(G.out("logf", l, si)[gc, :], lg[0:tp, :])
                    cum_tile(lg[0:tp, :], NPT + ti, tp)
                n_kt = NPT + (b + 1) * TPB
                O.tt(biasF[:, 0:n_kt, :], bc(carry[:].unsqueeze(1), [128, n_kt, 4]), fcum[:, 0:n_kt, :], ALU.subtract)

                def rows_fn(kt):
                    return 128 if kt < NPT else tp

                def q0_fn(kt, b=b):
                    return max(0, (kt - NPT - b * TPB) * tp)

                def mask_fn(kt, Et, c0, b=b):
                    if kt >= NPT + b * TPB:
                        O.tt(Et[0:tp, c0:c0 + tp], Et[0:tp, c0:c0 + tp], triu_b[0:tp, 0:tp], ALU.mult, eng="pool")
                attention(ar_attn(ar), n_kt,
                          lambda h, kt: KTf[(h % 2) * 64:(h % 2) * 64 + 64, h // 2, kt * 128:kt * 128 + rows_fn(kt)],
                          lambda h, fqT=fqT: fqT[(h % 2) * 64:(h % 2) * 64 + 64, h // 2, :],
                          lambda h, kt: Vf[0:rows_fn(kt), kt, h, :],
                          64 ** -0.5,
                          lambda hp, bcols=bcols: mixT[:, 6 + hp, bcols],
                          TB, bias_fn=lambda h, kt: biasF[0:rows_fn(kt), kt, h:h + 1], q0_fn=q0_fn, mask_fn=mask_fn,
                          rows_fn=rows_fn)

    def p3_gdn(G, l):
        tp = G.tp
        CH = min(64, tp)
        NCH = tp // CH
        GB = min(256, G.T)
        NGB = G.T // GB
        TPG = GB // tp
        NLEV = 6 if CH == 64 else 5
        ar = Arena(S, ARENA0, SB_LIMIT)
        Wg = ar.alloc("Wg", [128, 8, 2056], BF16)
        halo4 = ar.alloc("halo", [128, 12, 4], F32)
        halo = halo4[:, :, 0:3]
        NR = 3
        pre = [ar.alloc(f"pre{i}", [128, 3 + GB], F32) for i in range(NR)]
        acc = [ar.alloc(f"acc{i}", [128, GB], F32) for i in range(NR)]
        sq16s = [ar.alloc(f"gsq16{i}", [128, GB], BF16) for i in range(NR)]
        rsts = [ar.alloc(f"grst{i}", [128, GB], F32) for i in range(NR)]
        qnTs = [ar.alloc(f"qnT{i}", [128, 4, GB], BF16) for i in range(2)]
        knT = ar.alloc("knT", [128, 4, GB], BF16)
        vT = ar.alloc("vT", [128, 4, GB], BF16)
        ktm = ar.alloc("gktm", [128, TPG, 4, 128], BF16)
        vtm = ar.alloc("gvtm", [128, TPG, 4, 128], BF16)
        zs = ar.alloc("zs", [128, TPG, 4, 128], BF16)
        gg = ar.alloc("gg", [128, TPG, 4], F32)
        bet = ar.alloc("bet", [128, TPG, 4], F32)
        xb = ar.alloc("gxb", [128, 8, GB], BF16)
        dtb = ar.alloc("dtb", [128, 4], F32)
        nega = ar.alloc("nega", [128, 4], F32)
        ngb = ar.alloc("ngb", [128, 128], F32)
        Sf = ar.alloc("Sf", [128, 4, 128], F32)
        Sb = ar.alloc("Sb", [128, 4, 128], BF16)
        Bg = ar.alloc("Bg", [128, 4, 128], F32)
        Dm = ar.alloc("Dm", [128, 4, 128], F32)
        DTm = ar.alloc("DTm", [128, 4, 128], F32)
        tmpf = ar.alloc("tmpf", [128, 4, 128], F32)
        Xm = ar.alloc("Xm", [128, 4, 128], BF16)
        XTm = ar.alloc("XTm", [128, 4, 128], BF16)
        Pm = [ar.alloc(f"Pm{i}", [128, 4, 128], BF16) for i in range(2)]
        PTm = [ar.alloc(f"PTm{i}", [128, 4, 128], BF16) for i in range(2)]
        Ym = ar.alloc("Ym", [128, 4, 128], BF16)
        vb = ar.alloc("vb", [128, 4, 128], BF16)
        kbg = ar.alloc("kbg", [128, 4, 128], BF16)
        kdecs = [ar.alloc(f"kdec{i}", [128, 4, 128], BF16) for i in range(2)]
        usbs = [ar.alloc(f"usb{i}", [128, 4, 128], F32) for i in range(2)]
        wTss = [ar.alloc(f"wTs{i}", [128, 4, 128], BF16) for i in range(2)]
        qkdTs = [ar.alloc(f"qkdT{i}", [128, 4, 128], BF16) for i in range(2)]
        vnew = ar.alloc("vnew", [128, 4, 128], BF16)
        osb = ar.alloc("osb", [128, 4, 128], F32)
        tsb = ar.alloc("tsb", [128, 4, 128], F32)
        obt = ar.alloc("obt", [128, 4, 128], BF16)
        svs = [ar.alloc(f"sv{i}", [128, 32], F32) for i in range(2)]
        epsb = ar.alloc("epsb", [128, 16], F32)
        cst = ar.alloc("cst", [128, 512], F32)
        wload(Wg[:], kpn(w_in[l, :, C_GQ:C_GQ + 2056]))
        O.dma(dtb[:], pbc(dt_bias[l:l + 1, :]))
        O.dma(nega[:], pbc(a_log[l:l + 1, :]))
        O.dma(ngb[:], pbc(norm_g[l:l + 1, :]))
        O.act(nega[:], nega[:], AF.Exp)
        O.ts(nega[:], nega[:], -1.0, None, ALU.mult)
        O.memset(epsb[:], 1e-6)

        def v4(ap, n=128):
            return ap.rearrange("p (h n) -> p h n", h=4)

        for si in range(G.nseq):
            if G.prompt:
                O.memset(halo4[:], 0.0)
                O.memset(Sf[:], 0.0)
                O.memset(Sb[:], 0.0)
            else:
                gs = G.gseq(si)
                O.dma(Sf[:], ci["gdn"][l, gs].rearrange("h k v -> k h v"))
                O.copy(Sb[:], Sf[:], eng="act")
                for grp in range(3):
                    O.dma(cst[0:3, :], ci["conv"][l, gs][:, grp * 512:(grp + 1) * 512])
                    for c4 in range(4):
                        c = grp * 4 + c4
                        O.tr(ps[7][:, c * 3:(c + 1) * 3], cst[0:3, c4 * 128:(c4 + 1) * 128], ident_f[0:3, 0:3])
                O.copy(halo[:], ps[7][:, 0:36].rearrange("p (c j) -> p c j", c=12), eng="act")
            for gbk in range(NGB):
                x0 = G.col0(si) + gbk * GB
                gcols = slice(x0, x0 + GB)
                O.copy(xb[:], xf[:, :, gcols], eng="dve")
                qnT = qnTs[gbk % 2]
                S.tag = "g_conv"
                for c in range(12):
                    pr = pre[c % NR]
                    ac = acc[c % NR]
                    sq16 = sq16s[c % NR]
                    rst = rsts[c % NR]
                    pb = ps[c % 2]
                    for k in range(8):
                        O.mm(pb[:, 0:GB], Wg[:, k, c * 128:(c + 1) * 128], xb[:, k, :], start=(k == 0), stop=(k == 7))
                    O.copy(pr[:, 0:3], halo[:, c, :], eng="dve")
                    O.copy(pr[:, 3:3 + GB], pb[:, 0:GB], eng="act")
                    O.copy(halo[:, c, :], pr[:, GB:GB + 3], eng="dve")
                    O.act(ac[:], pr[:, 0:GB], AF.Copy, scale=cwt[:, l, c, 0:1])
                    for j in range(1, 4):
                        O.stt(ac[:], pr[:, j:j + GB], cwt[:, l, c, j:j + 1], ac[:], ALU.mult, ALU.add)
                    if c < 8:
                        dst = qnT if c < 4 else knT
                        h = c % 4
                        O.act(ac[:], ac[:], AF.Silu)
                        O.act(sq16[:], ac[:], AF.Square)
                        O.mm(ps[2 + c % 2][:, 0:GB], ones_b[:], sq16[:])
                        O.act(rst[:], ps[2 + c % 2][:, 0:GB], AF.Sqrt, bias=epsb[:, 0:1])
                        O.recip(rst[:], rst[:])
                        if c < 4:
                            O.stt(dst[:, h, :], ac[:], GDN_DK ** -0.5, rst[:], ALU.mult, ALU.mult)
                        else:
                            O.tt(dst[:, h, :], ac[:], rst[:], ALU.mult)
                    else:
                        O.act(vT[:, c - 8, :], ac[:], AF.Silu)
                for t in range(TPG):
                    ti = gbk * TPG + t
                    tc = slice(t * tp, (t + 1) * tp)
                    gc = slice(x0 + t * tp, x0 + (t + 1) * tp)
                    R = slice(0, tp)
                    idb = ident_b[0:tp, 0:tp]
                    kdec, usb, wTs, qkdT, sv = (x_[ti % 2] for x_ in (kdecs, usbs, wTss, qkdTs, svs))
                    S.tag = "g_prep"
                    for h in range(4):
                        O.tr(psb(3)[R, h * 128:(h + 1) * 128], knT[:, h, tc], ident_b[:])
                    O.copy(ktm[R, t, :, :], v4(psb(3)[R, 0:512]), eng="act")
                    for h in range(4):
                        O.tr(psb(0)[R, h * 128:(h + 1) * 128], vT[:, h, tc], ident_b[:])
                    O.copy(vtm[R, t, :, :], v4(psb(0)[R, 0:512]), eng="act")
                    for k in range(8):
                        O.mm(ps[1][R, 0:512], xb[:, k, tc], Wg[:, k, 1536:2048], start=(k == 0), stop=(k == 7))
                    for k in range(8):
                        O.mm(ps[6][R, 0:8], xb[:, k, tc], Wg[:, k, 2048:2056], start=(k == 0), stop=(k == 7))
                    O.act(zs[R, t, :, :], v4(ps[1][R, 0:512]), AF.Silu)
                    g_t = gg[R, t, :]
                    b_t = bet[R, t, :]
                    O.tt(g_t, ps[6][R, 0:4], dtb[R, :], ALU.add)
                    O.ts(g_t, g_t, 30.0, None, ALU.min)
                    O.act(g_t, g_t, AF.Exp)
                    O.act(g_t, g_t, AF.Ln, bias=1.0)
                    O.tt(g_t, g_t, nega[R, :], ALU.mult)
                    O.act(b_t, ps[6][R, 4:8], AF.Sigmoid)
                    mle = Mle[R, R]
                    mgt = Mgt[R, R]
                    O.mm(ps[6][R, 16:20], mle, g_t)
                    O.mm(ps[6][R, 24:28], mgt, g_t)
                    O.mm(ps[6][:, 32:36], Sel0[R, :], g_t)
                    if NCH == 2:
                        O.mm(ps[6][:, 36:40], Sel1[R, :], g_t)
                    O.act(sv[R, 0:4], ps[6][R, 16:20], AF.Exp)
                    O.act(sv[R, 4:8], ps[6][R, 24:28], AF.Exp)
                    O.act(sv[:, 8:8 + 4 * NCH], ps[6][:, 32:32 + 4 * NCH], AF.Exp)
                    O.tt(sv[R, 16:20], sv[R, 0:4], b_t, ALU.mult)
                    O.ts(sv[R, 20:24], b_t, -1.0, None, ALU.mult)
                    O.tt(Bg[R, :, R], bc(mgt.unsqueeze(1), [tp, 4, tp]), bc(g_t.unsqueeze(2), [tp, 4, tp]), ALU.mult)
                    for h in range(4):
                        O.mm(ps[7][R, h * 128:h * 128 + tp], mle, Bg[R, h, R])
                    O.act(Dm[R, :, R], v4(ps[7][R, 0:512])[:, :, R], AF.Exp)
                    O.tt(Dm[R, :, R], Dm[R, :, R], bc(mgt.unsqueeze(1), [tp, 4, tp]), ALU.mult)
                    for h in range(4):
                        O.mm(ps[7][R, h * 128:h * 128 + tp], Bg[R, h, R], mle)
                    O.act(DTm[R, :, R], v4(ps[7][R, 0:512])[:, :, R], AF.Exp)
                    O.tt(DTm[R, :, R], DTm[R, :, R], bc(mle.unsqueeze(1), [tp, 4, tp]), ALU.mult)
                    for h in range(4):
                        O.mm(ps[0][R, h * 128:h * 128 + tp], knT[:, h, tc], knT[:, h, tc])
                    O.tt(tmpf[R, :, R], v4(ps[0][R, 0:512])[:, :, R], Dm[R, :, R], ALU.mult)
                    O.tt(Xm[R, :, R], tmpf[R, :, R], bc(sv[R, 20:24].unsqueeze(2), [tp, 4, tp]), ALU.mult)
                    for h in range(4):
                        O.mm(ps[1][R, h * 128:h * 128 + tp], knT[:, h, tc], qnT[:, h, tc])
                    O.tt(qkdT[R, :, R], v4(ps[1][R, 0:512])[:, :, R], DTm[R, :, R], ALU.mult)
                    for h in range(4):
                        O.tr(psb(3)[R, h * 128:h * 128 + tp], Xm[R, h, R], idb)
                    O.copy(XTm[R, :, R], v4(psb(3)[R, 0:512])[:, :, R], eng="act")
                    O.tt(Ym[R, :, R], XTm[R, :, R], bc(idb.unsqueeze(1), [tp, 4, tp]), ALU.add)
                    P_, PT_ = XTm, Xm
                    for lev in range(1, NLEV):
                        Pn, PTn = Pm[lev % 2], PTm[lev % 2]
                        lastl = lev == NLEV - 1
                        if not lastl:
                            for h in range(4):
                                O.mm(ps[2][R, h * 128:h * 128 + tp], PT_[R, h, R], P_[R, h, R])
                        for h in range(4):
                            O.mm(ps[6][R, h * 128:h * 128 + tp], P_[R, h, R], PT_[R, h, R])
                        if not lastl:
                            O.copy(Pn[R, :, R], v4(ps[2][R, 0:512])[:, :, R], eng="dve")
                        O.copy(PTn[R, :, R], v4(ps[6][R, 0:512])[:, :, R], eng="act")
                        for h in range(4):
                            O.mm(ps[7][R, h * 128:h * 128 + tp], PTn[R, h, R], Ym[R, h, R])
                        O.tt(Ym[R, :, R], v4(ps[7][R, 0:512])[:, :, R], Ym[R, :, R], ALU.add)
                        P_, PT_ = Pn, PTn
                    O.tt(vb[R], vtm[R, t, :, :], bc(b_t.unsqueeze(2), [tp, 4, 128]), ALU.mult, eng="pool")
                    O.tt(kbg[R], ktm[R, t, :, :], bc(sv[R, 16:20].unsqueeze(2), [tp, 4, 128]), ALU.mult)
                    O.tt(kdec[R], ktm[R, t, :, :], bc(sv[R, 4:8].unsqueeze(2), [tp, 4, 128]), ALU.mult, eng="pool")
                    for h in range(4):
                        O.mm(ps[0][R, h * 128:(h + 1) * 128], Ym[R, h, R], vb[R, h, :])
                    O.copy(usb[R], v4(ps[0][R, 0:512]), eng="act")
                    for h in range(4):
                        O.mm(ps[1][:, h * 128:h * 128 + tp], kbg[R, h, :], Ym[R, h, R])
                    O.copy(wTs[:, :, R], v4(ps[1][:, 0:512])[:, :, R], eng="act")
                    S.tag = "g_rec"
                    for cch in range(NCH):
                        rows = slice(cch * CH, (cch + 1) * CH)
                        ccols = slice(t * tp + cch * CH, t * tp + (cch + 1) * CH)
                        for h in range(4):
                            O.mm(ps[4][rows, h * 128:(h + 1) * 128], wTs[:, h, rows], Sb[:, h, :])
                        O.tt(vnew[rows], usb[rows], v4(ps[4][rows, 0:512]), ALU.subtract)
                        for h in range(4):
                            O.mm(ps[5][rows, h * 128:(h + 1) * 128], qnT[:, h, ccols], Sb[:, h, :])
                        O.tt(tsb[rows], v4(ps[5][rows, 0:512]), bc(sv[rows, 0:4].unsqueeze(2), [CH, 4, 128]), ALU.mult)
                        for h in range(4):
                            O.mm(ps[4][rows, h * 128:(h + 1) * 128], qkdT[rows, h, rows], vnew[rows, h, :])
                        for h in range(4):
                            O.mm(ps[5][:, h * 128:(h + 1) * 128], kdec[rows, h, :], vnew[rows, h, :])
                        O.tt(osb[rows], tsb[rows], v4(ps[4][rows, 0:512]), ALU.add)
                        O.tt(Sf[:], Sf[:], bc(sv[:, 8 + 4 * cch:12 + 4 * cch].unsqueeze(2), [128, 4, 128]), ALU.mult)
                        O.tt(Sf[:], Sf[:], v4(ps[5][:, 0:512]), ALU.add)
                        O.copy(Sb[:], Sf[:], eng="act")
                    S.tag = "g_out"
                    for h in range(4):
                        O.act(tsb[R, h, :], osb[R, h, :], AF.Square, accum=sv[R, 24 + h:25 + h])
                    O.ts(sv[R, 24:28], sv[R, 24:28], 1.0 / GDN_DV, RMS_EPS, ALU.mult, ALU.add)
                    O.act(sv[R, 24:28], sv[R, 24:28], AF.Sqrt)
                    O.recip(sv[R, 24:28], sv[R, 24:28])
                    O.tt(tsb[R], osb[R], bc(sv[R, 24:28].unsqueeze(2), [tp, 4, 128]), ALU.mult)
                    O.tt(tsb[R], tsb[R], bc(ngb[R, :].unsqueeze(1), [tp, 4, 128]), ALU.mult)
                    O.tt(obt[R], tsb[R], zs[R, t, :, :], ALU.mult)
                    for h in range(4):
                        O.tr(psb(4)[:, h * tp:(h + 1) * tp], obt[R, h, :], idb)
                    O.copy(mixT[:, 2:6, gc], psb(4)[:, 0:4 * tp].rearrange("p (h n) -> p h n", h=4), eng="act")
            O.dma(G.out("gdn", l, si).rearrange("h k v -> k h v"), Sf[:])
            for grp in range(3):
                for c4 in range(4):
                    c = grp * 4 + c4
                    O.tr(ps[7][0:3, c4 * 128:(c4 + 1) * 128], halo[:, c, :], ident_f[:])
                O.copy(cst[0:3, :], ps[7][0:3, 0:512], eng="act")
                O.dma(G.out("conv", l, si)[:, grp * 512:(grp + 1) * 512], cst[0:3, :])

    def dense_blocks(G):
        tot = G.nseq * G.T
        bs = min(512, tot)
        return [slice(i, i + bs) for i in range(0, tot, bs)]

    def p4_out(G, l):
        ar = Arena(S, ARENA0, SB_LIMIT)
        Wo = ar.alloc("Wo", [128, 8, D], BF16)
        wload(Wo[:], kpn(w_out[l]))
        for bi, bcols in enumerate(dense_blocks(G)):
            n = bcols.stop - bcols.start
            for oc in range(8):
                pb = ps[oc % 4]
                for k in range(8):
                    O.mm(pb[:, 0:n], Wo[:, k, oc * 128:(oc + 1) * 128], mixT[:, k, bcols], start=(k == 0), stop=(k == 7))
                O.stt(xf[:, oc, bcols], xf[:, oc, bcols], ALPHA, pb[:, 0:n], ALU.mult, ALU.add)
            ln_block(ar_attn(ar), bcols, lnp[:, l, 0, 0, :], lnp[:, l, 0, 1, :])

    def p5_mem(G, l):
        TB, NB = G.TB, G.NB
        ar = Arena(S, ARENA0, SB_LIMIT)
        Wq = ar.alloc("Wq", [128, 8, 512], BF16)
        Wxo = ar.alloc("Wxo", [128, 4, D], BF16)
        KTm = ar.alloc("KTm", [128, 4, N_MEM], BF16)
        Vm = ar.alloc("Vm", [128, 2, 512], BF16)
        qmTs = [ar.alloc(f"qmT{i}", [128, 4, TB], BF16) for i in range(2)]
        omTs = [ar.alloc(f"omT{i}", [128, 4, TB], BF16) for i in range(2)]
        xbs = [ar.alloc(f"xb{i}", [128, 8, TB], BF16) for i in range(2)]
        if G.prompt:
            Wk = ar.alloc("Wk", [128, 8, 512], BF16)
            Wv = ar.alloc("Wv", [128, 8, 512], BF16)
            mst = [ar.alloc(f"mst{i}", [128, 512], F32) for i in range(2)]
            wload(Wk[:], kpn(w_mk[l]))
            wload(Wv[:], kpn(w_mv[l]))
        else:
            kmp = ar.alloc("kmp", [128, 2, 512], BF16)
        wload(Wq[:], kpn(w_xq[l]))
        wload(Wxo[:], kpn(w_xo[l]))
        for si in range(G.nseq):
            gs = G.gseq(si)
            if G.prompt:
                for h in range(4):
                    for k in range(8):
                        O.mm(ps[4][:, 0:N_MEM], Wk[:, k, h * 128:(h + 1) * 128], memT[:, k, :], start=(k == 0), stop=(k == 7))
                    O.copy(KTm[:, h, :], ps[4][:, 0:N_MEM], eng=evac_eng())
                for mt in range(2):
                    mc = slice(mt * 128, (mt + 1) * 128)
                    for wi, (W_, o_) in enumerate(((Wk, o_mk), (Wv, o_mv))):
                        pb = ps[5 + wi]
                        for k in range(8):
                            O.mm(pb[:, 0:512], memT[:, k, mc], W_[:, k, :], start=(k == 0), stop=(k == 7))
                        st = mst[wi]
                        O.copy(st[:], pb[:, 0:512], eng="act")
                        if wi == 1:
                            O.copy(Vm[:, mt, :], pb[:, 0:512], eng="dve")
                        O.dma(o_[l, gs, mc, :], st[:])
            else:
                wload(kmp[:], ci["mk"][l, gs].rearrange("(t p) n -> p t n", p=128))
                wload(Vm[:], ci["mv"][l, gs].rearrange("(t p) n -> p t n", p=128))
                for mt in range(2):
                    for h in range(4):
                        O.tr(psb(4)[:, h * 128:(h + 1) * 128], kmp[:, mt, h * 128:(h + 1) * 128], ident_b[:])
                    O.copy(KTm[:, :, mt * 128:(mt + 1) * 128], psb(4)[:, 0:512].rearrange("p (h n) -> p h n", h=4),
                           eng=evac_eng())
            for b in range(NB):
                x0 = G.col0(si) + b * TB
                bcols = slice(x0, x0 + TB)
                xb, qmT, omT = xbs[b % 2], qmTs[b % 2], omTs[b % 2]
                O.copy(xb[:], xf[:, :, bcols], eng="dve")
                for h in range(4):
                    for k in range(8):
                        O.mm(ps[4 + h % 2][:, 0:TB], Wq[:, k, h * 128:(h + 1) * 128], xb[:, k, :], start=(k == 0), stop=(k == 7))
                    O.copy(qmT[:, h, :], ps[4 + h % 2][:, 0:TB], eng=evac_eng())
                attention(ar_attn(ar), 2,
                          lambda h, kt: KTm[:, h, kt * 128:(kt + 1) * 128],
                          lambda h, qmT=qmT: qmT[:, h, :],
                          lambda h, kt: Vm[:, kt, h * 128:(h + 1) * 128],
                          128 ** -0.5,
                          lambda hp, omT=omT: omT[:, hp, :],
                          TB, dv=128)
                for oc in range(8):
                    pb = ps[4 + oc % 2]
                    for k in range(4):
                        O.mm(pb[:, 0:TB], Wxo[:, k, oc * 128:(oc + 1) * 128], omT[:, k, :], start=(k == 0), stop=(k == 3))
                    O.stt(xf[:, oc, bcols], xf[:, oc, bcols], ALPHA, pb[:, 0:TB], ALU.mult, ALU.add)
                ln_block(ar_attn(ar), bcols, lnp[:, l, 1, 0, :], lnp[:, l, 1, 1, :])

    def p6_ffn(G, l):
        ar = Arena(S, ARENA0, SB_LIMIT)
        blocks = dense_blocks(G)
        n = blocks[0].stop - blocks[0].start
        xbf = mixT
        W1 = [ar.alloc(f"W1_{i}", [128, 8, 1024], BF16) for i in range(2)]
        W2 = [ar.alloc(f"W2_{i}", [128, 8, 1024], BF16) for i in range(2)]
        hb = [ar.alloc(f"hb{i}", [128, 8, n], BF16) for i in range(2)]
        relu = [ar.alloc(f"relu{i}", [128, n], F32) for i in range(4)]
        for bcols in blocks:
            O.copy(xbf[:, :, bcols], xf[:, :, bcols], eng="act")
            O.ts(xf[:, :, bcols], xf[:, :, bcols], ALPHA, None, ALU.mult)
        for q in range(4):
            w1, w2 = W1[q % 2], W2[q % 2]
            wload(w1[:], kpn(w_ff1[l, :, q * 1024:(q + 1) * 1024]))
            wload(w2[:], kpn(w_ff2[l, q * 1024:(q + 1) * 1024, :]))
            for bi, bcols in enumerate(blocks):
                hh = hb[bi % 2]
                for hc in range(8):
                    pb = ps[hc % 4]
                    for k in range(8):
                        O.mm(pb[:, 0:n], w1[:, k, hc * 128:(hc + 1) * 128], xbf[:, k, bcols], start=(k == 0), stop=(k == 7))
                    rl = relu[hc % 4]
                    O.act(rl[:], pb[:, 0:n], AF.Relu)
                    if hc % 2 == 0:
                        O.tt(hh[:, hc, :], rl[:], rl[:], ALU.mult)
                    else:
                        O.act(hh[:, hc, :], rl[:], AF.Square)
                for oc in range(8):
                    pb = ps[4 + oc % 4]
                    for k in range(8):
                        O.mm(pb[:, 0:n], w2[:, k, oc * 128:(oc + 1) * 128], hh[:, k, :], start=(k == 0), stop=(k == 7))
                    O.tt(xf[:, oc, bcols], xf[:, oc, bcols], pb[:, 0:n], ALU.add)
        for bi, bcols in enumerate(blocks):
            ln_block(Arena(S, ARENA0, SB_LIMIT), bcols, lnp[:, l, 2, 0, :], lnp[:, l, 2, 1, :])

    groups = []
    for s in range(NP):
        G = Grp()
        G.prompt, G.nseq, G.T, G.tp, G.past = True, 1, T, 128, 0
        G.gseq = lambda si, s=s: s
        G.x_in = lambda si, s=s: x_p[s]
        G.y_out = lambda si, s=s: y_p[s]
        G.out = lambda name, l, si, s=s: po[name][l, s]
        groups.append(G)
    if NS:
        G = Grp()
        G.prompt, G.nseq, G.T, G.tp, G.past = False, NS, TS, TS, PAST
        G.gseq = lambda si: si
        G.x_in = lambda si: x_s[si]
        G.y_out = lambda si: y_s[si]
        G.out = lambda name, l, si: so[name][l, si]
        G.rope = lambda ti: (ropeCs[0:TS, 0, :], ropeSs[0:TS, 0, :])
        groups.append(G)
    for G in groups:
        G.NT = G.T // G.tp
        G.TB = min(512, G.T)
        G.NB = G.T // G.TB
        G.TPB = G.TB // G.tp
        G.NPT = G.past // 128
        G.col0 = lambda si, G=G: si * G.T
        S.tag = "input"
        input_stage(G)
        for l in range(L if cfg.upto >= 1 else 0):
            S.tag = "p1_mla"
            p1_mla(G, l)
            if cfg.upto < 2:
                continue
            S.tag = "p2_fox"
            p2_fox(G, l)
            if cfg.upto < 3:
                continue
            S.tag = "p3_gdn"
            if cfg.gdn:
                p3_gdn(G, l)
            else:
                O.memset(mixT[:, 2:6, 0:G.nseq * G.T], 0.0, eng="pool")
            if cfg.upto < 4:
                continue
            S.tag = "p4_out"
            p4_out(G, l)
            if cfg.upto < 5:
                continue
            S.tag = "p5_mem"
            p5_mem(G, l)
            if cfg.upto < 6:
                continue
            S.tag = "p6_ffn"
            p6_ffn(G, l)
        S.tag = "final"
        final_stage(G)

    if cfg.resched:
        S.reschedule(window=cfg.window)
    S.emit()
    return nc, S


def ar_attn(ar):
    return Arena(ar.S, ar.off, ar.limit)


def _fm(v):
    return np.ascontiguousarray(np.asarray(v, np.float32).reshape(8, 128).T)


def prep_core_inputs(cfg, inp, core, consts):
    NP, NS, L = cfg.n_prompt, cfg.n_sample, cfg.depth
    m = {}
    if NP:
        sl = slice(core * NP, (core + 1) * NP)
        m["x_prompt"] = np.ascontiguousarray(inp["x_prompt"][sl, :cfg.T])
        m["mem_prompt"] = np.ascontiguousarray(inp["mem_prompt"][sl])
    if NS:
        ss = slice(core * NS, (core + 1) * NS)
        m["x_sample"] = np.ascontiguousarray(inp["x_sample"][ss])
        for k in ("cache_mla_ckv", "cache_mla_kpe", "cache_fox_logf", "state_gdn", "state_gdn_conv"):
            m[k] = np.ascontiguousarray(np.asarray(inp[k], np.float32)[:L, ss])
        for k in ("cache_fox_k", "cache_fox_v"):
            a = np.asarray(inp[k], np.float32)[:L, ss]
            m[k] = np.ascontiguousarray(a.reshape(a.shape[0], a.shape[1], a.shape[2], 256))
        for k in ("cache_mem_k", "cache_mem_v"):
            a = np.asarray(inp[k], np.float32)[:L, ss]
            m[k] = np.ascontiguousarray(a.reshape(a.shape[0], a.shape[1], N_MEM, 512))
        m["ropeCs"] = consts["ropeCs"]
        m["ropeSs"] = consts["ropeSs"]
    for k in ("w_in", "w_uq", "w_ukv", "w_out", "w_xq", "w_mk", "w_mv", "w_xo", "w_ff1", "w_ff2",
              "qa_g", "kva_g", "fox_bf"):
        m[k] = np.ascontiguousarray(np.asarray(inp[k], np.float32)[:L])
    m["ln_in_fm"] = np.ascontiguousarray(np.stack([_fm(inp["ln_in_g"]), _fm(inp["ln_in_b"])], 1))
    lg = np.asarray(inp["ln_g"], np.float32)[:L]
    lb = np.asarray(inp["ln_b"], np.float32)[:L]
    arr = np.zeros((128, L, 3, 2, 8), np.float32)
    for l in range(L):
        for w in range(3):
            arr[:, l, w, 0, :] = _fm(lg[l, w])
            arr[:, l, w, 1, :] = _fm(lb[l, w])
    m["ln_fm"] = arr
    cw = np.asarray(inp["gdn_conv_w"], np.float32)[:L]
    m["conv_fm"] = np.ascontiguousarray(cw.reshape(L, 4, 12, 128).transpose(3, 0, 2, 1))
    for k in ("gdn_a_log", "gdn_dt_bias", "gdn_norm_g"):
        m[k] = np.ascontiguousarray(np.asarray(inp[k], np.float32)[:L])
    for k in ("ident_f", "triu_f", "ones_f", "ropeC", "ropeS", "Mle_g", "Mgt_g", "Sel0_g", "Sel1_g"):
        m[k] = consts[k]
    return m


N_CORES = 8
_OUT_ORDER = ["y_prompt", "y_sample", "p_mla_ckv", "p_mla_kpe", "p_fox_k", "p_fox_v", "p_fox_logf", "p_gdn",
              "p_gdn_conv", "p_mem_k", "p_mem_v", "s_mla_ckv", "s_mla_kpe", "s_fox_k", "s_fox_v", "s_fox_logf",
              "s_gdn", "s_gdn_conv"]


def kernel(**inputs):
    inp = {k: np.asarray(v) for k, v in inputs.items()}
    B, T = inp["x_prompt"].shape[:2]
    BS, TS = inp["x_sample"].shape[:2]
    past = inp["cache_mla_ckv"].shape[2]
    cfg = Cfg(n_prompt=B // N_CORES, T=T, n_sample=BS // N_CORES, TS=TS, past=past, depth=inp["w_in"].shape[0])
    nc, _ = build(cfg)
    consts = host_consts(cfg)
    in_maps = [prep_core_inputs(cfg, inp, c, consts) for c in range(N_CORES)]
    res = run_bass_kernel_spmd(nc, in_maps, core_ids=list(range(N_CORES)))
    R = res.results
    outs = []
    for name in _OUT_ORDER:
        if name in ("y_prompt", "y_sample"):
            a = np.concatenate([r[name] for r in R], axis=0)
        else:
            a = np.concatenate([r[name] for r in R], axis=1)
        if name in ("p_fox_k", "p_fox_v", "s_fox_k", "s_fox_v"):
            a = a.reshape(a.shape[0], a.shape[1], a.shape[2], FOX_H, FOX_HD)
        elif name in ("p_mem_k", "p_mem_v"):
            a = a.reshape(a.shape[0], a.shape[1], N_MEM, MEM_H, MEM_HD)
        outs.append(np.ascontiguousarray(a, dtype=np.float32))
    return tuple(outs)
```
